# Optimizing a Trainium2 kernel written in Bass

```python
import math
import jax, jax.numpy as jnp
from jax import lax
import numpy as np

D_MODEL = 2048
BATCH = 2
SEQ = 8192
DEPTH = 1

EPS = 1e-6
NEG = -1e30
BLOCK = 128
SWA_HEADS = 16
SWA_KV_HEADS = 4
SWA_HEAD_DIM = 64
SWA_WINDOW = 128
SWA_Q_W = SWA_HEADS * SWA_HEAD_DIM
SWA_KV_W = SWA_KV_HEADS * SWA_HEAD_DIM
MLA_HEADS = 8
MLA_Q_RANK = 512
MLA_KV_RANK = 256
MLA_NOPE_DIM = 128
MLA_ROPE_DIM = 64
MLA_V_DIM = 128
MLA_V_W = MLA_HEADS * MLA_V_DIM
ROPE_THETA = 10000.0
PEER_HEADS = 8
PEER_N_KEYS = 128
PEER_N_EXPERTS = PEER_N_KEYS * PEER_N_KEYS
PEER_QUERY_DIM = 256
PEER_HALF = PEER_QUERY_DIM // 2
PEER_TOPK = 16
PEER_TOKEN_CHUNK = 128
IN_WIDTHS = [SWA_Q_W, SWA_KV_W, SWA_KV_W, MLA_Q_RANK, MLA_KV_RANK, MLA_ROPE_DIM, D_MODEL, D_MODEL]
IN_WIDTH = sum(IN_WIDTHS)
IN_OFFSETS = [int(o) for o in np.cumsum(IN_WIDTHS)[:-1]]

kernel_name = "hybrid_swa_mla_peer_block"


def rms_norm(x, g):
    xf = x.astype(jnp.float32)
    y = xf * lax.rsqrt(jnp.mean(xf * xf, axis=-1, keepdims=True) + EPS)
    return (y * g.astype(jnp.float32)).astype(x.dtype)


def alibi_slopes(n_heads):
    return jnp.asarray([2.0 ** (-8.0 * (h + 1) / n_heads) for h in range(n_heads)], dtype=jnp.float32)


def rope_tables(pos, dim):
    inv = 1.0 / (ROPE_THETA ** (jnp.arange(0, dim, 2, dtype=jnp.float32) / dim))
    ang = pos.astype(jnp.float32)[..., None] * inv
    return jnp.cos(ang), jnp.sin(ang)


def apply_rope(t, cos, sin):
    tf = t.astype(jnp.float32)
    t1, t2 = jnp.split(tf, 2, axis=-1)
    out = jnp.concatenate([t1 * cos - t2 * sin, t1 * sin + t2 * cos], axis=-1)
    return out.astype(t.dtype)


def with_prev_block(t):
    prev = jnp.concatenate([jnp.zeros_like(t[:, :1]), t[:, :-1]], axis=1)
    return jnp.concatenate([prev, t], axis=2)


def swa_attention(q, k, v, pos, sinks):
    B, S = q.shape[:2]
    nb = S // BLOCK
    G = SWA_HEADS // SWA_KV_HEADS
    qb = q.reshape(B, nb, BLOCK, SWA_KV_HEADS, G, SWA_HEAD_DIM)
    kk = with_prev_block(k.reshape(B, nb, BLOCK, SWA_KV_HEADS, SWA_HEAD_DIM))
    vv = with_prev_block(v.reshape(B, nb, BLOCK, SWA_KV_HEADS, SWA_HEAD_DIM))
    pq = pos.astype(jnp.float32).reshape(B, nb, BLOCK)
    pk = with_prev_block(pq)
    logits = jnp.einsum('bnqhgd,bnkhd->bnhgqk', qb, kk).astype(jnp.float32) * (SWA_HEAD_DIM ** -0.5)
    dist = jnp.abs(pq[:, :, :, None] - pk[:, :, None, :])
    slopes = alibi_slopes(SWA_HEADS).reshape(SWA_KV_HEADS, G, 1, 1)
    logits = logits - slopes * dist[:, :, None, None]
    qi = jnp.arange(BLOCK)[:, None]
    kj = jnp.arange(2 * BLOCK)[None, :]
    rel = BLOCK + qi - kj
    band = (rel >= 0) & (rel < SWA_WINDOW)
    valid_prev = (jnp.arange(nb)[:, None, None] > 0) | (kj >= BLOCK)[None]
    mask = band[None] & valid_prev
    logits = jnp.where(mask[None, :, None, None], logits, NEG)
    sink = sinks.astype(jnp.float32).reshape(1, 1, SWA_KV_HEADS, G, 1, 1)
    m = jnp.maximum(jnp.max(logits, axis=-1, keepdims=True), sink)
    p = jnp.exp(logits - m)
    probs = (p / (jnp.sum(p, axis=-1, keepdims=True) + jnp.exp(sink - m))).astype(v.dtype)
    out = jnp.einsum('bnhgqk,bnkhd->bnqhgd', probs, vv)
    return out.reshape(B, S, SWA_Q_W)


def mla_attention(c_q, c_kv, k_rope_raw, pos, g_cq, w_uq, g_ckv, w_ukv):
    B, S = c_q.shape[:2]
    nb = S // BLOCK
    q = (rms_norm(c_q, g_cq) @ w_uq).reshape(B, S, MLA_HEADS, MLA_NOPE_DIM + MLA_ROPE_DIM)
    q_nope, q_rope = q[..., :MLA_NOPE_DIM], q[..., MLA_NOPE_DIM:]
    kv = (rms_norm(c_kv, g_ckv) @ w_ukv).reshape(B, S, MLA_HEADS, MLA_NOPE_DIM + MLA_V_DIM)
    k_nope, v = kv[..., :MLA_NOPE_DIM], kv[..., MLA_NOPE_DIM:]
    cos, sin = rope_tables(pos, MLA_ROPE_DIM)
    q_rope = apply_rope(q_rope, cos[:, :, None], sin[:, :, None])
    k_rope = apply_rope(k_rope_raw, cos, sin)
    scale = (MLA_NOPE_DIM + MLA_ROPE_DIM) ** -0.5
    qn_b = q_nope.reshape(B, nb, BLOCK, MLA_HEADS, MLA_NOPE_DIM).swapaxes(0, 1)
    qr_b = q_rope.reshape(B, nb, BLOCK, MLA_HEADS, MLA_ROPE_DIM).swapaxes(0, 1)
    key_idx = jnp.arange(S)

    def one_block(args):
        qn, qr, i = args
        s = (jnp.einsum('bqhd,bkhd->bhqk', qn, k_nope)
             + jnp.einsum('bqhr,bkr->bhqk', qr, k_rope)).astype(jnp.float32) * scale
        q_idx = i * BLOCK + jnp.arange(BLOCK)
        s = jnp.where(key_idx[None, :] <= q_idx[:, None], s, NEG)
        p = jax.nn.softmax(s, axis=-1).astype(v.dtype)
        return jnp.einsum('bhqk,bkhd->bqhd', p, v)

    out = lax.map(one_block, (qn_b, qr_b, jnp.arange(nb)))
    return out.swapaxes(0, 1).reshape(B, S, MLA_V_W)


def peer_ffn(h, w_q, keys, u, v):
    B, S, D = h.shape
    q = (h @ w_q).reshape(B, S, PEER_HEADS, 2, PEER_HALF)
    s = jnp.einsum('bshpd,hpnd->bshpn', q, keys).astype(jnp.float32)
    top_s, top_i = lax.top_k(s, PEER_TOPK)
    cand_s = (top_s[..., 0, :, None] + top_s[..., 1, None, :]).reshape(B, S, PEER_HEADS, PEER_TOPK * PEER_TOPK)
    cand_i = (top_i[..., 0, :, None] * PEER_N_KEYS + top_i[..., 1, None, :]).reshape(B, S, PEER_HEADS, PEER_TOPK * PEER_TOPK)
    best_s, best_pos = lax.top_k(cand_s, PEER_TOPK)
    expert = jnp.take_along_axis(cand_i, best_pos, axis=-1)
    gate = jax.nn.softmax(best_s, axis=-1)
    T = B * S
    HK = PEER_HEADS * PEER_TOPK
    nc = T // PEER_TOKEN_CHUNK
    xt = h.reshape(nc, PEER_TOKEN_CHUNK, D)
    it = expert.reshape(nc, PEER_TOKEN_CHUNK, HK)
    gt = gate.reshape(nc, PEER_TOKEN_CHUNK, HK)

    def one_chunk(args):
        xc, ic, gc = args
        a = jnp.einsum('cd,ced->ce', xc, u[ic])
        w = (jax.nn.gelu(a.astype(jnp.float32), approximate=False) * gc).astype(xc.dtype)
        return jnp.einsum('ce,ced->cd', w, v[ic])

    out = lax.map(one_chunk, (xt, it, gt))
    return out.reshape(B, S, D)


def setup_inputs(seed: int = 0) -> dict:
    key = jax.random.key(seed)
    ks = jax.random.split(key, 20)
    f32 = jnp.float32

    def nrm(k, shape, scale):
        return jax.random.normal(k, shape, f32) * scale

    def gain(k, shape):
        return 1.0 + 0.02 * jax.random.normal(k, shape, f32)

    x = jax.random.normal(ks[0], (BATCH, SEQ, D_MODEL), f32)
    offset = jax.random.randint(ks[1], (BATCH, 1), 0, 1024, dtype=jnp.int32)
    positions = (offset + jnp.arange(SEQ, dtype=jnp.int32)[None, :]).astype(jnp.int32)
    return {
        "x": x,
        "positions": positions,
        "g_mix": gain(ks[2], (DEPTH, D_MODEL)),
        "w_in": nrm(ks[3], (DEPTH, D_MODEL, IN_WIDTH), D_MODEL ** -0.5),
        "sinks": nrm(ks[4], (DEPTH, SWA_HEADS), 0.5),
        "g_cq": gain(ks[5], (DEPTH, MLA_Q_RANK)),
        "w_uq": nrm(ks[6], (DEPTH, MLA_Q_RANK, MLA_HEADS * (MLA_NOPE_DIM + MLA_ROPE_DIM)), MLA_Q_RANK ** -0.5),
        "g_ckv": gain(ks[7], (DEPTH, MLA_KV_RANK)),
        "w_ukv": nrm(ks[8], (DEPTH, MLA_KV_RANK, MLA_HEADS * (MLA_NOPE_DIM + MLA_V_DIM)), MLA_KV_RANK ** -0.5),
        "w_a_proj": nrm(ks[9], (DEPTH, SWA_Q_W, D_MODEL), SWA_Q_W ** -0.5),
        "w_b_proj": nrm(ks[10], (DEPTH, MLA_V_W, D_MODEL), MLA_V_W ** -0.5),
        "w_o": nrm(ks[11], (DEPTH, D_MODEL, D_MODEL), D_MODEL ** -0.5),
        "g_ffn": gain(ks[12], (DEPTH, D_MODEL)),
        "w_peer_q": nrm(ks[13], (DEPTH, D_MODEL, PEER_HEADS * PEER_QUERY_DIM), D_MODEL ** -0.5),
        "peer_keys": nrm(ks[14], (DEPTH, PEER_HEADS, 2, PEER_N_KEYS, PEER_HALF), PEER_HALF ** -0.5),
        "peer_u": nrm(ks[15], (DEPTH, PEER_N_EXPERTS, D_MODEL), D_MODEL ** -0.5),
        "peer_v": nrm(ks[16], (DEPTH, PEER_N_EXPERTS, D_MODEL), (PEER_HEADS * PEER_TOPK) ** -0.5),
        "g_final": gain(ks[17], (D_MODEL,)),
    }


def reference(x, positions, g_mix, w_in, sinks, g_cq, w_uq, g_ckv, w_ukv, w_a_proj, w_b_proj, w_o,
              g_ffn, w_peer_q, peer_keys, peer_u, peer_v, g_final):
    for l in range(DEPTH):
        h = rms_norm(x, g_mix[l])
        z = h @ w_in[l]
        q_a, k_a, v_a, c_q, c_kv, k_r, gate_a, gate_b = jnp.split(z, IN_OFFSETS, axis=-1)
        y_a = swa_attention(q_a, k_a, v_a, positions, sinks[l]) @ w_a_proj[l]
        y_b = mla_attention(c_q, c_kv, k_r, positions, g_cq[l], w_uq[l], g_ckv[l], w_ukv[l]) @ w_b_proj[l]
        mixed = jax.nn.sigmoid(gate_a) * y_a + jax.nn.sigmoid(gate_b) * y_b
        x = x + mixed @ w_o[l]
        x = x + peer_ffn(rms_norm(x, g_ffn[l]), w_peer_q[l], peer_keys[l], peer_u[l], peer_v[l])
    return rms_norm(x, g_final)
```

```python
import math
from contextlib import ExitStack

import numpy as np
import concourse.bass as bass
import concourse.mybir as mybir
from concourse.bass_utils import run_bass_kernel_spmd

F32 = mybir.dt.float32
BF16 = mybir.dt.bfloat16
I32 = mybir.dt.int32
U32 = mybir.dt.uint32
AF = mybir.ActivationFunctionType
ALU = mybir.AluOpType
AX = mybir.AxisListType

D = 2048
EPS = 1e-6
NKC = D // 128
PI = math.pi
TWO_PI = 2.0 * math.pi
MLA_SCALE = 192 ** -0.5
N_EXP = 16384


class Res:
    __slots__ = ("name", "w", "rd")

    def __init__(self, name=""):
        self.name = name
        self.w = None
        self.rd = {}


class Sched:
    ENGS = ("pe", "act", "dve", "pool", "sp")
    NDMA = {"sp": 8, "act": 4, "pool": 8}

    def __init__(self, nc, es):
        self.nc = nc
        self.sem = {}
        for e in ("pe", "act", "dve", "pool"):
            self.sem[e] = es.enter_context(nc.semaphore("c_" + e))
        for q, n in self.NDMA.items():
            for k in range(n):
                self.sem[("d", q, k)] = es.enter_context(nc.semaphore(f"d_{q}{k}"))
        self.cnt = {k: 0 for k in self.sem}
        self.seen = {e: {} for e in self.ENGS}
        self.items = {e: [] for e in self.ENGS}
        self.drr = {q: 0 for q in self.NDMA}

    def _deps(self, eng, reads, writes):
        need = {}

        def add(kv):
            if kv is None:
                return
            k, v = kv
            if k == "pe" and eng == "pe":
                return
            if need.get(k, 0) < v:
                need[k] = v
        for r in reads:
            add(r.w)
        for w in writes:
            add(w.w)
            for k, v in w.rd.items():
                add((k, v))
        waits = []
        sn = self.seen[eng]
        for k, v in need.items():
            if sn.get(k, 0) < v:
                sn[k] = v
                waits.append((k, v))
        return waits

    @staticmethod
    def _mark(key, val, reads, writes):
        for r in reads:
            if r.rd.get(key, 0) < val:
                r.rd[key] = val
        for w in writes:
            w.w = (key, val)
            w.rd = {}

    def op(self, eng, fn, r=(), w=()):
        waits = self._deps(eng, r, w)
        self.cnt[eng] += 1
        self._mark(eng, self.cnt[eng], r, w)
        self.items[eng].append((waits, fn, (eng, 1)))

    def dma(self, q, fn, r=(), w=()):
        k = self.drr[q]
        self.drr[q] = (k + 1) % self.NDMA[q]
        key = ("d", q, k)
        waits = self._deps(q, r, w)
        prev = self.cnt[key]
        if prev and self.seen[q].get(key, 0) < prev:
            self.seen[q][key] = prev
            waits.append((key, prev))
        self.cnt[key] += 16
        self._mark(key, self.cnt[key], r, w)
        self.items[q].append((waits, fn, (key, 16)))

    def barrier(self):
        for e in self.ENGS:
            waits = []
            for k, v in self.cnt.items():
                if v and self.seen[e].get(k, 0) < v:
                    self.seen[e][k] = v
                    waits.append((k, v))
            if waits:
                self.items[e].append((waits, None, None))

    def flush(self):
        self.barrier()
        nc, items, sem = self.nc, self.items, self.sem

        def emit(h, lst):
            for waits, fn, inc in lst:
                for k, v in waits:
                    h.wait_ge(sem[k], v)
                if fn is not None:
                    fn(h).then_inc(sem[inc[0]], inc[1])

        with nc.Block() as block:
            @block.tensor
            def _(e):
                emit(e, items["pe"])

            @block.scalar
            def _(e):
                emit(e, items["act"])

            @block.vector
            def _(e):
                emit(e, items["dve"])

            @block.gpsimd
            def _(e):
                emit(e, items["pool"])

            @block.sync
            def _(e):
                emit(e, items["sp"])
        self.items = {e: [] for e in self.ENGS}


class Tl:
    __slots__ = ("t", "r")

    def __init__(self, t, name):
        self.t = t
        self.r = Res(name)


def build(NJ):
    NS = 4 * NJ
    SQ = 128 * NS
    TK = 128 * NJ
    nc = bass.Bass("TRN2", target_bir_lowering=False)

    def din(name, shape, dt=F32):
        return nc.dram_tensor(name, list(shape), dt, kind="ExternalInput").ap()

    def dscr(name, shape, dt):
        return nc.dram_tensor(name, list(shape), dt, kind="Internal").ap()

    x_seq = din("x_seq", [SQ, D])
    x_own = din("x_own", [TK, D])
    x_prev = din("x_prev", [TK, D])
    pos_seq = din("pos_seq", [128, NS], I32)
    pos_own = din("pos_own", [128, NJ], I32)
    pos_kv_bc = din("pos_kv_bc", [NJ, 128, 256], I32)
    swa_maskb = din("swa_maskb", [2, 128, 256])
    mla_mask = din("mla_mask", [128, 512])
    invf_bc = din("invf_bc", [128, 32])
    g_mix_pc = din("g_mix_pc", [128, NKC])
    g_ffn_bc = din("g_ffn_bc", [128, D])
    g_fin_bc = din("g_fin_bc", [128, D])
    g_ckv_bc = din("g_ckv_bc", [128, 256])
    g_cq_bc = din("g_cq_bc", [128, 512])
    sinks_bc = din("sinks_bc", [128, 16])
    w_kv = din("w_kv", [D, 320])
    w_kva = din("w_kva", [D, 512])
    w_cq = din("w_cq", [D, 512])
    w_qa = din("w_qa", [D, 1024])
    w_gate = din("w_gate", [D, 4096])
    w_uq = din("w_uq", [512, 1536])
    w_ukv_k = din("w_ukv_k", [256, 1024])
    w_ukv_v = din("w_ukv_v", [256, 1024])
    w_a = din("w_a", [1024, D])
    w_b = din("w_b", [1024, D])
    w_o = din("w_o", [D, D])
    w_pq = din("w_pq", [D, D])
    keysT = din("keysT", [128, 16, 128])
    u_tab = din("u_tab", [N_EXP, D])
    v_tab = din("v_tab", [N_EXP, D])
    ident_in = din("ident", [128, 128])
    out_d = nc.dram_tensor("out", [TK, D], F32, kind="ExternalOutput").ap()

    qnT_d = dscr("qnT_d", [8, 128, TK], BF16)
    qrT_d = dscr("qrT_d", [8, 64, TK], BF16)
    swa_d = dscr("swa_d", [TK, 1024], BF16)
    mla_d = dscr("mla_d", [TK, 1024], BF16)
    zg_d = dscr("zg_d", [TK, 4096], F32)
    x1_d = dscr("x1_d", [TK, D], F32)
    mixed_d = dscr("mixed_d", [TK, D], BF16)
    hn2_d = dscr("hn2_d", [TK, D], F32)
    uv_b = dscr("uv_b", [N_EXP, 2 * D], BF16)
    r_uvb = Res("uv_b")
    r_qT = Res("qT_d")
    r_swa = [Res(f"swa_d{j}") for j in range(NJ)]
    r_mla = [Res(f"mla_d{j}") for j in range(NJ)]
    r_zg = [Res(f"zg_d{j}") for j in range(NJ)]
    r_x1 = [Res(f"x1_d{j}") for j in range(NJ)]
    r_hn2 = [Res(f"hn2_d{j}") for j in range(NJ)]

    slopes = [2.0 ** (-8.0 * (h + 1) / 16) for h in range(16)]

    with ExitStack() as G:
        S = Sched(nc, G)

        def sb(es, name, shape, dt=F32):
            return Tl(es.enter_context(nc.sbuf_tensor(name, list(shape), dt)), name)

        def psum(es, name, shape, dt=F32):
            return Tl(es.enter_context(nc.psum_tensor(name, list(shape), dt)), name)

        PF = [psum(G, f"pf{i}", [128, 512], F32) for i in range(6)]
        PB = [psum(G, f"pb{i}", [128, 1024], BF16) for i in range(2)]

        ident_f = sb(G, "ident_f", [128, 128])
        ident = sb(G, "ident_b", [128, 128], BF16)
        S.dma("sp", lambda e: e.dma_start(out=ident_f.t[:], in_=ident_in), w=[ident_f.r])
        S.op("dve", lambda e: e.tensor_copy(out=ident.t[:], in_=ident_f.t[:]), r=[ident_f.r], w=[ident.r])
        gmix = sb(G, "gmix", [128, NKC])
        S.dma("sp", lambda e: e.dma_start(out=gmix.t[:], in_=g_mix_pc), w=[gmix.r])
        invf = sb(G, "invf", [128, 32])
        S.dma("sp", lambda e: e.dma_start(out=invf.t[:], in_=invf_bc), w=[invf.r])
        GK = ExitStack()
        ckvT = sb(GK, "ckvT", [128, 2, SQ], BF16)
        kropeT = sb(GK, "kropeT", [64, SQ], BF16)
        r_ckvT = [Res(f"ckvT{t}") for t in range(NS)]

        def load_w(es, name, w_ap, kchunks, ncols, dst=None, col0=0, ncol_dst=None):
            t = dst if dst is not None else sb(es, name, [128, kchunks, ncols], BF16)
            src = w_ap.rearrange("(c p) n -> p c n", p=128)
            step = max(1, 4096 // ncols)
            for c0 in range(0, kchunks, step):
                c1 = min(kchunks, c0 + step)
                S.dma("pool", lambda e, c0=c0, c1=c1: e.dma_start(out=t.t[:, c0:c1, :], in_=src[:, c0:c1, :]),
                      w=[t.r])
            return t

        def rstd_from_ss(ss, rstd, n):
            S.op("dve", lambda e: e.tensor_scalar(out=rstd.t[:], in0=ss.t[:], scalar1=1.0 / n, scalar2=EPS,
                                                  op0=ALU.mult, op1=ALU.add), r=[ss.r], w=[rstd.r])
            S.op("act", lambda e: e.activation(out=rstd.t[:], in_=rstd.t[:], func=AF.Sqrt), r=[rstd.r], w=[rstd.r])
            S.op("dve", lambda e: e.reciprocal(out=rstd.t[:], in_=rstd.t[:]), r=[rstd.r], w=[rstd.r])

        def range_reduce(src, dst, ki, kf, tmp):
            S.op("dve", lambda e: e.tensor_scalar(out=ki.t[:], in0=src.t[:], scalar1=1.0 / TWO_PI, scalar2=None,
                                                  op0=ALU.mult), r=[src.r], w=[ki.r])
            S.op("dve", lambda e: e.tensor_copy(out=kf.t[:], in_=ki.t[:]), r=[ki.r], w=[kf.r])
            S.op("dve", lambda e: e.scalar_tensor_tensor(out=dst.t[:], in0=kf.t[:], scalar=-TWO_PI, in1=src.t[:],
                                                         op0=ALU.mult, op1=ALU.add), r=[kf.r, src.r], w=[dst.r])
            S.op("dve", lambda e: e.tensor_scalar(out=tmp.t[:], in0=dst.t[:], scalar1=PI, scalar2=-TWO_PI,
                                                  op0=ALU.is_gt, op1=ALU.mult), r=[dst.r], w=[tmp.r])
            S.op("dve", lambda e: e.tensor_tensor(out=dst.t[:], in0=dst.t[:], in1=tmp.t[:], op=ALU.add),
                 r=[dst.r, tmp.r], w=[dst.r])
            S.op("dve", lambda e: e.tensor_scalar(out=tmp.t[:], in0=dst.t[:], scalar1=-PI, scalar2=TWO_PI,
                                                  op0=ALU.is_lt, op1=ALU.mult), r=[dst.r], w=[tmp.r])
            S.op("dve", lambda e: e.tensor_tensor(out=dst.t[:], in0=dst.t[:], in1=tmp.t[:], op=ALU.add),
                 r=[dst.r, tmp.r], w=[dst.r])
            S.op("dve", lambda e: e.tensor_scalar(out=dst.t[:], in0=dst.t[:], scalar1=-3.1415925, scalar2=3.1415925,
                                                  op0=ALU.max, op1=ALU.min), r=[dst.r], w=[dst.r])

        def cos_sin(es, name, pos_ap, n):
            cs = sb(es, name + "_cos", [128, n, 32])
            sn = sb(es, name + "_sin", [128, n, 32])
            with ExitStack() as L:
                pi_ = sb(L, name + "_pi", [128, n], I32)
                pf_ = sb(L, name + "_pf", [128, n])
                ang = sb(L, name + "_ang", [128, n, 32])
                red = sb(L, name + "_red", [128, n, 32])
                ki = sb(L, name + "_ki", [128, n, 32], I32)
                kf = sb(L, name + "_kf", [128, n, 32])
                tmp = sb(L, name + "_tmp", [128, n, 32])
                S.dma("sp", lambda e: e.dma_start(out=pi_.t[:], in_=pos_ap), w=[pi_.r])
                S.op("dve", lambda e: e.tensor_copy(out=pf_.t[:], in_=pi_.t[:]), r=[pi_.r], w=[pf_.r])
                S.op("dve", lambda e: e.tensor_tensor(out=ang.t[:],
                                                      in0=pf_.t[:].unsqueeze(2).to_broadcast([128, n, 32]),
                                                      in1=invf.t[:].unsqueeze(1).to_broadcast([128, n, 32]),
                                                      op=ALU.mult), r=[pf_.r, invf.r], w=[ang.r])
                range_reduce(ang, red, ki, kf, tmp)
                S.op("act", lambda e: e.activation(out=sn.t[:], in_=red.t[:], func=AF.Sin), r=[red.r], w=[sn.r])
                S.op("dve", lambda e: e.tensor_scalar(out=ang.t[:], in0=ang.t[:], scalar1=PI / 2, scalar2=None,
                                                      op0=ALU.add), r=[ang.r], w=[ang.r])
                range_reduce(ang, red, ki, kf, tmp)
                S.op("act", lambda e: e.activation(out=cs.t[:], in_=red.t[:], func=AF.Sin), r=[red.r], w=[cs.r])
                S.flush()
            return cs, sn

        class NormCtx:
            def __init__(self, es, name, gpc):
                self.x = [sb(es, f"{name}_x{i}", [128, D]) for i in range(2)]
                self.junk = sb(es, name + "_junk", [128, D], BF16)
                self.ss = sb(es, name + "_ss", [128, 1])
                self.rstd = sb(es, name + "_rstd", [128, 1])
                self.xn = sb(es, name + "_xn", [128, D], BF16)
                self.gpc = gpc
                self.i = 0

            def run(self, x_ap, hT, hT_slice, extra_r=()):
                xt = self.x[self.i % 2]
                self.i += 1
                S.dma("sp", lambda e: e.dma_start(out=xt.t[:], in_=x_ap), r=list(extra_r), w=[xt.r])
                S.op("act", lambda e: e.activation(out=self.junk.t[:], in_=xt.t[:], func=AF.Square,
                                                   accum_out=self.ss.t[:]), r=[xt.r], w=[self.junk.r, self.ss.r])
                rstd_from_ss(self.ss, self.rstd, D)
                S.op("dve", lambda e: e.tensor_scalar(out=self.xn.t[:], in0=xt.t[:], scalar1=self.rstd.t[:, 0:1],
                                                      scalar2=None, op0=ALU.mult), r=[xt.r, self.rstd.r], w=[self.xn.r])
                for half in range(2):
                    pb = PB[half]
                    for cc in range(8):
                        c = half * 8 + cc
                        S.op("pe", lambda e, c=c, cc=cc, pb=pb: e.transpose(
                            out=pb.t[:, cc * 128:(cc + 1) * 128], in_=self.xn.t[:, c * 128:(c + 1) * 128],
                            identity=ident.t[:]), r=[self.xn.r, ident.r], w=[pb.r])
                    for cc in range(8):
                        c = half * 8 + cc
                        eng = "act" if cc % 2 == 0 else "dve"
                        if eng == "act":
                            S.op("act", lambda e, c=c, cc=cc, pb=pb: e.activation(
                                out=hT_slice(c), in_=pb.t[:, cc * 128:(cc + 1) * 128], func=AF.Copy,
                                scale=self.gpc.t[:, c:c + 1]), r=[pb.r, self.gpc.r], w=[hT.r])
                        else:
                            S.op("dve", lambda e, c=c, cc=cc, pb=pb: e.tensor_scalar(
                                out=hT_slice(c), in0=pb.t[:, cc * 128:(cc + 1) * 128],
                                scalar1=self.gpc.t[:, c:c + 1], scalar2=None, op0=ALU.mult),
                                r=[pb.r, self.gpc.r], w=[hT.r])
                return xt

        def proj(hT, hT_slice, w, col0, ncols, pf):
            for c in range(NKC):
                S.op("pe", lambda e, c=c: e.matmul(pf.t[:, 0:ncols], lhsT=hT_slice(c), rhs=w.t[:, c, col0:col0 + ncols],
                                                   start=(c == 0), stop=(c == NKC - 1)), r=[hT.r, w.r], w=[pf.r])

        def rope(src_ap_fn, dst_ap_fn, cs_ap, sn_ap, tmps, rsrc, rdst, rcs):
            a, b_ = tmps
            S.op("dve", lambda e: e.tensor_tensor(out=a.t[:], in0=src_ap_fn(0, 32), in1=cs_ap, op=ALU.mult),
                 r=[rsrc, rcs], w=[a.r])
            S.op("dve", lambda e: e.tensor_tensor(out=b_.t[:], in0=src_ap_fn(32, 64), in1=sn_ap, op=ALU.mult),
                 r=[rsrc, rcs], w=[b_.r])
            S.op("dve", lambda e: e.tensor_tensor(out=dst_ap_fn(0, 32), in0=a.t[:], in1=b_.t[:], op=ALU.subtract),
                 r=[a.r, b_.r], w=[rdst])
            S.op("dve", lambda e: e.tensor_tensor(out=a.t[:], in0=src_ap_fn(0, 32), in1=sn_ap, op=ALU.mult),
                 r=[rsrc, rcs], w=[a.r])
            S.op("dve", lambda e: e.tensor_tensor(out=b_.t[:], in0=src_ap_fn(32, 64), in1=cs_ap, op=ALU.mult),
                 r=[rsrc, rcs], w=[b_.r])
            S.op("dve", lambda e: e.tensor_tensor(out=dst_ap_fn(32, 64), in0=a.t[:], in1=b_.t[:], op=ALU.add),
                 r=[a.r, b_.r], w=[rdst])

        S.flush()

        with ExitStack() as P:
            cs_s, sn_s = cos_sin(P, "cs_seq", pos_seq, NS)
            wkv = load_w(P, "wkv", w_kv, NKC, 320)
            gckv = sb(P, "gckv", [128, 256])
            S.dma("sp", lambda e: e.dma_start(out=gckv.t[:], in_=g_ckv_bc), w=[gckv.r])
            nctx = NormCtx(P, "n1", gmix)
            hT = [sb(P, f"hT1_{i}", [128, NKC, 128], BF16) for i in range(2)]
            ss2 = sb(P, "ss2", [128, 1])
            rstd2 = sb(P, "rstd2", [128, 1])
            junk2 = sb(P, "junk2", [128, 256], BF16)
            ckvn = sb(P, "ckvn", [128, 256], BF16)
            krs = sb(P, "krs", [128, 64])
            kro = sb(P, "kro", [128, 64], BF16)
            ta = sb(P, "ropa", [128, 32])
            tb = sb(P, "ropb", [128, 32])
            for t in range(NS):
                h = hT[t % 2]
                nctx.run(x_seq[t * 128:(t + 1) * 128, :], h, lambda c, h=h: h.t[:, c, :])
                pf = PF[t % 2]
                proj(h, lambda c, h=h: h.t[:, c, :], wkv, 0, 320, pf)
                S.op("act", lambda e, pf=pf: e.activation(out=junk2.t[:], in_=pf.t[:, 0:256], func=AF.Square,
                                                          accum_out=ss2.t[:]), r=[pf.r], w=[junk2.r, ss2.r])
                rstd_from_ss(ss2, rstd2, 256)
                S.op("dve", lambda e, pf=pf: e.scalar_tensor_tensor(out=ckvn.t[:], in0=pf.t[:, 0:256],
                                                                    scalar=rstd2.t[:, 0:1], in1=gckv.t[:],
                                                                    op0=ALU.mult, op1=ALU.mult),
                     r=[pf.r, rstd2.r, gckv.r], w=[ckvn.r])
                S.op("act", lambda e, pf=pf: e.activation(out=krs.t[:], in_=pf.t[:, 256:320], func=AF.Copy),
                     r=[pf.r], w=[krs.r])
                rope(lambda lo, hi: krs.t[:, lo:hi], lambda lo, hi: kro.t[:, lo:hi],
                     cs_s.t[:, t, :], sn_s.t[:, t, :], (ta, tb), krs.r, kro.r, cs_s.r)
                pb = PB[t % 2]
                for cc in range(2):
                    S.op("pe", lambda e, cc=cc, pb=pb: e.transpose(out=pb.t[:, cc * 128:(cc + 1) * 128],
                                                                   in_=ckvn.t[:, cc * 128:(cc + 1) * 128],
                                                                   identity=ident.t[:]),
                         r=[ckvn.r, ident.r], w=[pb.r])
                S.op("pe", lambda e, pb=pb: e.transpose(out=pb.t[0:64, 256:384], in_=kro.t[:, 0:64],
                                                        identity=ident.t[:]), r=[kro.r, ident.r], w=[pb.r])
                S.op("act", lambda e, pb=pb, t=t: e.activation(
                    out=ckvT.t[:, :, t * 128:(t + 1) * 128],
                    in_=pb.t[:, 0:256].rearrange("p (c k) -> p c k", c=2), func=AF.Copy),
                    r=[pb.r], w=[r_ckvT[t]])
                S.op("dve", lambda e, pb=pb, t=t: e.tensor_copy(out=kropeT.t[:, t * 128:(t + 1) * 128],
                                                                in_=pb.t[0:64, 256:384]),
                     r=[pb.r], w=[r_ckvT[t]])
            S.flush()

        with ExitStack() as P:
            cs_o, sn_o = cos_sin(P, "cs_own", pos_own, NJ)
            kv_prev = sb(P, "kv_prev", [128, NJ, 512], BF16)
            kv_own = sb(P, "kv_own", [128, NJ, 512], BF16)
            r_kvp = [Res(f"kvp{j}") for j in range(NJ)]
            r_kvo = [Res(f"kvo{j}") for j in range(NJ)]
            nctx = NormCtx(P, "n2", gmix)
            hT = [sb(P, f"hT2_{i}", [128, NKC, 128], BF16) for i in range(2)]
            with ExitStack() as P2:
                wkva = load_w(P2, "wkva", w_kva, NKC, 512)
                wcq = load_w(P2, "wcq", w_cq, NKC, 512)
                wuq = load_w(P2, "wuq", w_uq, 4, 1536)
                gcq = sb(P2, "gcq", [128, 512])
                S.dma("sp", lambda e: e.dma_start(out=gcq.t[:], in_=g_cq_bc), w=[gcq.r])
                for j in range(NJ):
                    h = hT[j % 2]
                    nctx.run(x_prev[j * 128:(j + 1) * 128, :], h, lambda c, h=h: h.t[:, c, :])
                    pf = PF[j % 2]
                    proj(h, lambda c, h=h: h.t[:, c, :], wkva, 0, 512, pf)
                    S.op("act", lambda e, pf=pf, j=j: e.activation(out=kv_prev.t[:, j, :], in_=pf.t[:, 0:512],
                                                                  func=AF.Copy), r=[pf.r], w=[r_kvp[j]])
                ss3 = sb(P2, "ss3", [128, 1])
                rstd3 = sb(P2, "rstd3", [128, 1])
                junk3 = sb(P2, "junk3", [128, 512], BF16)
                cqn = sb(P2, "cqn", [128, 512], BF16)
                cqT = sb(P2, "cqT", [128, 4, 128], BF16)
                qf = sb(P2, "qf", [128, 1536])
                qb = sb(P2, "qb", [128, 1536], BF16)
                ta = sb(P2, "ropa2", [128, 32])
                tb = sb(P2, "ropb2", [128, 32])
                qnT = [sb(P2, f"qnT{i}", [128, 8, 128], BF16) for i in range(2)]
                qrT = [sb(P2, f"qrT{i}", [64, 8, 128], BF16) for i in range(2)]
                for j in range(NJ):
                    h = hT[j % 2]
                    nctx.run(x_own[j * 128:(j + 1) * 128, :], h, lambda c, h=h: h.t[:, c, :])
                    hs = (lambda c, h=h: h.t[:, c, :])
                    proj(h, hs, wkva, 0, 512, PF[0])
                    S.op("act", lambda e, j=j: e.activation(out=kv_own.t[:, j, :], in_=PF[0].t[:, 0:512],
                                                           func=AF.Copy), r=[PF[0].r], w=[r_kvo[j]])
                    proj(h, hs, wcq, 0, 512, PF[1])
                    S.op("act", lambda e: e.activation(out=junk3.t[:], in_=PF[1].t[:, 0:512], func=AF.Square,
                                                       accum_out=ss3.t[:]), r=[PF[1].r], w=[junk3.r, ss3.r])
                    rstd_from_ss(ss3, rstd3, 512)
                    S.op("dve", lambda e: e.scalar_tensor_tensor(out=cqn.t[:], in0=PF[1].t[:, 0:512],
                                                                 scalar=rstd3.t[:, 0:1], in1=gcq.t[:],
                                                                 op0=ALU.mult, op1=ALU.mult),
                         r=[PF[1].r, rstd3.r, gcq.r], w=[cqn.r])
                    for cc in range(4):
                        S.op("pe", lambda e, cc=cc: e.transpose(out=PB[0].t[:, cc * 128:(cc + 1) * 128],
                                                                in_=cqn.t[:, cc * 128:(cc + 1) * 128],
                                                                identity=ident.t[:]),
                             r=[cqn.r, ident.r], w=[PB[0].r])
                    S.op("dve", lambda e: e.tensor_copy(out=cqT.t[:],
                                                        in_=PB[0].t[:, 0:512].rearrange("p (c k) -> p c k", c=4)),
                         r=[PB[0].r], w=[cqT.r])
                    for g3 in range(3):
                        pf = PF[2 + g3]
                        for cc in range(4):
                            S.op("pe", lambda e, cc=cc, g3=g3, pf=pf: e.matmul(
                                pf.t[:, 0:512], lhsT=cqT.t[:, cc, :], rhs=wuq.t[:, cc, g3 * 512:(g3 + 1) * 512],
                                start=(cc == 0), stop=(cc == 3)), r=[cqT.r, wuq.r], w=[pf.r])
                        S.op("act", lambda e, g3=g3, pf=pf: e.activation(out=qf.t[:, g3 * 512:(g3 + 1) * 512],
                                                                        in_=pf.t[:, 0:512], func=AF.Copy),
                             r=[pf.r], w=[qf.r])
                    S.op("act", lambda e: e.activation(out=qb.t[:], in_=qf.t[:], func=AF.Copy), r=[qf.r], w=[qb.r])
                    for hh in range(8):
                        base = hh * 192 + 128
                        rope(lambda lo, hi, base=base: qf.t[:, base + lo:base + hi],
                             lambda lo, hi, base=base: qb.t[:, base + lo:base + hi],
                             cs_o.t[:, j, :], sn_o.t[:, j, :], (ta, tb), qf.r, qb.r, cs_o.r)
                    qn = qnT[j % 2]
                    qr = qrT[j % 2]
                    for hh in range(8):
                        S.op("pe", lambda e, hh=hh: e.transpose(out=PB[1].t[:, hh * 128:(hh + 1) * 128],
                                                                in_=qb.t[:, hh * 192:hh * 192 + 128],
                                                                identity=ident.t[:]),
                             r=[qb.r, ident.r], w=[PB[1].r])
                    S.op("dve", lambda e, qn=qn: e.tensor_copy(
                        out=qn.t[:], in_=PB[1].t[:, 0:1024].rearrange("p (c k) -> p c k", c=8)),
                        r=[PB[1].r], w=[qn.r])
                    for hh in range(8):
                        S.op("pe", lambda e, hh=hh: e.transpose(out=PB[0].t[0:64, hh * 128:(hh + 1) * 128],
                                                                in_=qb.t[:, hh * 192 + 128:hh * 192 + 192],
                                                                identity=ident.t[:]),
                             r=[qb.r, ident.r], w=[PB[0].r])
                    S.op("act", lambda e, qr=qr: e.activation(
                        out=qr.t[:], in_=PB[0].t[0:64, 0:1024].rearrange("p (c k) -> p c k", c=8), func=AF.Copy),
                        r=[PB[0].r], w=[qr.r])
                    S.dma("sp", lambda e, qn=qn, j=j: e.dma_start(
                        out=qnT_d[:, :, j * 128:(j + 1) * 128].rearrange("h p k -> p h k"), in_=qn.t[:]),
                        r=[qn.r], w=[r_qT])
                    S.dma("sp", lambda e, qr=qr, j=j: e.dma_start(
                        out=qrT_d[:, :, j * 128:(j + 1) * 128].rearrange("h p k -> p h k"), in_=qr.t[:]),
                        r=[qr.r], w=[r_qT])
                S.flush()

            with ExitStack() as P3:
                wqa = load_w(P3, "wqa", w_qa, NKC, 1024)
                sinks = sb(P3, "sinks", [128, 16])
                S.dma("sp", lambda e: e.dma_start(out=sinks.t[:], in_=sinks_bc), w=[sinks.r])
                maskb = sb(P3, "maskb", [128, 2, 256])
                S.dma("sp", lambda e: e.dma_start(out=maskb.t[:], in_=swa_maskb.rearrange("a p k -> p a k")),
                      w=[maskb.r])
                posq_i = sb(P3, "posq_i", [128, NJ], I32)
                posq = sb(P3, "posq", [128, NJ])
                S.dma("sp", lambda e: e.dma_start(out=posq_i.t[:], in_=pos_own), w=[posq_i.r])
                S.op("dve", lambda e: e.tensor_copy(out=posq.t[:], in_=posq_i.t[:]), r=[posq_i.r], w=[posq.r])
                posk_i = sb(P3, "posk_i", [128, 256], I32)
                dist = sb(P3, "dist", [128, 256])
                bias = sb(P3, "bias", [128, 16, 256])
                qa = sb(P3, "qa", [128, 1024], BF16)
                qaT = sb(P3, "qaT", [64, 16, 128], BF16)
                kaT = sb(P3, "kaT", [64, 4, 256], BF16)
                L = sb(P3, "swaL", [128, 4, 256])
                mx = sb(P3, "swamx", [128, 4])
                dd = sb(P3, "swadd", [128, 4])
                rsum = sb(P3, "swarsum", [128, 4])
                den = sb(P3, "swaden", [128, 4])
                pexp = sb(P3, "swap", [128, 4, 256], BF16)
                pT = sb(P3, "swapT", [128, 8, 128], BF16)
                so = [sb(P3, f"swao{i}", [128, 1024], BF16) for i in range(2)]
                for j in range(NJ):
                    h = hT[j % 2]
                    hs = (lambda c, h=h: h.t[:, c, :])
                    nctx.run(x_own[j * 128:(j + 1) * 128, :], h, hs)
                    for g2 in range(2):
                        proj(h, hs, wqa, g2 * 512, 512, PF[g2])
                        S.op("act", lambda e, g2=g2: e.activation(out=qa.t[:, g2 * 512:(g2 + 1) * 512],
                                                                 in_=PF[g2].t[:, 0:512], func=AF.Copy),
                             r=[PF[g2].r], w=[qa.r])
                    for half in range(2):
                        for hh in range(8):
                            hd = half * 8 + hh
                            S.op("pe", lambda e, hd=hd, hh=hh, half=half: e.transpose(
                                out=PB[half].t[0:64, hh * 128:(hh + 1) * 128], in_=qa.t[:, hd * 64:(hd + 1) * 64],
                                identity=ident.t[:]), r=[qa.r, ident.r], w=[PB[half].r])
                        S.op("dve", lambda e, half=half: e.tensor_copy(
                            out=qaT.t[:, half * 8:(half + 1) * 8, :],
                            in_=PB[half].t[0:64, 0:1024].rearrange("p (c k) -> p c k", c=8)),
                            r=[PB[half].r], w=[qaT.r])
                    for kvh in range(4):
                        S.op("pe", lambda e, kvh=kvh, j=j: e.transpose(
                            out=PB[0].t[0:64, kvh * 256:kvh * 256 + 128], in_=kv_prev.t[:, j, kvh * 64:(kvh + 1) * 64],
                            identity=ident.t[:]), r=[r_kvp[j], ident.r], w=[PB[0].r])
                        S.op("pe", lambda e, kvh=kvh, j=j: e.transpose(
                            out=PB[0].t[0:64, kvh * 256 + 128:kvh * 256 + 256],
                            in_=kv_own.t[:, j, kvh * 64:(kvh + 1) * 64],
                            identity=ident.t[:]), r=[r_kvo[j], ident.r], w=[PB[0].r])
                    S.op("dve", lambda e: e.tensor_copy(
                        out=kaT.t[:], in_=PB[0].t[0:64, 0:1024].rearrange("p (c k) -> p c k", c=4)),
                        r=[PB[0].r], w=[kaT.r])
                    S.dma("sp", lambda e, j=j: e.dma_start(out=posk_i.t[:], in_=pos_kv_bc[j]), w=[posk_i.r])
                    S.op("dve", lambda e: e.tensor_copy(out=dist.t[:], in_=posk_i.t[:]), r=[posk_i.r], w=[dist.r])
                    S.op("dve", lambda e, j=j: e.tensor_scalar(out=dist.t[:], in0=dist.t[:], scalar1=posq.t[:, j:j + 1],
                                                               scalar2=None, op0=ALU.subtract),
                         r=[dist.r, posq.r], w=[dist.r])
                    S.op("dve", lambda e: e.scalar_tensor_tensor(out=dist.t[:], in0=dist.t[:], scalar=-1.0,
                                                                 in1=dist.t[:], op0=ALU.mult, op1=ALU.max),
                         r=[dist.r], w=[dist.r])
                    mi = 0 if j == 0 else 1
                    for hd in range(16):
                        S.op("pool" if hd % 2 else "dve", lambda e, hd=hd: e.tensor_scalar(
                            out=bias.t[:, hd, :], in0=dist.t[:], scalar1=-slopes[hd], scalar2=None, op0=ALU.mult),
                            r=[dist.r], w=[bias.r])
                    S.op("dve", lambda e, mi=mi: e.tensor_tensor(
                        out=bias.t[:], in0=bias.t[:], in1=maskb.t[:, mi, :].unsqueeze(1).to_broadcast([128, 16, 256]),
                        op=ALU.add), r=[bias.r, maskb.r], w=[bias.r])
                    sot = so[j % 2]
                    for kvh in range(4):
                        for gq in range(4):
                            hd = kvh * 4 + gq
                            pfl = PF[gq // 2]
                            S.op("pe", lambda e, hd=hd, gq=gq, kvh=kvh, pfl=pfl: e.matmul(
                                pfl.t[:, (gq % 2) * 256:(gq % 2) * 256 + 256], lhsT=qaT.t[:, hd, :],
                                rhs=kaT.t[:, kvh, :], start=True, stop=True), r=[qaT.r, kaT.r], w=[pfl.r])
                        for half in range(2):
                            S.op("dve", lambda e, half=half, kvh=kvh: e.scalar_tensor_tensor(
                                out=L.t[:, half * 2:half * 2 + 2, :],
                                in0=PF[half].t[:, 0:512].rearrange("p (a k) -> p a k", a=2), scalar=0.125,
                                in1=bias.t[:, kvh * 4 + half * 2:kvh * 4 + half * 2 + 2, :],
                                op0=ALU.mult, op1=ALU.add), r=[PF[half].r, bias.r], w=[L.r])
                        S.op("dve", lambda e: e.tensor_reduce(out=mx.t[:], in_=L.t[:], axis=AX.X, op=ALU.max),
                             r=[L.r], w=[mx.r])
                        S.op("dve", lambda e, kvh=kvh: e.tensor_tensor(out=mx.t[:], in0=mx.t[:],
                                                                       in1=sinks.t[:, kvh * 4:kvh * 4 + 4], op=ALU.max),
                             r=[mx.r, sinks.r], w=[mx.r])
                        S.op("dve", lambda e, kvh=kvh: e.tensor_tensor(out=dd.t[:], in0=sinks.t[:, kvh * 4:kvh * 4 + 4],
                                                                       in1=mx.t[:], op=ALU.subtract),
                             r=[mx.r, sinks.r], w=[dd.r])
                        S.op("dve", lambda e: e.tensor_scalar(out=mx.t[:], in0=mx.t[:], scalar1=-1.0, scalar2=None,
                                                              op0=ALU.mult), r=[mx.r], w=[mx.r])
                        for gq in range(4):
                            S.op("act", lambda e, gq=gq: e.activation(out=pexp.t[:, gq, :], in_=L.t[:, gq, :],
                                                                      func=AF.Exp, bias=mx.t[:, gq:gq + 1], scale=1.0,
                                                                      accum_out=rsum.t[:, gq:gq + 1]),
                                 r=[L.r, mx.r], w=[pexp.r, rsum.r])
                        S.op("act", lambda e: e.activation(out=dd.t[:], in_=dd.t[:], func=AF.Exp), r=[dd.r], w=[dd.r])
                        S.op("dve", lambda e: e.tensor_tensor(out=den.t[:], in0=rsum.t[:], in1=dd.t[:], op=ALU.add),
                             r=[rsum.r, dd.r], w=[den.r])
                        S.op("dve", lambda e: e.reciprocal(out=den.t[:], in_=den.t[:]), r=[den.r], w=[den.r])
                        for gq in range(4):
                            for kb in range(2):
                                S.op("pe", lambda e, gq=gq, kb=kb: e.transpose(
                                    out=PB[1].t[:, (gq * 2 + kb) * 128:(gq * 2 + kb + 1) * 128],
                                    in_=pexp.t[:, gq, kb * 128:(kb + 1) * 128], identity=ident.t[:]),
                                    r=[pexp.r, ident.r], w=[PB[1].r])
                        S.op("act", lambda e: e.activation(
                            out=pT.t[:], in_=PB[1].t[:, 0:1024].rearrange("p (c k) -> p c k", c=8), func=AF.Copy),
                            r=[PB[1].r], w=[pT.r])
                        po = PF[2 + kvh % 2]
                        for gq in range(4):
                            S.op("pe", lambda e, gq=gq, kvh=kvh, j=j, po=po: e.matmul(
                                po.t[:, gq * 64:(gq + 1) * 64], lhsT=pT.t[:, gq * 2, :],
                                rhs=kv_prev.t[:, j, 256 + kvh * 64:256 + (kvh + 1) * 64], start=True, stop=False),
                                r=[pT.r, r_kvp[j]], w=[po.r])
                            S.op("pe", lambda e, gq=gq, kvh=kvh, j=j, po=po: e.matmul(
                                po.t[:, gq * 64:(gq + 1) * 64], lhsT=pT.t[:, gq * 2 + 1, :],
                                rhs=kv_own.t[:, j, 256 + kvh * 64:256 + (kvh + 1) * 64], start=False, stop=True),
                                r=[pT.r, r_kvo[j]], w=[po.r])
                        for gq in range(4):
                            hd = kvh * 4 + gq
                            S.op("dve", lambda e, gq=gq, hd=hd, po=po, sot=sot: e.tensor_scalar(
                                out=sot.t[:, hd * 64:(hd + 1) * 64], in0=po.t[:, gq * 64:(gq + 1) * 64],
                                scalar1=den.t[:, gq:gq + 1], scalar2=None, op0=ALU.mult),
                                r=[po.r, den.r], w=[sot.r])
                    S.dma("sp", lambda e, j=j, sot=sot: e.dma_start(out=swa_d[j * 128:(j + 1) * 128, :], in_=sot.t[:]),
                          r=[sot.r], w=[r_swa[j]])
                S.flush()

        with ExitStack() as P:
            nctx = NormCtx(P, "n3", gmix)
            hTo = sb(P, "hTo", [128, NKC, TK], BF16)
            r_hTo = [Res(f"hTo{j}") for j in range(NJ)]
            for j in range(NJ):
                hj = Tl(hTo.t, "x")
                hj.r = r_hTo[j]
                nctx.run(x_own[j * 128:(j + 1) * 128, :], hj, lambda c, j=j: hTo.t[:, c, j * 128:(j + 1) * 128])
            wg = [sb(P, f"wg{i}", [128, NKC, 512], BF16) for i in range(2)]
            stg = [sb(P, f"stg{i}", [128, 512]) for i in range(3)]
            si = 0
            for g8 in range(8):
                w = wg[g8 % 2]
                srcw = w_gate[:, g8 * 512:(g8 + 1) * 512].rearrange("(c p) n -> p c n", p=128)
                for c0 in range(0, NKC, 8):
                    S.dma("pool", lambda e, w=w, srcw=srcw, c0=c0: e.dma_start(out=w.t[:, c0:c0 + 8, :],
                                                                              in_=srcw[:, c0:c0 + 8, :]), w=[w.r])
                for j in range(NJ):
                    pf = PF[(g8 * NJ + j) % 4]
                    hj = Tl(hTo.t, "x")
                    hj.r = r_hTo[j]
                    proj(hj, lambda c, j=j: hTo.t[:, c, j * 128:(j + 1) * 128], w, 0, 512, pf)
                    st = stg[si % 3]
                    si += 1
                    if si % 2:
                        S.op("act", lambda e, st=st, pf=pf: e.activation(out=st.t[:], in_=pf.t[:, 0:512], func=AF.Copy),
                             r=[pf.r], w=[st.r])
                    else:
                        S.op("dve", lambda e, st=st, pf=pf: e.tensor_copy(out=st.t[:], in_=pf.t[:, 0:512]),
                             r=[pf.r], w=[st.r])
                    S.dma("sp", lambda e, st=st, j=j, g8=g8: e.dma_start(
                        out=zg_d[j * 128:(j + 1) * 128, g8 * 512:(g8 + 1) * 512], in_=st.t[:]),
                        r=[st.r], w=[r_zg[j]])
            S.flush()

        with ExitStack() as P:
            wk = load_w(P, "wukvk", w_ukv_k, 2, 1024)
            wv = load_w(P, "wukvv", w_ukv_v, 2, 1024)
            mmask = sb(P, "mmask", [128, 512])
            S.dma("sp", lambda e: e.dma_start(out=mmask.t[:], in_=mla_mask), w=[mmask.r])
            for tab, c0 in ((u_tab, 0), (v_tab, D)):
                for i in range(16):
                    S.dma("pool", lambda e, tab=tab, c0=c0, i=i: e.dma_start(
                        out=uv_b[i * 1024:(i + 1) * 1024, c0:c0 + D], in_=tab[i * 1024:(i + 1) * 1024, :]),
                        w=[r_uvb])
            knT = sb(P, "knT", [128, 2, SQ], BF16)
            Vsb = sb(P, "Vsb", [128, NS, 256], BF16)
            qn2 = sb(P, "qn2", [128, 2, TK], BF16)
            qr2 = sb(P, "qr2", [64, 2, TK], BF16)
            SETS = []
            for bs in range(2):
                SETS.append(dict(
                    S1=PF[0 + 3 * bs], S2=PF[1 + 3 * bs], O=PF[2 + 3 * bs], PT=PB[bs],
                    rmax=sb(P, f"rmax{bs}", [128, NJ]), rsm=sb(P, f"rsm{bs}", [128, NJ]),
                    mrow=sb(P, f"mrow{bs}", [128, 1]), lrow=sb(P, f"lrow{bs}", [128, 1]),
                    Pb=[sb(P, f"Pb{bs}_{i}", [128, 512], BF16) for i in range(2)],
                    Pf=sb(P, f"Pf{bs}", [128, 512]),
                    PTs=[sb(P, f"PTs{bs}_{i}", [128, 4, 128], BF16) for i in range(2)]))
            mo = [sb(P, f"mo{i}", [128, 256], BF16) for i in range(2)]
            allk = list(r_ckvT)

            def interleave(gens):
                gens = list(gens)
                while gens:
                    for g in list(gens):
                        try:
                            next(g)
                        except StopIteration:
                            gens.remove(g)

            def mla_unit(j, hh, B, mot):
                qs = slice(j * 128, (j + 1) * 128)
                rmax, rsm, mrow, lrow = B["rmax"], B["rsm"], B["mrow"], B["lrow"]

                def scores(c, pf):
                    S.op("pe", lambda e: e.matmul(pf.t[:, 0:512], lhsT=qn2.t[:, hh, qs],
                                                  rhs=knT.t[:, hh, c * 512:(c + 1) * 512], start=True,
                                                  stop=False), r=[qn2.r, knT.r], w=[pf.r])
                    S.op("pe", lambda e: e.matmul(pf.t[:, 0:512], lhsT=qr2.t[:, hh, qs],
                                                  rhs=kropeT.t[:, c * 512:(c + 1) * 512], start=False,
                                                  stop=True), r=[qr2.r] + allk[c * 4:c * 4 + 4], w=[pf.r])
                for c in range(j + 1):
                    pf = B["S1"]
                    scores(c, pf)
                    yield
                    S.op("dve", lambda e, c=c, pf=pf: e.tensor_reduce(out=rmax.t[:, c:c + 1], in_=pf.t[:, 0:512],
                                                                     axis=AX.X, op=ALU.max), r=[pf.r], w=[rmax.r])
                S.op("dve", lambda e: e.tensor_reduce(out=mrow.t[:], in_=rmax.t[:, 0:j + 1], axis=AX.X,
                                                      op=ALU.max), r=[rmax.r], w=[mrow.r])
                S.op("dve", lambda e: e.tensor_scalar(out=mrow.t[:], in0=mrow.t[:], scalar1=-MLA_SCALE,
                                                      scalar2=None, op0=ALU.mult), r=[mrow.r], w=[mrow.r])
                po = B["O"]
                for c in range(j + 1):
                    pf = B["S2"]
                    scores(c, pf)
                    yield
                    pbt = B["Pb"][c % 2]
                    if c < j:
                        S.op("act", lambda e, c=c, pf=pf, pbt=pbt: e.activation(
                            out=pbt.t[:], in_=pf.t[:, 0:512], func=AF.Exp, bias=mrow.t[:, 0:1],
                            scale=MLA_SCALE, accum_out=rsm.t[:, c:c + 1]),
                            r=[pf.r, mrow.r], w=[pbt.r, rsm.r])
                    else:
                        Pf = B["Pf"]
                        S.op("act", lambda e, pf=pf, Pf=Pf: e.activation(
                            out=Pf.t[:], in_=pf.t[:, 0:512], func=AF.Exp, bias=mrow.t[:, 0:1],
                            scale=MLA_SCALE), r=[pf.r, mrow.r], w=[Pf.r])
                        S.op("dve", lambda e, Pf=Pf: e.tensor_tensor(out=Pf.t[:], in0=Pf.t[:], in1=mmask.t[:],
                                                                     op=ALU.mult), r=[Pf.r, mmask.r], w=[Pf.r])
                        S.op("dve", lambda e, c=c, Pf=Pf: e.tensor_reduce(out=rsm.t[:, c:c + 1], in_=Pf.t[:],
                                                                          axis=AX.X, op=ALU.add),
                             r=[Pf.r], w=[rsm.r])
                        S.op("act", lambda e, pbt=pbt, Pf=Pf: e.activation(out=pbt.t[:], in_=Pf.t[:], func=AF.Copy),
                             r=[Pf.r], w=[pbt.r])
                    ptp = B["PT"]
                    for kk in range(4):
                        S.op("pe", lambda e, kk=kk, pbt=pbt, ptp=ptp: e.transpose(
                            out=ptp.t[:, kk * 128:(kk + 1) * 128], in_=pbt.t[:, kk * 128:(kk + 1) * 128],
                            identity=ident.t[:]), r=[pbt.r, ident.r], w=[ptp.r])
                    yield
                    pts = B["PTs"][c % 2]
                    if c % 2:
                        S.op("act", lambda e, pts=pts, ptp=ptp: e.activation(
                            out=pts.t[:], in_=ptp.t[:, 0:512].rearrange("p (c k) -> p c k", c=4), func=AF.Copy),
                            r=[ptp.r], w=[pts.r])
                    else:
                        S.op("dve", lambda e, pts=pts, ptp=ptp: e.tensor_copy(
                            out=pts.t[:], in_=ptp.t[:, 0:512].rearrange("p (c k) -> p c k", c=4)),
                            r=[ptp.r], w=[pts.r])
                    for kk in range(4):
                        S.op("pe", lambda e, kk=kk, c=c, pts=pts: e.matmul(
                            po.t[:, 0:128], lhsT=pts.t[:, kk, :],
                            rhs=Vsb.t[:, c * 4 + kk, hh * 128:(hh + 1) * 128],
                            start=(c == 0 and kk == 0), stop=(c == j and kk == 3)),
                            r=[pts.r, Vsb.r], w=[po.r])
                    yield
                S.op("dve", lambda e: e.tensor_reduce(out=lrow.t[:], in_=rsm.t[:, 0:j + 1], axis=AX.X,
                                                      op=ALU.add), r=[rsm.r], w=[lrow.r])
                S.op("dve", lambda e: e.reciprocal(out=lrow.t[:], in_=lrow.t[:]), r=[lrow.r], w=[lrow.r])
                S.op("dve", lambda e: e.tensor_scalar(
                    out=mot.t[:, hh * 128:(hh + 1) * 128], in0=po.t[:, 0:128], scalar1=lrow.t[:, 0:1],
                    scalar2=None, op0=ALU.mult), r=[po.r, lrow.r], w=[mot.r])

            for hp in range(4):
                for hh in range(2):
                    hd = hp * 2 + hh
                    for kc in range(NS // 4):
                        pf = PF[(kc % 2) * 3 + 1]
                        for cc in range(2):
                            S.op("pe", lambda e, cc=cc, hd=hd, kc=kc, pf=pf: e.matmul(
                                pf.t[:, 0:512], lhsT=wk.t[:, cc, hd * 128:(hd + 1) * 128],
                                rhs=ckvT.t[:, cc, kc * 512:(kc + 1) * 512], start=(cc == 0), stop=(cc == 1)),
                                r=[wk.r] + allk[kc * 4:kc * 4 + 4], w=[pf.r])
                        if kc % 2:
                            S.op("act", lambda e, hh=hh, kc=kc, pf=pf: e.activation(
                                out=knT.t[:, hh, kc * 512:(kc + 1) * 512], in_=pf.t[:, 0:512], func=AF.Copy),
                                r=[pf.r], w=[knT.r])
                        else:
                            S.op("dve", lambda e, hh=hh, kc=kc, pf=pf: e.tensor_copy(
                                out=knT.t[:, hh, kc * 512:(kc + 1) * 512], in_=pf.t[:, 0:512]),
                                r=[pf.r], w=[knT.r])
                for kt in range(NS):
                    pf = PF[(kt % 2) * 3 + 2]
                    for cc in range(2):
                        S.op("pe", lambda e, cc=cc, kt=kt, pf=pf, hp=hp: e.matmul(
                            pf.t[:, 0:256], lhsT=ckvT.t[:, cc, kt * 128:(kt + 1) * 128],
                            rhs=wv.t[:, cc, hp * 256:(hp + 1) * 256], start=(cc == 0), stop=(cc == 1)),
                            r=[wv.r, allk[kt]], w=[pf.r])
                    if kt % 2:
                        S.op("act", lambda e, kt=kt, pf=pf: e.activation(
                            out=Vsb.t[:, kt, :], in_=pf.t[:, 0:256], func=AF.Copy), r=[pf.r], w=[Vsb.r])
                    else:
                        S.op("dve", lambda e, kt=kt, pf=pf: e.tensor_copy(out=Vsb.t[:, kt, :], in_=pf.t[:, 0:256]),
                             r=[pf.r], w=[Vsb.r])
                S.dma("sp", lambda e, hp=hp: e.dma_start(out=qn2.t[:], in_=qnT_d[hp * 2:hp * 2 + 2].rearrange(
                    "h p k -> p h k")), r=[r_qT], w=[qn2.r])
                S.dma("sp", lambda e, hp=hp: e.dma_start(out=qr2.t[:], in_=qrT_d[hp * 2:hp * 2 + 2].rearrange(
                    "h p k -> p h k")), r=[r_qT], w=[qr2.r])
                for j in range(NJ):
                    mot = mo[j % 2]
                    interleave([mla_unit(j, 0, SETS[0], mot), mla_unit(j, 1, SETS[1], mot)])
                    S.dma("sp", lambda e, j=j, hp=hp, mot=mot: e.dma_start(
                        out=mla_d[j * 128:(j + 1) * 128, hp * 256:(hp + 1) * 256], in_=mot.t[:]),
                        r=[mot.r], w=[r_mla[j]])
                S.flush()

        GK.close()
        with ExitStack() as P:
            r_mixd = [Res(f"mixd{j}") for j in range(NJ)]
            with ExitStack() as P5:
                wa = load_w(P5, "wa", w_a, 8, D)
                wb = load_w(P5, "wb", w_b, 8, D)
                at = [sb(P5, f"at{i}", [128, 1024], BF16) for i in range(2)]
                bt = [sb(P5, f"bt{i}", [128, 1024], BF16) for i in range(2)]
                aT = sb(P5, "aT", [128, 8, 128], BF16)
                bT = sb(P5, "bT", [128, 8, 128], BF16)
                gt = [sb(P5, f"gt{i}", [128, 4096]) for i in range(2)]
                t1 = sb(P5, "mix_t1", [128, 512])
                mixeds = [sb(P5, f"mixed{i}", [128, D], BF16) for i in range(2)]
                for j in range(NJ):
                    mixed = mixeds[j % 2]
                    a_, b_, g_ = at[j % 2], bt[j % 2], gt[j % 2]
                    S.dma("sp", lambda e, j=j, a_=a_: e.dma_start(out=a_.t[:], in_=swa_d[j * 128:(j + 1) * 128, :]),
                          r=[r_swa[j]], w=[a_.r])
                    S.dma("sp", lambda e, j=j, b_=b_: e.dma_start(out=b_.t[:], in_=mla_d[j * 128:(j + 1) * 128, :]),
                          r=[r_mla[j]], w=[b_.r])
                    S.dma("sp", lambda e, j=j, g_=g_: e.dma_start(out=g_.t[:], in_=zg_d[j * 128:(j + 1) * 128, :]),
                          r=[r_zg[j]], w=[g_.r])
                    S.op("act", lambda e, g_=g_: e.activation(out=g_.t[:], in_=g_.t[:], func=AF.Sigmoid),
                         r=[g_.r], w=[g_.r])
                    for src, dstT, pb in ((a_, aT, PB[0]), (b_, bT, PB[1])):
                        for cc in range(8):
                            S.op("pe", lambda e, cc=cc, src=src, pb=pb: e.transpose(
                                out=pb.t[:, cc * 128:(cc + 1) * 128], in_=src.t[:, cc * 128:(cc + 1) * 128],
                                identity=ident.t[:]), r=[src.r, ident.r], w=[pb.r])
                        S.op("dve", lambda e, dstT=dstT, pb=pb: e.tensor_copy(
                            out=dstT.t[:], in_=pb.t[:, 0:1024].rearrange("p (c k) -> p c k", c=8)),
                            r=[pb.r], w=[dstT.r])
                    for g4 in range(4):
                        pa, pbb = PF[(g4 % 2) * 2], PF[(g4 % 2) * 2 + 1]
                        for cc in range(8):
                            S.op("pe", lambda e, cc=cc, g4=g4, pa=pa: e.matmul(
                                pa.t[:, 0:512], lhsT=aT.t[:, cc, :], rhs=wa.t[:, cc, g4 * 512:(g4 + 1) * 512],
                                start=(cc == 0), stop=(cc == 7)), r=[aT.r, wa.r], w=[pa.r])
                        for cc in range(8):
                            S.op("pe", lambda e, cc=cc, g4=g4, pbb=pbb: e.matmul(
                                pbb.t[:, 0:512], lhsT=bT.t[:, cc, :], rhs=wb.t[:, cc, g4 * 512:(g4 + 1) * 512],
                                start=(cc == 0), stop=(cc == 7)), r=[bT.r, wb.r], w=[pbb.r])
                        S.op("dve", lambda e, g4=g4, pa=pa, g_=g_: e.tensor_tensor(
                            out=t1.t[:], in0=pa.t[:, 0:512], in1=g_.t[:, g4 * 512:(g4 + 1) * 512], op=ALU.mult),
                            r=[pa.r, g_.r], w=[t1.r])
                        S.op("dve", lambda e, g4=g4, pbb=pbb, g_=g_: e.tensor_tensor(
                            out=g_.t[:, 2048 + g4 * 512:2048 + (g4 + 1) * 512], in0=pbb.t[:, 0:512],
                            in1=g_.t[:, 2048 + g4 * 512:2048 + (g4 + 1) * 512], op=ALU.mult),
                            r=[pbb.r, g_.r], w=[g_.r])
                        S.op("dve", lambda e, g4=g4, g_=g_, mixed=mixed: e.tensor_tensor(
                            out=mixed.t[:, g4 * 512:(g4 + 1) * 512], in0=t1.t[:],
                            in1=g_.t[:, 2048 + g4 * 512:2048 + (g4 + 1) * 512], op=ALU.add),
                            r=[t1.r, g_.r], w=[mixed.r])
                    S.dma("sp", lambda e, j=j, mixed=mixed: e.dma_start(out=mixed_d[j * 128:(j + 1) * 128, :],
                                                                        in_=mixed.t[:]), r=[mixed.r], w=[r_mixd[j]])
                S.flush()
            with ExitStack() as P5:
                wo = load_w(P5, "wo", w_o, NKC, D)
                xt = [sb(P5, f"x5_{i}", [128, D]) for i in range(2)]
                mxs = [sb(P5, f"mx5_{i}", [128, D], BF16) for i in range(2)]
                mixTs = [sb(P5, f"mixT5_{i}", [128, NKC, 128], BF16) for i in range(2)]
                for j in range(NJ):
                    x_ = xt[j % 2]
                    mx_ = mxs[j % 2]
                    mixT = mixTs[j % 2]
                    S.dma("sp", lambda e, j=j, x_=x_: e.dma_start(out=x_.t[:], in_=x_own[j * 128:(j + 1) * 128, :]),
                          w=[x_.r])
                    S.dma("sp", lambda e, j=j, mx_=mx_: e.dma_start(out=mx_.t[:], in_=mixed_d[j * 128:(j + 1) * 128, :]),
                          r=[r_mixd[j]], w=[mx_.r])
                    for half in range(2):
                        pb = PB[half]
                        for cc in range(8):
                            c = half * 8 + cc
                            S.op("pe", lambda e, c=c, cc=cc, pb=pb, mx_=mx_: e.transpose(
                                out=pb.t[:, cc * 128:(cc + 1) * 128], in_=mx_.t[:, c * 128:(c + 1) * 128],
                                identity=ident.t[:]), r=[mx_.r, ident.r], w=[pb.r])
                        S.op("act", lambda e, half=half, pb=pb, mixT=mixT: e.activation(
                            out=mixT.t[:, half * 8:(half + 1) * 8, :],
                            in_=pb.t[:, 0:1024].rearrange("p (c k) -> p c k", c=8), func=AF.Copy),
                            r=[pb.r], w=[mixT.r])
                    for g4 in range(4):
                        pf = PF[g4]
                        for c in range(NKC):
                            S.op("pe", lambda e, c=c, g4=g4, pf=pf, mixT=mixT: e.matmul(
                                pf.t[:, 0:512], lhsT=mixT.t[:, c, :],
                                rhs=wo.t[:, c, g4 * 512:(g4 + 1) * 512], start=(c == 0), stop=(c == NKC - 1)),
                                r=[mixT.r, wo.r], w=[pf.r])
                        S.op("dve", lambda e, g4=g4, pf=pf, x_=x_: e.tensor_tensor(
                            out=x_.t[:, g4 * 512:(g4 + 1) * 512], in0=pf.t[:, 0:512],
                            in1=x_.t[:, g4 * 512:(g4 + 1) * 512], op=ALU.add), r=[pf.r, x_.r], w=[x_.r])
                    S.dma("sp", lambda e, j=j, x_=x_: e.dma_start(out=x1_d[j * 128:(j + 1) * 128, :], in_=x_.t[:]),
                          r=[x_.r], w=[r_x1[j]])
                S.flush()

        with ExitStack() as P:
            idx_all = sb(P, "idx_all", [128, NJ, 128], I32)
            gate_all = sb(P, "gate_all", [128, NJ, 128])
            r_idx = [Res(f"idx{j}") for j in range(NJ)]
            with ExitStack() as P6:
                wpq = load_w(P6, "wpq", w_pq, NKC, D)
                kT_f = sb(P6, "kT_f", [128, 16, 128])
                kT = sb(P6, "kT_b", [128, 16, 128], BF16)
                S.dma("sp", lambda e: e.dma_start(out=kT_f.t[:], in_=keysT), w=[kT_f.r])
                S.op("dve", lambda e: e.tensor_copy(out=kT.t[:], in_=kT_f.t[:]), r=[kT_f.r], w=[kT.r])
                gffn = sb(P6, "gffn", [128, D])
                S.dma("sp", lambda e: e.dma_start(out=gffn.t[:], in_=g_ffn_bc), w=[gffn.r])
                iota16 = sb(P6, "iota16", [128, 16])
                S.op("pool", lambda e: e.iota(iota16.t[:], pattern=[[1, 16]], base=0, channel_multiplier=0,
                                              allow_small_or_imprecise_dtypes=True), w=[iota16.r])
                xt = [sb(P6, f"x6_{i}", [128, D]) for i in range(2)]
                junk = sb(P6, "junk6", [128, D], BF16)
                ss = sb(P6, "ss6", [128, 1])
                rstd = sb(P6, "rstd6", [128, 1])
                hn = [sb(P6, f"hn6_{i}", [128, D]) for i in range(2)]
                hnb = sb(P6, "hnb6", [128, D], BF16)
                hT6 = sb(P6, "hT6", [128, NKC, 128], BF16)
                qpT = [sb(P6, f"qpT{i}", [128, 128], BF16) for i in range(2)]
                sc = sb(P6, "sc6", [128, 16, 128])
                scw = sb(P6, "scw6", [128, 16, 128])
                tops = sb(P6, "tops", [128, 16, 16])
                topi = sb(P6, "topi", [128, 16, 16], U32)
                topif = sb(P6, "topif", [128, 16, 16])
                cand = sb(P6, "cand", [128, 8, 256])
                candw = sb(P6, "candw", [128, 8, 256])
                best = sb(P6, "best", [128, 8, 16])
                bpos = sb(P6, "bpos", [128, 8, 16], U32)
                k1u = sb(P6, "k1u", [128, 8, 16], U32)
                k2u = sb(P6, "k2u", [128, 8, 16], U32)
                k1f = sb(P6, "k1f", [128, 8, 16])
                k2f = sb(P6, "k2f", [128, 8, 16])
                oh = sb(P6, "oh6", [128, 16, 16])
                i1 = sb(P6, "i1", [128, 8, 16])
                i2 = sb(P6, "i2", [128, 8, 16])
                bm = sb(P6, "bm", [128, 8])
                bs = sb(P6, "bs", [128, 8])
                for j in range(NJ):
                    x_ = xt[j % 2]
                    hn_ = hn[j % 2]
                    S.dma("sp", lambda e, j=j, x_=x_: e.dma_start(out=x_.t[:], in_=x1_d[j * 128:(j + 1) * 128, :]),
                          r=[r_x1[j]], w=[x_.r])
                    S.op("act", lambda e, x_=x_: e.activation(out=junk.t[:], in_=x_.t[:], func=AF.Square,
                                                              accum_out=ss.t[:]), r=[x_.r], w=[junk.r, ss.r])
                    rstd_from_ss(ss, rstd, D)
                    S.op("dve", lambda e, x_=x_, hn_=hn_: e.scalar_tensor_tensor(
                        out=hn_.t[:], in0=x_.t[:], scalar=rstd.t[:, 0:1], in1=gffn.t[:], op0=ALU.mult, op1=ALU.mult),
                        r=[x_.r, rstd.r, gffn.r], w=[hn_.r])
                    S.dma("sp", lambda e, j=j, hn_=hn_: e.dma_start(out=hn2_d[j * 128:(j + 1) * 128, :], in_=hn_.t[:]),
                          r=[hn_.r], w=[r_hn2[j]])
                    S.op("act", lambda e, hn_=hn_: e.activation(out=hnb.t[:], in_=hn_.t[:], func=AF.Copy),
                         r=[hn_.r], w=[hnb.r])
                    for half in range(2):
                        pb = PB[half]
                        for cc in range(8):
                            c = half * 8 + cc
                            S.op("pe", lambda e, c=c, cc=cc, pb=pb: e.transpose(
                                out=pb.t[:, cc * 128:(cc + 1) * 128], in_=hnb.t[:, c * 128:(c + 1) * 128],
                                identity=ident.t[:]), r=[hnb.r, ident.r], w=[pb.r])
                        S.op("dve" if half else "act", (lambda e, half=half, pb=pb: e.tensor_copy(
                            out=hT6.t[:, half * 8:(half + 1) * 8, :],
                            in_=pb.t[:, 0:1024].rearrange("p (c k) -> p c k", c=8))) if half else
                            (lambda e, half=half, pb=pb: e.activation(
                                out=hT6.t[:, half * 8:(half + 1) * 8, :],
                                in_=pb.t[:, 0:1024].rearrange("p (c k) -> p c k", c=8), func=AF.Copy)),
                            r=[pb.r], w=[hT6.r])
                    for hp_ in range(16):
                        pq = PF[4 + hp_ % 2]
                        for c in range(NKC):
                            S.op("pe", lambda e, c=c, hp_=hp_, pq=pq: e.matmul(
                                pq.t[:, 0:128], lhsT=wpq.t[:, c, hp_ * 128:(hp_ + 1) * 128], rhs=hT6.t[:, c, :],
                                start=(c == 0), stop=(c == NKC - 1)), r=[wpq.r, hT6.r], w=[pq.r])
                        qp = qpT[hp_ % 2]
                        S.op("act", lambda e, pq=pq, qp=qp: e.activation(out=qp.t[:], in_=pq.t[:, 0:128], func=AF.Copy),
                             r=[pq.r], w=[qp.r])
                        psc = PF[hp_ // 4]
                        S.op("pe", lambda e, hp_=hp_, qp=qp, psc=psc: e.matmul(
                            psc.t[:, (hp_ % 4) * 128:(hp_ % 4 + 1) * 128], lhsT=qp.t[:], rhs=kT.t[:, hp_, :],
                            start=True, stop=True), r=[qp.r, kT.r], w=[psc.r])
                    for q4 in range(4):
                        S.op("dve", lambda e, q4=q4: e.tensor_copy(
                            out=sc.t[:, q4 * 4:(q4 + 1) * 4, :],
                            in_=PF[q4].t[:, 0:512].rearrange("p (a k) -> p a k", a=4)), r=[PF[q4].r], w=[sc.r])
                    for hp_ in range(16):
                        S.op("dve", lambda e, hp_=hp_: e.max(out=tops.t[:, hp_, 0:8], in_=sc.t[:, hp_, :]),
                             r=[sc.r], w=[tops.r])
                        S.op("dve", lambda e, hp_=hp_: e.max_index(out=topi.t[:, hp_, 0:8], in_max=tops.t[:, hp_, 0:8],
                                                                   in_values=sc.t[:, hp_, :]),
                             r=[sc.r, tops.r], w=[topi.r])
                        S.op("dve", lambda e, hp_=hp_: e.match_replace(out=scw.t[:, hp_, :],
                                                                       in_to_replace=tops.t[:, hp_, 0:8],
                                                                       in_values=sc.t[:, hp_, :], imm_value=-1e30),
                             r=[sc.r, tops.r], w=[scw.r])
                        S.op("dve", lambda e, hp_=hp_: e.max(out=tops.t[:, hp_, 8:16], in_=scw.t[:, hp_, :]),
                             r=[scw.r], w=[tops.r])
                        S.op("dve", lambda e, hp_=hp_: e.max_index(out=topi.t[:, hp_, 8:16],
                                                                   in_max=tops.t[:, hp_, 8:16],
                                                                   in_values=scw.t[:, hp_, :]),
                             r=[scw.r, tops.r], w=[topi.r])
                    S.op("dve", lambda e: e.tensor_copy(out=topif.t[:], in_=topi.t[:]), r=[topi.r], w=[topif.r])
                    for hd in range(8):
                        S.op("dve", lambda e, hd=hd: e.tensor_tensor(
                            out=cand.t[:, hd, :].rearrange("p (a b) -> p a b", a=16),
                            in0=tops.t[:, 2 * hd, :].unsqueeze(2).to_broadcast([128, 16, 16]),
                            in1=tops.t[:, 2 * hd + 1, :].unsqueeze(1).to_broadcast([128, 16, 16]), op=ALU.add),
                            r=[tops.r], w=[cand.r])
                    for hd in range(8):
                        S.op("dve", lambda e, hd=hd: e.max(out=best.t[:, hd, 0:8], in_=cand.t[:, hd, :]),
                             r=[cand.r], w=[best.r])
                        S.op("dve", lambda e, hd=hd: e.max_index(out=bpos.t[:, hd, 0:8], in_max=best.t[:, hd, 0:8],
                                                                 in_values=cand.t[:, hd, :]),
                             r=[cand.r, best.r], w=[bpos.r])
                        S.op("dve", lambda e, hd=hd: e.match_replace(out=candw.t[:, hd, :],
                                                                     in_to_replace=best.t[:, hd, 0:8],
                                                                     in_values=cand.t[:, hd, :], imm_value=-1e30),
                             r=[cand.r, best.r], w=[candw.r])
                        S.op("dve", lambda e, hd=hd: e.max(out=best.t[:, hd, 8:16], in_=candw.t[:, hd, :]),
                             r=[candw.r], w=[best.r])
                        S.op("dve", lambda e, hd=hd: e.max_index(out=bpos.t[:, hd, 8:16], in_max=best.t[:, hd, 8:16],
                                                                 in_values=candw.t[:, hd, :]),
                             r=[candw.r, best.r], w=[bpos.r])
                    S.op("dve", lambda e: e.tensor_single_scalar(out=k1u.t[:], in_=bpos.t[:], scalar=4,
                                                                 op=ALU.logical_shift_right), r=[bpos.r], w=[k1u.r])
                    S.op("dve", lambda e: e.tensor_single_scalar(out=k2u.t[:], in_=bpos.t[:], scalar=15,
                                                                 op=ALU.bitwise_and), r=[bpos.r], w=[k2u.r])
                    S.op("dve", lambda e: e.tensor_copy(out=k1f.t[:], in_=k1u.t[:]), r=[k1u.r], w=[k1f.r])
                    S.op("dve", lambda e: e.tensor_copy(out=k2f.t[:], in_=k2u.t[:]), r=[k2u.r], w=[k2f.r])
                    for hd in range(8):
                        for side, kf_, io_ in ((0, k1f, i1), (1, k2f, i2)):
                            S.op("dve", lambda e, hd=hd, kf_=kf_: e.tensor_tensor(
                                out=oh.t[:], in0=iota16.t[:].unsqueeze(1).to_broadcast([128, 16, 16]),
                                in1=kf_.t[:, hd, :].unsqueeze(2).to_broadcast([128, 16, 16]), op=ALU.is_equal),
                                r=[iota16.r, kf_.r], w=[oh.r])
                            S.op("dve", lambda e, hd=hd, side=side: e.tensor_tensor(
                                out=oh.t[:], in0=oh.t[:],
                                in1=topif.t[:, 2 * hd + side, :].unsqueeze(1).to_broadcast([128, 16, 16]),
                                op=ALU.mult), r=[oh.r, topif.r], w=[oh.r])
                            S.op("dve", lambda e, hd=hd, io_=io_: e.tensor_reduce(out=io_.t[:, hd, :], in_=oh.t[:],
                                                                                  axis=AX.X, op=ALU.add),
                                 r=[oh.r], w=[io_.r])
                    S.op("dve", lambda e: e.scalar_tensor_tensor(out=i1.t[:], in0=i1.t[:], scalar=128.0, in1=i2.t[:],
                                                                 op0=ALU.mult, op1=ALU.add),
                         r=[i1.r, i2.r], w=[i1.r])
                    S.op("dve", lambda e, j=j: e.tensor_copy(
                        out=idx_all.t[:, j, :].rearrange("p (a b) -> p a b", a=8), in_=i1.t[:]),
                        r=[i1.r], w=[r_idx[j]])
                    S.op("dve", lambda e: e.tensor_reduce(out=bm.t[:], in_=best.t[:], axis=AX.X, op=ALU.max),
                         r=[best.r], w=[bm.r])
                    S.op("dve", lambda e: e.tensor_tensor(out=best.t[:], in0=best.t[:],
                                                          in1=bm.t[:].unsqueeze(2).to_broadcast([128, 8, 16]),
                                                          op=ALU.subtract), r=[best.r, bm.r], w=[best.r])
                    S.op("act", lambda e: e.activation(out=best.t[:], in_=best.t[:], func=AF.Exp),
                         r=[best.r], w=[best.r])
                    S.op("dve", lambda e: e.tensor_reduce(out=bs.t[:], in_=best.t[:], axis=AX.X, op=ALU.add),
                         r=[best.r], w=[bs.r])
                    S.op("dve", lambda e: e.reciprocal(out=bs.t[:], in_=bs.t[:]), r=[bs.r], w=[bs.r])
                    S.op("dve", lambda e, j=j: e.tensor_tensor(
                        out=gate_all.t[:, j, :].rearrange("p (a b) -> p a b", a=8), in0=best.t[:],
                        in1=bs.t[:].unsqueeze(2).to_broadcast([128, 8, 16]), op=ALU.mult),
                        r=[best.r, bs.r], w=[r_idx[j]])
                S.flush()

            with ExitStack() as P7:
                gfin = sb(P7, "gfin", [128, D])
                S.dma("sp", lambda e: e.dma_start(out=gfin.t[:], in_=g_fin_bc), w=[gfin.r])
                NB = 12
                GS = 4
                gb = [sb(P7, f"gb{i}", [128, 2 * D], BF16) for i in range(NB)]
                hn7 = [sb(P7, f"hn7_{i}", [128, D]) for i in range(1)]
                hnb7 = [sb(P7, f"hnb7_{i}", [128, D], BF16) for i in range(2)]
                x7 = [sb(P7, f"x7_{i}", [128, D]) for i in range(2)]
                acc = sb(P7, "acc7", [128, D])
                junkb = sb(P7, "junk7b", [128, D], BF16)
                diag = [sb(P7, f"diag{i}", [128, 128], BF16) for i in range(12)]
                prods = [sb(P7, f"prod{i}", [128, D], BF16) for i in range(3)]
                junkd = sb(P7, "junk7d", [128, D], BF16)
                avt = [sb(P7, f"av7_{i}", [128, 128]) for i in range(2)]
                tvt = [sb(P7, f"tv7_{i}", [128, 128]) for i in range(2)]
                wvt = [sb(P7, f"wv7_{i}", [128, 128]) for i in range(2)]
                ss = sb(P7, "ss7", [128, 1])
                rstd = sb(P7, "rstd7", [128, 1])
                bi = 0
                gi = 0
                for j in range(NJ):
                    hn_ = hn7[0]
                    hb_ = hnb7[j % 2]
                    x_ = x7[j % 2]
                    S.dma("sp", lambda e, j=j, hn_=hn_: e.dma_start(out=hn_.t[:], in_=hn2_d[j * 128:(j + 1) * 128, :]),
                          r=[r_hn2[j]], w=[hn_.r])
                    S.dma("sp", lambda e, j=j, x_=x_: e.dma_start(out=x_.t[:], in_=x1_d[j * 128:(j + 1) * 128, :]),
                          r=[r_x1[j]], w=[x_.r])
                    S.op("act", lambda e, hn_=hn_, hb_=hb_: e.activation(out=hb_.t[:], in_=hn_.t[:], func=AF.Copy),
                         r=[hn_.r], w=[hb_.r])
                    av = avt[j % 2]
                    tv = tvt[j % 2]
                    wv7 = wvt[j % 2]
                    L1, L2 = 3, 2
                    gl = {}

                    def stage1(s_, j=j, hb_=hb_, av=av, gl=gl):
                        nonlocal bi
                        g_ = gb[bi % NB]
                        bi += 1
                        gl[s_] = g_
                        S.dma("pool", lambda e: e.indirect_dma_start(
                            out=g_.t[:], out_offset=None, in_=uv_b,
                            in_offset=bass.IndirectOffsetOnAxis(ap=idx_all.t[:, j, s_:s_ + 1], axis=0)),
                            r=[r_idx[j], r_uvb], w=[g_.r])
                        if s_ % 2 == 0:
                            S.op("dve", lambda e: e.scalar_tensor_tensor(
                                out=junkd.t[:], in0=g_.t[:, 0:D], scalar=1.0, in1=hb_.t[:], op0=ALU.mult,
                                op1=ALU.mult, accum_out=av.t[:, s_:s_ + 1]), r=[g_.r, hb_.r], w=[junkd.r, av.r])
                        else:
                            pr = prods[(s_ // 2) % 3]
                            S.op("dve", lambda e: e.tensor_tensor(
                                out=pr.t[:], in0=g_.t[:, 0:D], in1=hb_.t[:], op=ALU.mult),
                                r=[g_.r, hb_.r], w=[pr.r])
                            S.op("act", lambda e: e.activation(
                                out=junkb.t[:], in_=pr.t[:], func=AF.Copy, accum_out=av.t[:, s_:s_ + 1]),
                                r=[pr.r], w=[junkb.r, av.r])

                    def stage2a(s_, j=j, av=av, tv=tv, wv7=wv7):
                        S.op("act", lambda e: e.activation(out=tv.t[:, s_:s_ + 1], in_=av.t[:, s_:s_ + 1],
                                                           func=AF.Gelu), r=[av.r], w=[tv.r])
                        S.op("dve", lambda e: e.tensor_tensor(
                            out=wv7.t[:, s_:s_ + 1], in0=tv.t[:, s_:s_ + 1], in1=gate_all.t[:, j, s_:s_ + 1],
                            op=ALU.mult), r=[tv.r, r_idx[j]], w=[wv7.r])

                    def stage2b(s_, wv7=wv7, gl=gl):
                        g_ = gl.pop(s_)
                        dg = diag[s_ % 12]
                        S.op("act", lambda e: e.activation(
                            out=dg.t[:], in_=ident.t[:], func=AF.Copy, scale=wv7.t[:, s_:s_ + 1]),
                            r=[ident.r, wv7.r], w=[dg.r])
                        for q4 in range(4):
                            S.op("pe", lambda e, q4=q4: e.matmul(
                                PF[q4].t[:, 0:512], lhsT=dg.t[:], rhs=g_.t[:, D + q4 * 512:D + (q4 + 1) * 512],
                                start=(s_ == 0), stop=(s_ == 127)), r=[dg.r, g_.r], w=[PF[q4].r])

                    for i in range(128 + L1 + L2):
                        if i < 128:
                            stage1(i)
                        if 0 <= i - L1 < 128:
                            stage2a(i - L1)
                        if 0 <= i - L1 - L2 < 128:
                            stage2b(i - L1 - L2)
                    for q4 in range(4):
                        S.op("dve", lambda e, q4=q4, x_=x_: e.tensor_tensor(
                            out=x_.t[:, q4 * 512:(q4 + 1) * 512], in0=PF[q4].t[:, 0:512],
                            in1=x_.t[:, q4 * 512:(q4 + 1) * 512], op=ALU.add), r=[PF[q4].r, x_.r], w=[x_.r])
                    S.op("act", lambda e, x_=x_: e.activation(out=junkb.t[:], in_=x_.t[:], func=AF.Square,
                                                              accum_out=ss.t[:]), r=[x_.r], w=[junkb.r, ss.r])
                    rstd_from_ss(ss, rstd, D)
                    S.op("dve", lambda e, x_=x_: e.scalar_tensor_tensor(
                        out=acc.t[:], in0=x_.t[:], scalar=rstd.t[:, 0:1], in1=gfin.t[:], op0=ALU.mult, op1=ALU.mult),
                        r=[x_.r, rstd.r, gfin.r], w=[acc.r])
                    S.dma("sp", lambda e, j=j: e.dma_start(out=out_d[j * 128:(j + 1) * 128, :], in_=acc.t[:]),
                          r=[acc.r], w=[Res("out")])
                S.flush()
    return nc


def make_in_maps(inp, NJ, cores):
    NS = 4 * NJ
    SQ = 128 * NS
    f32 = np.float32
    x = np.asarray(inp["x"], f32)
    pos = np.asarray(inp["positions"], np.int32)
    w_in = np.asarray(inp["w_in"], f32)[0]
    offs = np.cumsum([0, 1024, 256, 256, 512, 256, 64, 2048, 2048])
    q_a, k_a, v_a, c_q, c_kv, k_r, g_a, g_b = [slice(int(offs[i]), int(offs[i + 1])) for i in range(8)]
    C = np.ascontiguousarray

    def bc(v, n=128):
        return C(np.broadcast_to(np.asarray(v, f32)[None, :], (n, len(v))))

    w_ukv = np.asarray(inp["w_ukv"], f32)[0].reshape(256, 8, 256)
    shared = {
        "invf_bc": bc(1.0 / (10000.0 ** (np.arange(0, 64, 2, dtype=np.float32) / 64)).astype(f32)),
        "g_mix_pc": C(np.asarray(inp["g_mix"], f32)[0].reshape(NKC, 128).T),
        "g_ffn_bc": bc(np.asarray(inp["g_ffn"], f32)[0]),
        "g_fin_bc": bc(np.asarray(inp["g_final"], f32)),
        "g_ckv_bc": bc(np.asarray(inp["g_ckv"], f32)[0]),
        "g_cq_bc": bc(np.asarray(inp["g_cq"], f32)[0]),
        "sinks_bc": bc(np.asarray(inp["sinks"], f32)[0]),
        "w_kv": C(np.concatenate([w_in[:, c_kv], w_in[:, k_r]], axis=1)),
        "w_kva": C(np.concatenate([w_in[:, k_a], w_in[:, v_a]], axis=1)),
        "w_cq": C(w_in[:, c_q]),
        "w_qa": C(w_in[:, q_a]),
        "w_gate": C(np.concatenate([w_in[:, g_a], w_in[:, g_b]], axis=1)),
        "w_uq": C(np.asarray(inp["w_uq"], f32)[0]),
        "w_ukv_k": C(w_ukv[:, :, :128].reshape(256, 1024)),
        "w_ukv_v": C(w_ukv[:, :, 128:].reshape(256, 1024)),
        "w_a": C(np.asarray(inp["w_a_proj"], f32)[0]),
        "w_b": C(np.asarray(inp["w_b_proj"], f32)[0]),
        "w_o": C(np.asarray(inp["w_o"], f32)[0]),
        "w_pq": C(np.asarray(inp["w_peer_q"], f32)[0]),
        "keysT": C(np.asarray(inp["peer_keys"], f32)[0].reshape(16, 128, 128).transpose(2, 0, 1)),
        "u_tab": C(np.asarray(inp["peer_u"], f32)[0]),
        "v_tab": C(np.asarray(inp["peer_v"], f32)[0]),
        "ident": np.eye(128, dtype=f32),
    }
    qi = np.arange(128)[:, None]
    kj = np.arange(256)[None, :]
    rel = 128 + qi - kj
    band = (rel >= 0) & (rel < 128)
    maps = []
    for c in cores:
        b, r = c // 4, c % 4
        xs = x[b, :SQ]
        ps = pos[b, :SQ]
        own = [4 * j + r for j in range(NJ)]
        m = dict(shared)
        m["x_seq"] = C(xs)
        m["x_own"] = C(np.concatenate([xs[t * 128:(t + 1) * 128] for t in own], axis=0))
        xp = []
        pk = []
        for t in own:
            if t == 0:
                xp.append(np.zeros((128, D), f32))
                pprev = np.zeros(128, np.int32)
            else:
                xp.append(xs[(t - 1) * 128:t * 128])
                pprev = ps[(t - 1) * 128:t * 128]
            row = np.concatenate([pprev, ps[t * 128:(t + 1) * 128]])
            pk.append(np.broadcast_to(row[None, :], (128, 256)))
        m["x_prev"] = C(np.concatenate(xp, axis=0))
        m["pos_kv_bc"] = C(np.stack(pk, axis=0).astype(np.int32))
        m["pos_seq"] = C(ps.reshape(NS, 128).T)
        m["pos_own"] = C(np.stack([ps[t * 128:(t + 1) * 128] for t in own], axis=1))
        mb0 = np.where(band & ((kj >= 128) | (own[0] > 0)), 0.0, -1e30).astype(f32)
        mb1 = np.where(band, 0.0, -1e30).astype(f32)
        m["swa_maskb"] = C(np.stack([mb0, mb1], axis=0))
        mm = np.zeros((128, 512), f32)
        for kk in range(4):
            if kk < r:
                mm[:, kk * 128:(kk + 1) * 128] = 1.0
            elif kk == r:
                mm[:, kk * 128:(kk + 1) * 128] = (np.arange(128)[None, :] <= np.arange(128)[:, None])
        m["mla_mask"] = mm
        maps.append(m)
    return maps


def run(inputs, NJ, cores):
    nc = build(NJ)
    maps = make_in_maps(inputs, NJ, cores)
    res = run_bass_kernel_spmd(nc, maps, core_ids=list(range(len(cores))))
    return [r["out"] for r in res.results]


def kernel(**inputs):
    NJ = 16
    cores = list(range(8))
    outs = run(inputs, NJ, cores)
    B, SQ = 2, 8192
    out = np.empty((B, SQ, D), np.float32)
    for c, o in zip(cores, outs):
        b, r = c // 4, c % 4
        for j in range(NJ):
            t = 4 * j + r
            out[b, t * 128:(t + 1) * 128] = o[j * 128:(j + 1) * 128]
    return out
```

```python
import math
from contextlib import ExitStack

import numpy as np
import concourse.bass as bass
import concourse.mybir as mybir
from concourse.bass_utils import run_bass_kernel_spmd

F32 = mybir.dt.float32
BF16 = mybir.dt.bfloat16
I32 = mybir.dt.int32
U32 = mybir.dt.uint32
AF = mybir.ActivationFunctionType
ALU = mybir.AluOpType
AX = mybir.AxisListType

D = 2048
EPS = 1e-6
NKC = D // 128
PI = math.pi
TWO_PI = 2.0 * math.pi
MLA_SCALE = 192 ** -0.5
N_EXP = 16384


class Res:
    __slots__ = ("name", "w", "rd")

    def __init__(self, name=""):
        self.name = name
        self.w = None
        self.rd = {}


class Sched:
    ENGS = ("pe", "act", "dve", "pool", "sp")
    NDMA = {"sp": 8, "act": 4, "pool": 8}

    def __init__(self, nc, es):
        self.nc = nc
        self.sem = {}
        for e in ("pe", "act", "dve", "pool"):
            self.sem[e] = es.enter_context(nc.semaphore("c_" + e))
        for q, n in self.NDMA.items():
            for k in range(n):
                self.sem[("d", q, k)] = es.enter_context(nc.semaphore(f"d_{q}{k}"))
        self.cnt = {k: 0 for k in self.sem}
        self.seen = {e: {} for e in self.ENGS}
        self.items = {e: [] for e in self.ENGS}
        self.drr = {q: 0 for q in self.NDMA}

    def _deps(self, eng, reads, writes):
        need = {}

        def add(kv):
            if kv is None:
                return
            k, v = kv
            if k == "pe" and eng == "pe":
                return
            if need.get(k, 0) < v:
                need[k] = v
        for r in reads:
            add(r.w)
        for w in writes:
            add(w.w)
            for k, v in w.rd.items():
                add((k, v))
        waits = []
        sn = self.seen[eng]
        for k, v in need.items():
            if sn.get(k, 0) < v:
                sn[k] = v
                waits.append((k, v))
        return waits

    @staticmethod
    def _mark(key, val, reads, writes):
        for r in reads:
            if r.rd.get(key, 0) < val:
                r.rd[key] = val
        for w in writes:
            w.w = (key, val)
            w.rd = {}

    def op(self, eng, fn, r=(), w=()):
        waits = self._deps(eng, r, w)
        self.cnt[eng] += 1
        self._mark(eng, self.cnt[eng], r, w)
        self.items[eng].append((waits, fn, (eng, 1)))

    def dma(self, q, fn, r=(), w=()):
        k = self.drr[q]
        self.drr[q] = (k + 1) % self.NDMA[q]
        key = ("d", q, k)
        waits = self._deps(q, r, w)
        prev = self.cnt[key]
        if prev and self.seen[q].get(key, 0) < prev:
            self.seen[q][key] = prev
            waits.append((key, prev))
        self.cnt[key] += 16
        self._mark(key, self.cnt[key], r, w)
        self.items[q].append((waits, fn, (key, 16)))

    def barrier(self):
        for e in self.ENGS:
            waits = []
            for k, v in self.cnt.items():
                if v and self.seen[e].get(k, 0) < v:
                    self.seen[e][k] = v
                    waits.append((k, v))
            if waits:
                self.items[e].append((waits, None, None))

    def flush(self):
        self.barrier()
        nc, items, sem = self.nc, self.items, self.sem

        def emit(h, lst):
            for waits, fn, inc in lst:
                for k, v in waits:
                    h.wait_ge(sem[k], v)
                if fn is not None:
                    fn(h).then_inc(sem[inc[0]], inc[1])

        with nc.Block() as block:
            @block.tensor
            def _(e):
                emit(e, items["pe"])

            @block.scalar
            def _(e):
                emit(e, items["act"])

            @block.vector
            def _(e):
                emit(e, items["dve"])

            @block.gpsimd
            def _(e):
                emit(e, items["pool"])

            @block.sync
            def _(e):
                emit(e, items["sp"])
        self.items = {e: [] for e in self.ENGS}


class Tl:
    __slots__ = ("t", "r")

    def __init__(self, t, name):
        self.t = t
        self.r = Res(name)


def build(NJ):
    NS = 4 * NJ
    SQ = 128 * NS
    TK = 128 * NJ
    nc = bass.Bass("TRN2", target_bir_lowering=False)

    def din(name, shape, dt=F32):
        return nc.dram_tensor(name, list(shape), dt, kind="ExternalInput").ap()

    def dscr(name, shape, dt):
        return nc.dram_tensor(name, list(shape), dt, kind="Internal").ap()

    x_seq = din("x_seq", [SQ, D])
    x_own = din("x_own", [TK, D])
    x_prev = din("x_prev", [TK, D])
    pos_seq = din("pos_seq", [128, NS], I32)
    pos_own = din("pos_own", [128, NJ], I32)
    pos_kv_bc = din("pos_kv_bc", [NJ, 128, 256], I32)
    swa_maskb = din("swa_maskb", [2, 128, 256])
    mla_mask = din("mla_mask", [128, 512])
    invf_bc = din("invf_bc", [128, 32])
    g_mix_pc = din("g_mix_pc", [128, NKC])
    g_ffn_bc = din("g_ffn_bc", [128, D])
    g_fin_bc = din("g_fin_bc", [128, D])
    g_ckv_bc = din("g_ckv_bc", [128, 256])
    g_cq_bc = din("g_cq_bc", [128, 512])
    sinks_bc = din("sinks_bc", [128, 16])
    w_kv = din("w_kv", [D, 320])
    w_kva = din("w_kva", [D, 512])
    w_cq = din("w_cq", [D, 512])
    w_qa = din("w_qa", [D, 1024])
    w_gate = din("w_gate", [D, 4096])
    w_uq = din("w_uq", [512, 1536])
    w_ukv_k = din("w_ukv_k", [256, 1024])
    w_ukv_v = din("w_ukv_v", [256, 1024])
    w_a = din("w_a", [1024, D])
    w_b = din("w_b", [1024, D])
    w_o = din("w_o", [D, D])
    w_pq = din("w_pq", [D, D])
    keysT = din("keysT", [128, 16, 128])
    u_tab = din("u_tab", [N_EXP, D])
    v_tab = din("v_tab", [N_EXP, D])
    ident_in = din("ident", [128, 128])
    out_d = nc.dram_tensor("out", [TK, D], F32, kind="ExternalOutput").ap()

    qnT_d = dscr("qnT_d", [8, 128, TK], BF16)
    qrT_d = dscr("qrT_d", [8, 64, TK], BF16)
    swa_d = dscr("swa_d", [TK, 1024], BF16)
    mla_d = dscr("mla_d", [TK, 1024], BF16)
    zg_d = dscr("zg_d", [TK, 4096], F32)
    x1_d = dscr("x1_d", [TK, D], F32)
    mixed_d = dscr("mixed_d", [TK, D], BF16)
    hn2_d = dscr("hn2_d", [TK, D], F32)
    uv_b = dscr("uv_b", [N_EXP, 2 * D], BF16)
    r_uvb = Res("uv_b")
    r_qT = Res("qT_d")
    r_swa = [Res(f"swa_d{j}") for j in range(NJ)]
    r_mla = [Res(f"mla_d{j}") for j in range(NJ)]
    r_zg = [Res(f"zg_d{j}") for j in range(NJ)]
    r_x1 = [Res(f"x1_d{j}") for j in range(NJ)]
    r_hn2 = [Res(f"hn2_d{j}") for j in range(NJ)]

    slopes = [2.0 ** (-8.0 * (h + 1) / 16) for h in range(16)]

    with ExitStack() as G:
        S = Sched(nc, G)

        def sb(es, name, shape, dt=F32):
            return Tl(es.enter_context(nc.sbuf_tensor(name, list(shape), dt)), name)

        def psum(es, name, shape, dt=F32):
            return Tl(es.enter_context(nc.psum_tensor(name, list(shape), dt)), name)

        PF = [psum(G, f"pf{i}", [128, 512], F32) for i in range(6)]
        PB = [psum(G, f"pb{i}", [128, 1024], BF16) for i in range(2)]

        ident_f = sb(G, "ident_f", [128, 128])
        ident = sb(G, "ident_b", [128, 128], BF16)
        S.dma("sp", lambda e: e.dma_start(out=ident_f.t[:], in_=ident_in), w=[ident_f.r])
        S.op("dve", lambda e: e.tensor_copy(out=ident.t[:], in_=ident_f.t[:]), r=[ident_f.r], w=[ident.r])
        gmix = sb(G, "gmix", [128, NKC])
        S.dma("sp", lambda e: e.dma_start(out=gmix.t[:], in_=g_mix_pc), w=[gmix.r])
        invf = sb(G, "invf", [128, 32])
        S.dma("sp", lambda e: e.dma_start(out=invf.t[:], in_=invf_bc), w=[invf.r])
        GK = ExitStack()
        ckvT = sb(GK, "ckvT", [128, 2, SQ], BF16)
        kropeT = sb(GK, "kropeT", [128, SQ], BF16)
        S.op("dve", lambda e: e.memset(kropeT.t[64:128, :], 0.0), w=[kropeT.r])
        r_ckvT = [Res(f"ckvT{t}") for t in range(NS)]

        def load_w(es, name, w_ap, kchunks, ncols, dst=None, col0=0, ncol_dst=None):
            t = dst if dst is not None else sb(es, name, [128, kchunks, ncols], BF16)
            src = w_ap.rearrange("(c p) n -> p c n", p=128)
            step = max(1, 4096 // ncols)
            for c0 in range(0, kchunks, step):
                c1 = min(kchunks, c0 + step)
                S.dma("pool", lambda e, c0=c0, c1=c1: e.dma_start(out=t.t[:, c0:c1, :], in_=src[:, c0:c1, :]),
                      w=[t.r])
            return t

        def rstd_from_ss(ss, rstd, n):
            S.op("dve", lambda e: e.tensor_scalar(out=rstd.t[:], in0=ss.t[:], scalar1=1.0 / n, scalar2=EPS,
                                                  op0=ALU.mult, op1=ALU.add), r=[ss.r], w=[rstd.r])
            S.op("act", lambda e: e.activation(out=rstd.t[:], in_=rstd.t[:], func=AF.Sqrt), r=[rstd.r], w=[rstd.r])
            S.op("dve", lambda e: e.reciprocal(out=rstd.t[:], in_=rstd.t[:]), r=[rstd.r], w=[rstd.r])

        def range_reduce(src, dst, ki, kf, tmp):
            S.op("dve", lambda e: e.tensor_scalar(out=ki.t[:], in0=src.t[:], scalar1=1.0 / TWO_PI, scalar2=None,
                                                  op0=ALU.mult), r=[src.r], w=[ki.r])
            S.op("dve", lambda e: e.tensor_copy(out=kf.t[:], in_=ki.t[:]), r=[ki.r], w=[kf.r])
            S.op("dve", lambda e: e.scalar_tensor_tensor(out=dst.t[:], in0=kf.t[:], scalar=-TWO_PI, in1=src.t[:],
                                                         op0=ALU.mult, op1=ALU.add), r=[kf.r, src.r], w=[dst.r])
            S.op("dve", lambda e: e.tensor_scalar(out=tmp.t[:], in0=dst.t[:], scalar1=PI, scalar2=-TWO_PI,
                                                  op0=ALU.is_gt, op1=ALU.mult), r=[dst.r], w=[tmp.r])
            S.op("dve", lambda e: e.tensor_tensor(out=dst.t[:], in0=dst.t[:], in1=tmp.t[:], op=ALU.add),
                 r=[dst.r, tmp.r], w=[dst.r])
            S.op("dve", lambda e: e.tensor_scalar(out=tmp.t[:], in0=dst.t[:], scalar1=-PI, scalar2=TWO_PI,
                                                  op0=ALU.is_lt, op1=ALU.mult), r=[dst.r], w=[tmp.r])
            S.op("dve", lambda e: e.tensor_tensor(out=dst.t[:], in0=dst.t[:], in1=tmp.t[:], op=ALU.add),
                 r=[dst.r, tmp.r], w=[dst.r])
            S.op("dve", lambda e: e.tensor_scalar(out=dst.t[:], in0=dst.t[:], scalar1=-3.1415925, scalar2=3.1415925,
                                                  op0=ALU.max, op1=ALU.min), r=[dst.r], w=[dst.r])

        def cos_sin(es, name, pos_ap, n):
            cs = sb(es, name + "_cos", [128, n, 32])
            sn = sb(es, name + "_sin", [128, n, 32])
            with ExitStack() as L:
                pi_ = sb(L, name + "_pi", [128, n], I32)
                pf_ = sb(L, name + "_pf", [128, n])
                ang = sb(L, name + "_ang", [128, n, 32])
                red = sb(L, name + "_red", [128, n, 32])
                ki = sb(L, name + "_ki", [128, n, 32], I32)
                kf = sb(L, name + "_kf", [128, n, 32])
                tmp = sb(L, name + "_tmp", [128, n, 32])
                S.dma("sp", lambda e: e.dma_start(out=pi_.t[:], in_=pos_ap), w=[pi_.r])
                S.op("dve", lambda e: e.tensor_copy(out=pf_.t[:], in_=pi_.t[:]), r=[pi_.r], w=[pf_.r])
                S.op("dve", lambda e: e.tensor_tensor(out=ang.t[:],
                                                      in0=pf_.t[:].unsqueeze(2).to_broadcast([128, n, 32]),
                                                      in1=invf.t[:].unsqueeze(1).to_broadcast([128, n, 32]),
                                                      op=ALU.mult), r=[pf_.r, invf.r], w=[ang.r])
                range_reduce(ang, red, ki, kf, tmp)
                S.op("act", lambda e: e.activation(out=sn.t[:], in_=red.t[:], func=AF.Sin), r=[red.r], w=[sn.r])
                S.op("dve", lambda e: e.tensor_scalar(out=ang.t[:], in0=ang.t[:], scalar1=PI / 2, scalar2=None,
                                                      op0=ALU.add), r=[ang.r], w=[ang.r])
                range_reduce(ang, red, ki, kf, tmp)
                S.op("act", lambda e: e.activation(out=cs.t[:], in_=red.t[:], func=AF.Sin), r=[red.r], w=[cs.r])
                S.flush()
            return cs, sn

        class NormCtx:
            def __init__(self, es, name, gpc):
                self.x = [sb(es, f"{name}_x{i}", [128, D]) for i in range(2)]
                self.junk = sb(es, name + "_junk", [128, D], BF16)
                self.ss = sb(es, name + "_ss", [128, 1])
                self.rstd = sb(es, name + "_rstd", [128, 1])
                self.xn = sb(es, name + "_xn", [128, D], BF16)
                self.gpc = gpc
                self.i = 0

            def run(self, x_ap, hT, hT_slice, extra_r=()):
                xt = self.x[self.i % 2]
                self.i += 1
                S.dma("sp", lambda e: e.dma_start(out=xt.t[:], in_=x_ap), r=list(extra_r), w=[xt.r])
                S.op("act", lambda e: e.activation(out=self.junk.t[:], in_=xt.t[:], func=AF.Square,
                                                   accum_out=self.ss.t[:]), r=[xt.r], w=[self.junk.r, self.ss.r])
                rstd_from_ss(self.ss, self.rstd, D)
                S.op("dve", lambda e: e.tensor_scalar(out=self.xn.t[:], in0=xt.t[:], scalar1=self.rstd.t[:, 0:1],
                                                      scalar2=None, op0=ALU.mult), r=[xt.r, self.rstd.r], w=[self.xn.r])
                for half in range(2):
                    pb = PB[half]
                    for cc in range(8):
                        c = half * 8 + cc
                        S.op("pe", lambda e, c=c, cc=cc, pb=pb: e.transpose(
                            out=pb.t[:, cc * 128:(cc + 1) * 128], in_=self.xn.t[:, c * 128:(c + 1) * 128],
                            identity=ident.t[:]), r=[self.xn.r, ident.r], w=[pb.r])
                    for cc in range(8):
                        c = half * 8 + cc
                        eng = "act" if cc % 2 == 0 else "dve"
                        if eng == "act":
                            S.op("act", lambda e, c=c, cc=cc, pb=pb: e.activation(
                                out=hT_slice(c), in_=pb.t[:, cc * 128:(cc + 1) * 128], func=AF.Copy,
                                scale=self.gpc.t[:, c:c + 1]), r=[pb.r, self.gpc.r], w=[hT.r])
                        else:
                            S.op("dve", lambda e, c=c, cc=cc, pb=pb: e.tensor_scalar(
                                out=hT_slice(c), in0=pb.t[:, cc * 128:(cc + 1) * 128],
                                scalar1=self.gpc.t[:, c:c + 1], scalar2=None, op0=ALU.mult),
                                r=[pb.r, self.gpc.r], w=[hT.r])
                return xt

        def proj(hT, hT_slice, w, col0, ncols, pf):
            for c in range(NKC):
                S.op("pe", lambda e, c=c: e.matmul(pf.t[:, 0:ncols], lhsT=hT_slice(c), rhs=w.t[:, c, col0:col0 + ncols],
                                                   start=(c == 0), stop=(c == NKC - 1)), r=[hT.r, w.r], w=[pf.r])

        def rope(src_ap_fn, dst_ap_fn, cs_ap, sn_ap, tmps, rsrc, rdst, rcs):
            a, b_ = tmps
            S.op("dve", lambda e: e.tensor_tensor(out=a.t[:], in0=src_ap_fn(0, 32), in1=cs_ap, op=ALU.mult),
                 r=[rsrc, rcs], w=[a.r])
            S.op("dve", lambda e: e.tensor_tensor(out=b_.t[:], in0=src_ap_fn(32, 64), in1=sn_ap, op=ALU.mult),
                 r=[rsrc, rcs], w=[b_.r])
            S.op("dve", lambda e: e.tensor_tensor(out=dst_ap_fn(0, 32), in0=a.t[:], in1=b_.t[:], op=ALU.subtract),
                 r=[a.r, b_.r], w=[rdst])
            S.op("dve", lambda e: e.tensor_tensor(out=a.t[:], in0=src_ap_fn(0, 32), in1=sn_ap, op=ALU.mult),
                 r=[rsrc, rcs], w=[a.r])
            S.op("dve", lambda e: e.tensor_tensor(out=b_.t[:], in0=src_ap_fn(32, 64), in1=cs_ap, op=ALU.mult),
                 r=[rsrc, rcs], w=[b_.r])
            S.op("dve", lambda e: e.tensor_tensor(out=dst_ap_fn(32, 64), in0=a.t[:], in1=b_.t[:], op=ALU.add),
                 r=[a.r, b_.r], w=[rdst])

        S.flush()

        with ExitStack() as P:
            cs_s, sn_s = cos_sin(P, "cs_seq", pos_seq, NS)
            wkv = load_w(P, "wkv", w_kv, NKC, 320)
            gckv = sb(P, "gckv", [128, 256])
            S.dma("sp", lambda e: e.dma_start(out=gckv.t[:], in_=g_ckv_bc), w=[gckv.r])
            nctx = NormCtx(P, "n1", gmix)
            hT = [sb(P, f"hT1_{i}", [128, NKC, 128], BF16) for i in range(2)]
            ss2 = sb(P, "ss2", [128, 1])
            rstd2 = sb(P, "rstd2", [128, 1])
            junk2 = sb(P, "junk2", [128, 256], BF16)
            ckvn = sb(P, "ckvn", [128, 256], BF16)
            krs = sb(P, "krs", [128, 64])
            kro = sb(P, "kro", [128, 64], BF16)
            ta = sb(P, "ropa", [128, 32])
            tb = sb(P, "ropb", [128, 32])
            for t in range(NS):
                h = hT[t % 2]
                nctx.run(x_seq[t * 128:(t + 1) * 128, :], h, lambda c, h=h: h.t[:, c, :])
                pf = PF[t % 2]
                proj(h, lambda c, h=h: h.t[:, c, :], wkv, 0, 320, pf)
                S.op("act", lambda e, pf=pf: e.activation(out=junk2.t[:], in_=pf.t[:, 0:256], func=AF.Square,
                                                          accum_out=ss2.t[:]), r=[pf.r], w=[junk2.r, ss2.r])
                rstd_from_ss(ss2, rstd2, 256)
                S.op("dve", lambda e, pf=pf: e.scalar_tensor_tensor(out=ckvn.t[:], in0=pf.t[:, 0:256],
                                                                    scalar=rstd2.t[:, 0:1], in1=gckv.t[:],
                                                                    op0=ALU.mult, op1=ALU.mult),
                     r=[pf.r, rstd2.r, gckv.r], w=[ckvn.r])
                S.op("act", lambda e, pf=pf: e.activation(out=krs.t[:], in_=pf.t[:, 256:320], func=AF.Copy),
                     r=[pf.r], w=[krs.r])
                rope(lambda lo, hi: krs.t[:, lo:hi], lambda lo, hi: kro.t[:, lo:hi],
                     cs_s.t[:, t, :], sn_s.t[:, t, :], (ta, tb), krs.r, kro.r, cs_s.r)
                pb = PB[t % 2]
                for cc in range(2):
                    S.op("pe", lambda e, cc=cc, pb=pb: e.transpose(out=pb.t[:, cc * 128:(cc + 1) * 128],
                                                                   in_=ckvn.t[:, cc * 128:(cc + 1) * 128],
                                                                   identity=ident.t[:]),
                         r=[ckvn.r, ident.r], w=[pb.r])
                S.op("pe", lambda e, pb=pb: e.transpose(out=pb.t[0:64, 256:384], in_=kro.t[:, 0:64],
                                                        identity=ident.t[:]), r=[kro.r, ident.r], w=[pb.r])
                S.op("act", lambda e, pb=pb, t=t: e.activation(
                    out=ckvT.t[:, :, t * 128:(t + 1) * 128],
                    in_=pb.t[:, 0:256].rearrange("p (c k) -> p c k", c=2), func=AF.Copy),
                    r=[pb.r], w=[r_ckvT[t]])
                S.op("dve", lambda e, pb=pb, t=t: e.tensor_copy(out=kropeT.t[0:64, t * 128:(t + 1) * 128],
                                                                in_=pb.t[0:64, 256:384]),
                     r=[pb.r, kropeT.r], w=[r_ckvT[t]])
            S.flush()

        with ExitStack() as P:
            cs_o, sn_o = cos_sin(P, "cs_own", pos_own, NJ)
            kv_prev = sb(P, "kv_prev", [128, NJ, 512], BF16)
            kv_own = sb(P, "kv_own", [128, NJ, 512], BF16)
            r_kvp = [Res(f"kvp{j}") for j in range(NJ)]
            r_kvo = [Res(f"kvo{j}") for j in range(NJ)]
            nctx = NormCtx(P, "n2", gmix)
            hT = [sb(P, f"hT2_{i}", [128, NKC, 128], BF16) for i in range(2)]
            with ExitStack() as P2:
                wkva = load_w(P2, "wkva", w_kva, NKC, 512)
                wcq = load_w(P2, "wcq", w_cq, NKC, 512)
                wuq = load_w(P2, "wuq", w_uq, 4, 1536)
                gcq = sb(P2, "gcq", [128, 512])
                S.dma("sp", lambda e: e.dma_start(out=gcq.t[:], in_=g_cq_bc), w=[gcq.r])
                for j in range(NJ):
                    h = hT[j % 2]
                    nctx.run(x_prev[j * 128:(j + 1) * 128, :], h, lambda c, h=h: h.t[:, c, :])
                    pf = PF[j % 2]
                    proj(h, lambda c, h=h: h.t[:, c, :], wkva, 0, 512, pf)
                    S.op("act", lambda e, pf=pf, j=j: e.activation(out=kv_prev.t[:, j, :], in_=pf.t[:, 0:512],
                                                                  func=AF.Copy), r=[pf.r], w=[r_kvp[j]])
                ss3 = sb(P2, "ss3", [128, 1])
                rstd3 = sb(P2, "rstd3", [128, 1])
                junk3 = sb(P2, "junk3", [128, 512], BF16)
                cqn = sb(P2, "cqn", [128, 512], BF16)
                cqT = sb(P2, "cqT", [128, 4, 128], BF16)
                qf = sb(P2, "qf", [128, 1536])
                qb = sb(P2, "qb", [128, 1536], BF16)
                ta = sb(P2, "ropa2", [128, 32])
                tb = sb(P2, "ropb2", [128, 32])
                qnT = [sb(P2, f"qnT{i}", [128, 8, 128], BF16) for i in range(2)]
                qrT = [sb(P2, f"qrT{i}", [64, 8, 128], BF16) for i in range(2)]
                for j in range(NJ):
                    h = hT[j % 2]
                    nctx.run(x_own[j * 128:(j + 1) * 128, :], h, lambda c, h=h: h.t[:, c, :])
                    hs = (lambda c, h=h: h.t[:, c, :])
                    proj(h, hs, wkva, 0, 512, PF[0])
                    S.op("act", lambda e, j=j: e.activation(out=kv_own.t[:, j, :], in_=PF[0].t[:, 0:512],
                                                           func=AF.Copy), r=[PF[0].r], w=[r_kvo[j]])
                    proj(h, hs, wcq, 0, 512, PF[1])
                    S.op("act", lambda e: e.activation(out=junk3.t[:], in_=PF[1].t[:, 0:512], func=AF.Square,
                                                       accum_out=ss3.t[:]), r=[PF[1].r], w=[junk3.r, ss3.r])
                    rstd_from_ss(ss3, rstd3, 512)
                    S.op("dve", lambda e: e.scalar_tensor_tensor(out=cqn.t[:], in0=PF[1].t[:, 0:512],
                                                                 scalar=rstd3.t[:, 0:1], in1=gcq.t[:],
                                                                 op0=ALU.mult, op1=ALU.mult),
                         r=[PF[1].r, rstd3.r, gcq.r], w=[cqn.r])
                    for cc in range(4):
                        S.op("pe", lambda e, cc=cc: e.transpose(out=PB[0].t[:, cc * 128:(cc + 1) * 128],
                                                                in_=cqn.t[:, cc * 128:(cc + 1) * 128],
                                                                identity=ident.t[:]),
                             r=[cqn.r, ident.r], w=[PB[0].r])
                    S.op("dve", lambda e: e.tensor_copy(out=cqT.t[:],
                                                        in_=PB[0].t[:, 0:512].rearrange("p (c k) -> p c k", c=4)),
                         r=[PB[0].r], w=[cqT.r])
                    for g3 in range(3):
                        pf = PF[2 + g3]
                        for cc in range(4):
                            S.op("pe", lambda e, cc=cc, g3=g3, pf=pf: e.matmul(
                                pf.t[:, 0:512], lhsT=cqT.t[:, cc, :], rhs=wuq.t[:, cc, g3 * 512:(g3 + 1) * 512],
                                start=(cc == 0), stop=(cc == 3)), r=[cqT.r, wuq.r], w=[pf.r])
                        S.op("act", lambda e, g3=g3, pf=pf: e.activation(out=qf.t[:, g3 * 512:(g3 + 1) * 512],
                                                                        in_=pf.t[:, 0:512], func=AF.Copy),
                             r=[pf.r], w=[qf.r])
                    S.op("act", lambda e: e.activation(out=qb.t[:], in_=qf.t[:], func=AF.Copy), r=[qf.r], w=[qb.r])
                    for hh in range(8):
                        base = hh * 192 + 128
                        rope(lambda lo, hi, base=base: qf.t[:, base + lo:base + hi],
                             lambda lo, hi, base=base: qb.t[:, base + lo:base + hi],
                             cs_o.t[:, j, :], sn_o.t[:, j, :], (ta, tb), qf.r, qb.r, cs_o.r)
                    qn = qnT[j % 2]
                    qr = qrT[j % 2]
                    for hh in range(8):
                        S.op("pe", lambda e, hh=hh: e.transpose(out=PB[1].t[:, hh * 128:(hh + 1) * 128],
                                                                in_=qb.t[:, hh * 192:hh * 192 + 128],
                                                                identity=ident.t[:]),
                             r=[qb.r, ident.r], w=[PB[1].r])
                    S.op("dve", lambda e, qn=qn: e.tensor_copy(
                        out=qn.t[:], in_=PB[1].t[:, 0:1024].rearrange("p (c k) -> p c k", c=8)),
                        r=[PB[1].r], w=[qn.r])
                    for hh in range(8):
                        S.op("pe", lambda e, hh=hh: e.transpose(out=PB[0].t[0:64, hh * 128:(hh + 1) * 128],
                                                                in_=qb.t[:, hh * 192 + 128:hh * 192 + 192],
                                                                identity=ident.t[:]),
                             r=[qb.r, ident.r], w=[PB[0].r])
                    S.op("act", lambda e, qr=qr: e.activation(
                        out=qr.t[:], in_=PB[0].t[0:64, 0:1024].rearrange("p (c k) -> p c k", c=8), func=AF.Copy),
                        r=[PB[0].r], w=[qr.r])
                    S.dma("sp", lambda e, qn=qn, j=j: e.dma_start(
                        out=qnT_d[:, :, j * 128:(j + 1) * 128].rearrange("h p k -> p h k"), in_=qn.t[:]),
                        r=[qn.r], w=[r_qT])
                    S.dma("sp", lambda e, qr=qr, j=j: e.dma_start(
                        out=qrT_d[:, :, j * 128:(j + 1) * 128].rearrange("h p k -> p h k"), in_=qr.t[:]),
                        r=[qr.r], w=[r_qT])
                S.flush()

            with ExitStack() as P3:
                wqa = load_w(P3, "wqa", w_qa, NKC, 1024)
                sinks = sb(P3, "sinks", [128, 16])
                S.dma("sp", lambda e: e.dma_start(out=sinks.t[:], in_=sinks_bc), w=[sinks.r])
                maskb = sb(P3, "maskb", [128, 2, 256])
                S.dma("sp", lambda e: e.dma_start(out=maskb.t[:], in_=swa_maskb.rearrange("a p k -> p a k")),
                      w=[maskb.r])
                posq_i = sb(P3, "posq_i", [128, NJ], I32)
                posq = sb(P3, "posq", [128, NJ])
                S.dma("sp", lambda e: e.dma_start(out=posq_i.t[:], in_=pos_own), w=[posq_i.r])
                S.op("dve", lambda e: e.tensor_copy(out=posq.t[:], in_=posq_i.t[:]), r=[posq_i.r], w=[posq.r])
                posk_i = sb(P3, "posk_i", [128, 256], I32)
                dist = sb(P3, "dist", [128, 256])
                bias = sb(P3, "bias", [128, 16, 256])
                qa = sb(P3, "qa", [128, 1024], BF16)
                qaT = sb(P3, "qaT", [64, 16, 128], BF16)
                kaT = sb(P3, "kaT", [64, 4, 256], BF16)
                L = sb(P3, "swaL", [128, 4, 256])
                mx = sb(P3, "swamx", [128, 4])
                dd = sb(P3, "swadd", [128, 4])
                rsum = sb(P3, "swarsum", [128, 4])
                den = sb(P3, "swaden", [128, 4])
                pexp = sb(P3, "swap", [128, 4, 256], BF16)
                pT = sb(P3, "swapT", [128, 8, 128], BF16)
                so = [sb(P3, f"swao{i}", [128, 1024], BF16) for i in range(2)]
                for j in range(NJ):
                    h = hT[j % 2]
                    hs = (lambda c, h=h: h.t[:, c, :])
                    nctx.run(x_own[j * 128:(j + 1) * 128, :], h, hs)
                    for g2 in range(2):
                        proj(h, hs, wqa, g2 * 512, 512, PF[g2])
                        S.op("act", lambda e, g2=g2: e.activation(out=qa.t[:, g2 * 512:(g2 + 1) * 512],
                                                                 in_=PF[g2].t[:, 0:512], func=AF.Copy),
                             r=[PF[g2].r], w=[qa.r])
                    for half in range(2):
                        for hh in range(8):
                            hd = half * 8 + hh
                            S.op("pe", lambda e, hd=hd, hh=hh, half=half: e.transpose(
                                out=PB[half].t[0:64, hh * 128:(hh + 1) * 128], in_=qa.t[:, hd * 64:(hd + 1) * 64],
                                identity=ident.t[:]), r=[qa.r, ident.r], w=[PB[half].r])
                        S.op("dve", lambda e, half=half: e.tensor_copy(
                            out=qaT.t[:, half * 8:(half + 1) * 8, :],
                            in_=PB[half].t[0:64, 0:1024].rearrange("p (c k) -> p c k", c=8)),
                            r=[PB[half].r], w=[qaT.r])
                    for kvh in range(4):
                        S.op("pe", lambda e, kvh=kvh, j=j: e.transpose(
                            out=PB[0].t[0:64, kvh * 256:kvh * 256 + 128], in_=kv_prev.t[:, j, kvh * 64:(kvh + 1) * 64],
                            identity=ident.t[:]), r=[r_kvp[j], ident.r], w=[PB[0].r])
                        S.op("pe", lambda e, kvh=kvh, j=j: e.transpose(
                            out=PB[0].t[0:64, kvh * 256 + 128:kvh * 256 + 256],
                            in_=kv_own.t[:, j, kvh * 64:(kvh + 1) * 64],
                            identity=ident.t[:]), r=[r_kvo[j], ident.r], w=[PB[0].r])
                    S.op("dve", lambda e: e.tensor_copy(
                        out=kaT.t[:], in_=PB[0].t[0:64, 0:1024].rearrange("p (c k) -> p c k", c=4)),
                        r=[PB[0].r], w=[kaT.r])
                    S.dma("sp", lambda e, j=j: e.dma_start(out=posk_i.t[:], in_=pos_kv_bc[j]), w=[posk_i.r])
                    S.op("dve", lambda e: e.tensor_copy(out=dist.t[:], in_=posk_i.t[:]), r=[posk_i.r], w=[dist.r])
                    S.op("dve", lambda e, j=j: e.tensor_scalar(out=dist.t[:], in0=dist.t[:], scalar1=posq.t[:, j:j + 1],
                                                               scalar2=None, op0=ALU.subtract),
                         r=[dist.r, posq.r], w=[dist.r])
                    S.op("dve", lambda e: e.scalar_tensor_tensor(out=dist.t[:], in0=dist.t[:], scalar=-1.0,
                                                                 in1=dist.t[:], op0=ALU.mult, op1=ALU.max),
                         r=[dist.r], w=[dist.r])
                    mi = 0 if j == 0 else 1
                    for hd in range(16):
                        S.op("pool" if hd % 2 else "dve", lambda e, hd=hd: e.tensor_scalar(
                            out=bias.t[:, hd, :], in0=dist.t[:], scalar1=-slopes[hd], scalar2=None, op0=ALU.mult),
                            r=[dist.r], w=[bias.r])
                    S.op("dve", lambda e, mi=mi: e.tensor_tensor(
                        out=bias.t[:], in0=bias.t[:], in1=maskb.t[:, mi, :].unsqueeze(1).to_broadcast([128, 16, 256]),
                        op=ALU.add), r=[bias.r, maskb.r], w=[bias.r])
                    sot = so[j % 2]
                    for kvh in range(4):
                        for gq in range(4):
                            hd = kvh * 4 + gq
                            pfl = PF[gq // 2]
                            S.op("pe", lambda e, hd=hd, gq=gq, kvh=kvh, pfl=pfl: e.matmul(
                                pfl.t[:, (gq % 2) * 256:(gq % 2) * 256 + 256], lhsT=qaT.t[:, hd, :],
                                rhs=kaT.t[:, kvh, :], start=True, stop=True), r=[qaT.r, kaT.r], w=[pfl.r])
                        for half in range(2):
                            S.op("dve", lambda e, half=half, kvh=kvh: e.scalar_tensor_tensor(
                                out=L.t[:, half * 2:half * 2 + 2, :],
                                in0=PF[half].t[:, 0:512].rearrange("p (a k) -> p a k", a=2), scalar=0.125,
                                in1=bias.t[:, kvh * 4 + half * 2:kvh * 4 + half * 2 + 2, :],
                                op0=ALU.mult, op1=ALU.add), r=[PF[half].r, bias.r], w=[L.r])
                        S.op("dve", lambda e: e.tensor_reduce(out=mx.t[:], in_=L.t[:], axis=AX.X, op=ALU.max),
                             r=[L.r], w=[mx.r])
                        S.op("dve", lambda e, kvh=kvh: e.tensor_tensor(out=mx.t[:], in0=mx.t[:],
                                                                       in1=sinks.t[:, kvh * 4:kvh * 4 + 4], op=ALU.max),
                             r=[mx.r, sinks.r], w=[mx.r])
                        S.op("dve", lambda e, kvh=kvh: e.tensor_tensor(out=dd.t[:], in0=sinks.t[:, kvh * 4:kvh * 4 + 4],
                                                                       in1=mx.t[:], op=ALU.subtract),
                             r=[mx.r, sinks.r], w=[dd.r])
                        S.op("dve", lambda e: e.tensor_scalar(out=mx.t[:], in0=mx.t[:], scalar1=-1.0, scalar2=None,
                                                              op0=ALU.mult), r=[mx.r], w=[mx.r])
                        for gq in range(4):
                            S.op("act", lambda e, gq=gq: e.activation(out=pexp.t[:, gq, :], in_=L.t[:, gq, :],
                                                                      func=AF.Exp, bias=mx.t[:, gq:gq + 1], scale=1.0,
                                                                      accum_out=rsum.t[:, gq:gq + 1]),
                                 r=[L.r, mx.r], w=[pexp.r, rsum.r])
                        S.op("act", lambda e: e.activation(out=dd.t[:], in_=dd.t[:], func=AF.Exp), r=[dd.r], w=[dd.r])
                        S.op("dve", lambda e: e.tensor_tensor(out=den.t[:], in0=rsum.t[:], in1=dd.t[:], op=ALU.add),
                             r=[rsum.r, dd.r], w=[den.r])
                        S.op("dve", lambda e: e.reciprocal(out=den.t[:], in_=den.t[:]), r=[den.r], w=[den.r])
                        for gq in range(4):
                            for kb in range(2):
                                S.op("pe", lambda e, gq=gq, kb=kb: e.transpose(
                                    out=PB[1].t[:, (gq * 2 + kb) * 128:(gq * 2 + kb + 1) * 128],
                                    in_=pexp.t[:, gq, kb * 128:(kb + 1) * 128], identity=ident.t[:]),
                                    r=[pexp.r, ident.r], w=[PB[1].r])
                        S.op("act", lambda e: e.activation(
                            out=pT.t[:], in_=PB[1].t[:, 0:1024].rearrange("p (c k) -> p c k", c=8), func=AF.Copy),
                            r=[PB[1].r], w=[pT.r])
                        po = PF[2 + kvh % 2]
                        for gq in range(4):
                            S.op("pe", lambda e, gq=gq, kvh=kvh, j=j, po=po: e.matmul(
                                po.t[:, gq * 64:(gq + 1) * 64], lhsT=pT.t[:, gq * 2, :],
                                rhs=kv_prev.t[:, j, 256 + kvh * 64:256 + (kvh + 1) * 64], start=True, stop=False),
                                r=[pT.r, r_kvp[j]], w=[po.r])
                            S.op("pe", lambda e, gq=gq, kvh=kvh, j=j, po=po: e.matmul(
                                po.t[:, gq * 64:(gq + 1) * 64], lhsT=pT.t[:, gq * 2 + 1, :],
                                rhs=kv_own.t[:, j, 256 + kvh * 64:256 + (kvh + 1) * 64], start=False, stop=True),
                                r=[pT.r, r_kvo[j]], w=[po.r])
                        for gq in range(4):
                            hd = kvh * 4 + gq
                            S.op("dve", lambda e, gq=gq, hd=hd, po=po, sot=sot: e.tensor_scalar(
                                out=sot.t[:, hd * 64:(hd + 1) * 64], in0=po.t[:, gq * 64:(gq + 1) * 64],
                                scalar1=den.t[:, gq:gq + 1], scalar2=None, op0=ALU.mult),
                                r=[po.r, den.r], w=[sot.r])
                    S.dma("sp", lambda e, j=j, sot=sot: e.dma_start(out=swa_d[j * 128:(j + 1) * 128, :], in_=sot.t[:]),
                          r=[sot.r], w=[r_swa[j]])
                S.flush()

        with ExitStack() as P:
            nctx = NormCtx(P, "n3", gmix)
            hTo = sb(P, "hTo", [128, NKC, TK], BF16)
            r_hTo = [Res(f"hTo{j}") for j in range(NJ)]
            for j in range(NJ):
                hj = Tl(hTo.t, "x")
                hj.r = r_hTo[j]
                nctx.run(x_own[j * 128:(j + 1) * 128, :], hj, lambda c, j=j: hTo.t[:, c, j * 128:(j + 1) * 128])
            wg = [sb(P, f"wg{i}", [128, NKC, 512], BF16) for i in range(2)]
            stg = [sb(P, f"stg{i}", [128, 512]) for i in range(3)]
            si = 0
            for g8 in range(8):
                w = wg[g8 % 2]
                srcw = w_gate[:, g8 * 512:(g8 + 1) * 512].rearrange("(c p) n -> p c n", p=128)
                for c0 in range(0, NKC, 8):
                    S.dma("pool", lambda e, w=w, srcw=srcw, c0=c0: e.dma_start(out=w.t[:, c0:c0 + 8, :],
                                                                              in_=srcw[:, c0:c0 + 8, :]), w=[w.r])
                for j in range(NJ):
                    pf = PF[(g8 * NJ + j) % 4]
                    hj = Tl(hTo.t, "x")
                    hj.r = r_hTo[j]
                    proj(hj, lambda c, j=j: hTo.t[:, c, j * 128:(j + 1) * 128], w, 0, 512, pf)
                    st = stg[si % 3]
                    si += 1
                    if si % 2:
                        S.op("act", lambda e, st=st, pf=pf: e.activation(out=st.t[:], in_=pf.t[:, 0:512], func=AF.Copy),
                             r=[pf.r], w=[st.r])
                    else:
                        S.op("dve", lambda e, st=st, pf=pf: e.tensor_copy(out=st.t[:], in_=pf.t[:, 0:512]),
                             r=[pf.r], w=[st.r])
                    S.dma("sp", lambda e, st=st, j=j, g8=g8: e.dma_start(
                        out=zg_d[j * 128:(j + 1) * 128, g8 * 512:(g8 + 1) * 512], in_=st.t[:]),
                        r=[st.r], w=[r_zg[j]])
            S.flush()

        with ExitStack() as P:
            wk = load_w(P, "wukvk", w_ukv_k, 2, 1024)
            wv = load_w(P, "wukvv", w_ukv_v, 2, 1024)
            mmask = sb(P, "mmask", [128, 512])
            S.dma("sp", lambda e: e.dma_start(out=mmask.t[:], in_=mla_mask), w=[mmask.r])
            for tab, c0 in ((u_tab, 0), (v_tab, D)):
                for i in range(16):
                    S.dma("pool", lambda e, tab=tab, c0=c0, i=i: e.dma_start(
                        out=uv_b[i * 1024:(i + 1) * 1024, c0:c0 + D], in_=tab[i * 1024:(i + 1) * 1024, :]),
                        w=[r_uvb])
            knT = sb(P, "knT", [128, 2, SQ], BF16)
            Vsb = sb(P, "Vsb", [128, NS, 256], BF16)
            qn2 = sb(P, "qn2", [128, 2, TK], BF16)
            qr2 = sb(P, "qr2", [128, 2, TK], BF16)
            qr2z = Res("qr2z")
            S.op("dve", lambda e: e.memset(qr2.t[64:128, :, :], 0.0), w=[qr2z])
            SETS = []
            for bs in range(2):
                SETS.append(dict(
                    S1=PF[0 + 3 * bs], S2=PF[1 + 3 * bs], O=PF[2 + 3 * bs], PT=PB[bs],
                    rmax=sb(P, f"rmax{bs}", [128, NJ]), rsm=sb(P, f"rsm{bs}", [128, NJ]),
                    mrow=sb(P, f"mrow{bs}", [128, 1]), lrow=sb(P, f"lrow{bs}", [128, 1]),
                    Pb=[sb(P, f"Pb{bs}_{i}", [128, 512], BF16) for i in range(2)],
                    Pf=sb(P, f"Pf{bs}", [128, 512]),
                    PTs=[sb(P, f"PTs{bs}_{i}", [128, 4, 128], BF16) for i in range(2)]))
            mo = [sb(P, f"mo{i}", [128, 256], BF16) for i in range(2)]
            allk = list(r_ckvT)

            def interleave(gens):
                gens = list(gens)
                while gens:
                    for g in list(gens):
                        try:
                            next(g)
                        except StopIteration:
                            gens.remove(g)

            def mla_unit(j, hh, B, mot):
                qs = slice(j * 128, (j + 1) * 128)
                rmax, rsm, mrow, lrow = B["rmax"], B["rsm"], B["mrow"], B["lrow"]

                def scores(c, pf):
                    S.op("pe", lambda e: e.matmul(pf.t[:, 0:512], lhsT=qn2.t[:, hh, qs],
                                                  rhs=knT.t[:, hh, c * 512:(c + 1) * 512], start=True,
                                                  stop=False), r=[qn2.r, knT.r], w=[pf.r])
                    S.op("pe", lambda e: e.matmul(pf.t[:, 0:512], lhsT=qr2.t[:, hh, qs],
                                                  rhs=kropeT.t[:, c * 512:(c + 1) * 512], start=False,
                                                  stop=True), r=[qr2.r, qr2z, kropeT.r] + allk[c * 4:c * 4 + 4], w=[pf.r])
                for c in range(j + 1):
                    pf = B["S1"]
                    scores(c, pf)
                    yield
                    S.op("dve", lambda e, c=c, pf=pf: e.tensor_reduce(out=rmax.t[:, c:c + 1], in_=pf.t[:, 0:512],
                                                                     axis=AX.X, op=ALU.max), r=[pf.r], w=[rmax.r])
                S.op("dve", lambda e: e.tensor_reduce(out=mrow.t[:], in_=rmax.t[:, 0:j + 1], axis=AX.X,
                                                      op=ALU.max), r=[rmax.r], w=[mrow.r])
                S.op("dve", lambda e: e.tensor_scalar(out=mrow.t[:], in0=mrow.t[:], scalar1=-MLA_SCALE,
                                                      scalar2=None, op0=ALU.mult), r=[mrow.r], w=[mrow.r])
                po = B["O"]
                for c in range(j + 1):
                    pf = B["S2"]
                    scores(c, pf)
                    yield
                    pbt = B["Pb"][c % 2]
                    if c < j:
                        S.op("act", lambda e, c=c, pf=pf, pbt=pbt: e.activation(
                            out=pbt.t[:], in_=pf.t[:, 0:512], func=AF.Exp, bias=mrow.t[:, 0:1],
                            scale=MLA_SCALE, accum_out=rsm.t[:, c:c + 1]),
                            r=[pf.r, mrow.r], w=[pbt.r, rsm.r])
                    else:
                        Pf = B["Pf"]
                        S.op("act", lambda e, pf=pf, Pf=Pf: e.activation(
                            out=Pf.t[:], in_=pf.t[:, 0:512], func=AF.Exp, bias=mrow.t[:, 0:1],
                            scale=MLA_SCALE), r=[pf.r, mrow.r], w=[Pf.r])
                        S.op("dve", lambda e, Pf=Pf: e.tensor_tensor(out=Pf.t[:], in0=Pf.t[:], in1=mmask.t[:],
                                                                     op=ALU.mult), r=[Pf.r, mmask.r], w=[Pf.r])
                        S.op("dve", lambda e, c=c, Pf=Pf: e.tensor_reduce(out=rsm.t[:, c:c + 1], in_=Pf.t[:],
                                                                          axis=AX.X, op=ALU.add),
                             r=[Pf.r], w=[rsm.r])
                        S.op("act", lambda e, pbt=pbt, Pf=Pf: e.activation(out=pbt.t[:], in_=Pf.t[:], func=AF.Copy),
                             r=[Pf.r], w=[pbt.r])
                    ptp = B["PT"]
                    for kk in range(4):
                        S.op("pe", lambda e, kk=kk, pbt=pbt, ptp=ptp: e.transpose(
                            out=ptp.t[:, kk * 128:(kk + 1) * 128], in_=pbt.t[:, kk * 128:(kk + 1) * 128],
                            identity=ident.t[:]), r=[pbt.r, ident.r], w=[ptp.r])
                    yield
                    pts = B["PTs"][c % 2]
                    if c % 2:
                        S.op("act", lambda e, pts=pts, ptp=ptp: e.activation(
                            out=pts.t[:], in_=ptp.t[:, 0:512].rearrange("p (c k) -> p c k", c=4), func=AF.Copy),
                            r=[ptp.r], w=[pts.r])
                    else:
                        S.op("dve", lambda e, pts=pts, ptp=ptp: e.tensor_copy(
                            out=pts.t[:], in_=ptp.t[:, 0:512].rearrange("p (c k) -> p c k", c=4)),
                            r=[ptp.r], w=[pts.r])
                    for kk in range(4):
                        S.op("pe", lambda e, kk=kk, c=c, pts=pts: e.matmul(
                            po.t[:, 0:128], lhsT=pts.t[:, kk, :],
                            rhs=Vsb.t[:, c * 4 + kk, hh * 128:(hh + 1) * 128],
                            start=(c == 0 and kk == 0), stop=(c == j and kk == 3)),
                            r=[pts.r, Vsb.r], w=[po.r])
                    yield
                S.op("dve", lambda e: e.tensor_reduce(out=lrow.t[:], in_=rsm.t[:, 0:j + 1], axis=AX.X,
                                                      op=ALU.add), r=[rsm.r], w=[lrow.r])
                S.op("dve", lambda e: e.reciprocal(out=lrow.t[:], in_=lrow.t[:]), r=[lrow.r], w=[lrow.r])
                S.op("dve", lambda e: e.tensor_scalar(
                    out=mot.t[:, hh * 128:(hh + 1) * 128], in0=po.t[:, 0:128], scalar1=lrow.t[:, 0:1],
                    scalar2=None, op0=ALU.mult), r=[po.r, lrow.r], w=[mot.r])

            for hp in range(4):
                for hh in range(2):
                    hd = hp * 2 + hh
                    for kc in range(NS // 4):
                        pf = PF[(kc % 2) * 3 + 1]
                        for cc in range(2):
                            S.op("pe", lambda e, cc=cc, hd=hd, kc=kc, pf=pf: e.matmul(
                                pf.t[:, 0:512], lhsT=wk.t[:, cc, hd * 128:(hd + 1) * 128],
                                rhs=ckvT.t[:, cc, kc * 512:(kc + 1) * 512], start=(cc == 0), stop=(cc == 1)),
                                r=[wk.r] + allk[kc * 4:kc * 4 + 4], w=[pf.r])
                        if kc % 2:
                            S.op("act", lambda e, hh=hh, kc=kc, pf=pf: e.activation(
                                out=knT.t[:, hh, kc * 512:(kc + 1) * 512], in_=pf.t[:, 0:512], func=AF.Copy),
                                r=[pf.r], w=[knT.r])
                        else:
                            S.op("dve", lambda e, hh=hh, kc=kc, pf=pf: e.tensor_copy(
                                out=knT.t[:, hh, kc * 512:(kc + 1) * 512], in_=pf.t[:, 0:512]),
                                r=[pf.r], w=[knT.r])
                for kt in range(NS):
                    pf = PF[(kt % 2) * 3 + 2]
                    for cc in range(2):
                        S.op("pe", lambda e, cc=cc, kt=kt, pf=pf, hp=hp: e.matmul(
                            pf.t[:, 0:256], lhsT=ckvT.t[:, cc, kt * 128:(kt + 1) * 128],
                            rhs=wv.t[:, cc, hp * 256:(hp + 1) * 256], start=(cc == 0), stop=(cc == 1)),
                            r=[wv.r, allk[kt]], w=[pf.r])
                    if kt % 2:
                        S.op("act", lambda e, kt=kt, pf=pf: e.activation(
                            out=Vsb.t[:, kt, :], in_=pf.t[:, 0:256], func=AF.Copy), r=[pf.r], w=[Vsb.r])
                    else:
                        S.op("dve", lambda e, kt=kt, pf=pf: e.tensor_copy(out=Vsb.t[:, kt, :], in_=pf.t[:, 0:256]),
                             r=[pf.r], w=[Vsb.r])
                S.dma("sp", lambda e, hp=hp: e.dma_start(out=qn2.t[:], in_=qnT_d[hp * 2:hp * 2 + 2].rearrange(
                    "h p k -> p h k")), r=[r_qT], w=[qn2.r])
                S.dma("sp", lambda e, hp=hp: e.dma_start(out=qr2.t[0:64, :, :], in_=qrT_d[hp * 2:hp * 2 + 2].rearrange(
                    "h p k -> p h k")), r=[r_qT, qr2z], w=[qr2.r])
                for j in range(NJ):
                    mot = mo[j % 2]
                    interleave([mla_unit(j, 0, SETS[0], mot), mla_unit(j, 1, SETS[1], mot)])
                    S.dma("sp", lambda e, j=j, hp=hp, mot=mot: e.dma_start(
                        out=mla_d[j * 128:(j + 1) * 128, hp * 256:(hp + 1) * 256], in_=mot.t[:]),
                        r=[mot.r], w=[r_mla[j]])
                S.flush()

        GK.close()
        with ExitStack() as P:
            r_mixd = [Res(f"mixd{j}") for j in range(NJ)]
            with ExitStack() as P5:
                wa = load_w(P5, "wa", w_a, 8, D)
                wb = load_w(P5, "wb", w_b, 8, D)
                at = [sb(P5, f"at{i}", [128, 1024], BF16) for i in range(2)]
                bt = [sb(P5, f"bt{i}", [128, 1024], BF16) for i in range(2)]
                aT = sb(P5, "aT", [128, 8, 128], BF16)
                bT = sb(P5, "bT", [128, 8, 128], BF16)
                gt = [sb(P5, f"gt{i}", [128, 4096]) for i in range(2)]
                t1 = sb(P5, "mix_t1", [128, 512])
                mixeds = [sb(P5, f"mixed{i}", [128, D], BF16) for i in range(2)]
                for j in range(NJ):
                    mixed = mixeds[j % 2]
                    a_, b_, g_ = at[j % 2], bt[j % 2], gt[j % 2]
                    S.dma("sp", lambda e, j=j, a_=a_: e.dma_start(out=a_.t[:], in_=swa_d[j * 128:(j + 1) * 128, :]),
                          r=[r_swa[j]], w=[a_.r])
                    S.dma("sp", lambda e, j=j, b_=b_: e.dma_start(out=b_.t[:], in_=mla_d[j * 128:(j + 1) * 128, :]),
                          r=[r_mla[j]], w=[b_.r])
                    S.dma("sp", lambda e, j=j, g_=g_: e.dma_start(out=g_.t[:], in_=zg_d[j * 128:(j + 1) * 128, :]),
                          r=[r_zg[j]], w=[g_.r])
                    S.op("act", lambda e, g_=g_: e.activation(out=g_.t[:], in_=g_.t[:], func=AF.Sigmoid),
                         r=[g_.r], w=[g_.r])
                    for src, dstT, pb in ((a_, aT, PB[0]), (b_, bT, PB[1])):
                        for cc in range(8):
                            S.op("pe", lambda e, cc=cc, src=src, pb=pb: e.transpose(
                                out=pb.t[:, cc * 128:(cc + 1) * 128], in_=src.t[:, cc * 128:(cc + 1) * 128],
                                identity=ident.t[:]), r=[src.r, ident.r], w=[pb.r])
                        S.op("dve", lambda e, dstT=dstT, pb=pb: e.tensor_copy(
                            out=dstT.t[:], in_=pb.t[:, 0:1024].rearrange("p (c k) -> p c k", c=8)),
                            r=[pb.r], w=[dstT.r])
                    for g4 in range(4):
                        pa, pbb = PF[(g4 % 2) * 2], PF[(g4 % 2) * 2 + 1]
                        for cc in range(8):
                            S.op("pe", lambda e, cc=cc, g4=g4, pa=pa: e.matmul(
                                pa.t[:, 0:512], lhsT=aT.t[:, cc, :], rhs=wa.t[:, cc, g4 * 512:(g4 + 1) * 512],
                                start=(cc == 0), stop=(cc == 7)), r=[aT.r, wa.r], w=[pa.r])
                        for cc in range(8):
                            S.op("pe", lambda e, cc=cc, g4=g4, pbb=pbb: e.matmul(
                                pbb.t[:, 0:512], lhsT=bT.t[:, cc, :], rhs=wb.t[:, cc, g4 * 512:(g4 + 1) * 512],
                                start=(cc == 0), stop=(cc == 7)), r=[bT.r, wb.r], w=[pbb.r])
                        S.op("dve", lambda e, g4=g4, pa=pa, g_=g_: e.tensor_tensor(
                            out=t1.t[:], in0=pa.t[:, 0:512], in1=g_.t[:, g4 * 512:(g4 + 1) * 512], op=ALU.mult),
                            r=[pa.r, g_.r], w=[t1.r])
                        S.op("dve", lambda e, g4=g4, pbb=pbb, g_=g_: e.tensor_tensor(
                            out=g_.t[:, 2048 + g4 * 512:2048 + (g4 + 1) * 512], in0=pbb.t[:, 0:512],
                            in1=g_.t[:, 2048 + g4 * 512:2048 + (g4 + 1) * 512], op=ALU.mult),
                            r=[pbb.r, g_.r], w=[g_.r])
                        S.op("dve", lambda e, g4=g4, g_=g_, mixed=mixed: e.tensor_tensor(
                            out=mixed.t[:, g4 * 512:(g4 + 1) * 512], in0=t1.t[:],
                            in1=g_.t[:, 2048 + g4 * 512:2048 + (g4 + 1) * 512], op=ALU.add),
                            r=[t1.r, g_.r], w=[mixed.r])
                    S.dma("sp", lambda e, j=j, mixed=mixed: e.dma_start(out=mixed_d[j * 128:(j + 1) * 128, :],
                                                                        in_=mixed.t[:]), r=[mixed.r], w=[r_mixd[j]])
                S.flush()
            with ExitStack() as P5:
                wo = load_w(P5, "wo", w_o, NKC, D)
                xt = [sb(P5, f"x5_{i}", [128, D]) for i in range(2)]
                mxs = [sb(P5, f"mx5_{i}", [128, D], BF16) for i in range(2)]
                mixTs = [sb(P5, f"mixT5_{i}", [128, NKC, 128], BF16) for i in range(2)]
                for j in range(NJ):
                    x_ = xt[j % 2]
                    mx_ = mxs[j % 2]
                    mixT = mixTs[j % 2]
                    S.dma("sp", lambda e, j=j, x_=x_: e.dma_start(out=x_.t[:], in_=x_own[j * 128:(j + 1) * 128, :]),
                          w=[x_.r])
                    S.dma("sp", lambda e, j=j, mx_=mx_: e.dma_start(out=mx_.t[:], in_=mixed_d[j * 128:(j + 1) * 128, :]),
                          r=[r_mixd[j]], w=[mx_.r])
                    for half in range(2):
                        pb = PB[half]
                        for cc in range(8):
                            c = half * 8 + cc
                            S.op("pe", lambda e, c=c, cc=cc, pb=pb, mx_=mx_: e.transpose(
                                out=pb.t[:, cc * 128:(cc + 1) * 128], in_=mx_.t[:, c * 128:(c + 1) * 128],
                                identity=ident.t[:]), r=[mx_.r, ident.r], w=[pb.r])
                        S.op("act", lambda e, half=half, pb=pb, mixT=mixT: e.activation(
                            out=mixT.t[:, half * 8:(half + 1) * 8, :],
                            in_=pb.t[:, 0:1024].rearrange("p (c k) -> p c k", c=8), func=AF.Copy),
                            r=[pb.r], w=[mixT.r])
                    for g4 in range(4):
                        pf = PF[g4]
                        for c in range(NKC):
                            S.op("pe", lambda e, c=c, g4=g4, pf=pf, mixT=mixT: e.matmul(
                                pf.t[:, 0:512], lhsT=mixT.t[:, c, :],
                                rhs=wo.t[:, c, g4 * 512:(g4 + 1) * 512], start=(c == 0), stop=(c == NKC - 1)),
                                r=[mixT.r, wo.r], w=[pf.r])
                        S.op("dve", lambda e, g4=g4, pf=pf, x_=x_: e.tensor_tensor(
                            out=x_.t[:, g4 * 512:(g4 + 1) * 512], in0=pf.t[:, 0:512],
                            in1=x_.t[:, g4 * 512:(g4 + 1) * 512], op=ALU.add), r=[pf.r, x_.r], w=[x_.r])
                    S.dma("sp", lambda e, j=j, x_=x_: e.dma_start(out=x1_d[j * 128:(j + 1) * 128, :], in_=x_.t[:]),
                          r=[x_.r], w=[r_x1[j]])
                S.flush()

        with ExitStack() as P:
            idx_all = sb(P, "idx_all", [128, NJ, 128], I32)
            gate_all = sb(P, "gate_all", [128, NJ, 128])
            r_idx = [Res(f"idx{j}") for j in range(NJ)]
            with ExitStack() as P6:
                wpq = load_w(P6, "wpq", w_pq, NKC, D)
                kT_f = sb(P6, "kT_f", [128, 16, 128])
                kT = sb(P6, "kT_b", [128, 16, 128], BF16)
                S.dma("sp", lambda e: e.dma_start(out=kT_f.t[:], in_=keysT), w=[kT_f.r])
                S.op("dve", lambda e: e.tensor_copy(out=kT.t[:], in_=kT_f.t[:]), r=[kT_f.r], w=[kT.r])
                gffn = sb(P6, "gffn", [128, D])
                S.dma("sp", lambda e: e.dma_start(out=gffn.t[:], in_=g_ffn_bc), w=[gffn.r])
                iota16 = sb(P6, "iota16", [128, 16])
                S.op("pool", lambda e: e.iota(iota16.t[:], pattern=[[1, 16]], base=0, channel_multiplier=0,
                                              allow_small_or_imprecise_dtypes=True), w=[iota16.r])
                xt = [sb(P6, f"x6_{i}", [128, D]) for i in range(2)]
                junk = sb(P6, "junk6", [128, D], BF16)
                ss = sb(P6, "ss6", [128, 1])
                rstd = sb(P6, "rstd6", [128, 1])
                hn = [sb(P6, f"hn6_{i}", [128, D]) for i in range(2)]
                hnb = sb(P6, "hnb6", [128, D], BF16)
                hT6 = sb(P6, "hT6", [128, NKC, 128], BF16)
                qpT = [sb(P6, f"qpT{i}", [128, 128], BF16) for i in range(2)]
                sc = sb(P6, "sc6", [128, 16, 128])
                scw = sb(P6, "scw6", [128, 16, 128])
                tops = sb(P6, "tops", [128, 16, 16])
                topi = sb(P6, "topi", [128, 16, 16], U32)
                topif = sb(P6, "topif", [128, 16, 16])
                cand = sb(P6, "cand", [128, 8, 256])
                candw = sb(P6, "candw", [128, 8, 256])
                best = sb(P6, "best", [128, 8, 16])
                bpos = sb(P6, "bpos", [128, 8, 16], U32)
                k1u = sb(P6, "k1u", [128, 8, 16], U32)
                k2u = sb(P6, "k2u", [128, 8, 16], U32)
                k1f = sb(P6, "k1f", [128, 8, 16])
                k2f = sb(P6, "k2f", [128, 8, 16])
                oh = sb(P6, "oh6", [128, 16, 16])
                i1 = sb(P6, "i1", [128, 8, 16])
                i2 = sb(P6, "i2", [128, 8, 16])
                bm = sb(P6, "bm", [128, 8])
                bs = sb(P6, "bs", [128, 8])
                for j in range(NJ):
                    x_ = xt[j % 2]
                    hn_ = hn[j % 2]
                    S.dma("sp", lambda e, j=j, x_=x_: e.dma_start(out=x_.t[:], in_=x1_d[j * 128:(j + 1) * 128, :]),
                          r=[r_x1[j]], w=[x_.r])
                    S.op("act", lambda e, x_=x_: e.activation(out=junk.t[:], in_=x_.t[:], func=AF.Square,
                                                              accum_out=ss.t[:]), r=[x_.r], w=[junk.r, ss.r])
                    rstd_from_ss(ss, rstd, D)
                    S.op("dve", lambda e, x_=x_, hn_=hn_: e.scalar_tensor_tensor(
                        out=hn_.t[:], in0=x_.t[:], scalar=rstd.t[:, 0:1], in1=gffn.t[:], op0=ALU.mult, op1=ALU.mult),
                        r=[x_.r, rstd.r, gffn.r], w=[hn_.r])
                    S.dma("sp", lambda e, j=j, hn_=hn_: e.dma_start(out=hn2_d[j * 128:(j + 1) * 128, :], in_=hn_.t[:]),
                          r=[hn_.r], w=[r_hn2[j]])
                    S.op("act", lambda e, hn_=hn_: e.activation(out=hnb.t[:], in_=hn_.t[:], func=AF.Copy),
                         r=[hn_.r], w=[hnb.r])
                    for half in range(2):
                        pb = PB[half]
                        for cc in range(8):
                            c = half * 8 + cc
                            S.op("pe", lambda e, c=c, cc=cc, pb=pb: e.transpose(
                                out=pb.t[:, cc * 128:(cc + 1) * 128], in_=hnb.t[:, c * 128:(c + 1) * 128],
                                identity=ident.t[:]), r=[hnb.r, ident.r], w=[pb.r])
                        S.op("dve" if half else "act", (lambda e, half=half, pb=pb: e.tensor_copy(
                            out=hT6.t[:, half * 8:(half + 1) * 8, :],
                            in_=pb.t[:, 0:1024].rearrange("p (c k) -> p c k", c=8))) if half else
                            (lambda e, half=half, pb=pb: e.activation(
                                out=hT6.t[:, half * 8:(half + 1) * 8, :],
                                in_=pb.t[:, 0:1024].rearrange("p (c k) -> p c k", c=8), func=AF.Copy)),
                            r=[pb.r], w=[hT6.r])
                    for hp_ in range(16):
                        pq = PF[4 + hp_ % 2]
                        for c in range(NKC):
                            S.op("pe", lambda e, c=c, hp_=hp_, pq=pq: e.matmul(
                                pq.t[:, 0:128], lhsT=wpq.t[:, c, hp_ * 128:(hp_ + 1) * 128], rhs=hT6.t[:, c, :],
                                start=(c == 0), stop=(c == NKC - 1)), r=[wpq.r, hT6.r], w=[pq.r])
                        qp = qpT[hp_ % 2]
                        S.op("act", lambda e, pq=pq, qp=qp: e.activation(out=qp.t[:], in_=pq.t[:, 0:128], func=AF.Copy),
                             r=[pq.r], w=[qp.r])
                        psc = PF[hp_ // 4]
                        S.op("pe", lambda e, hp_=hp_, qp=qp, psc=psc: e.matmul(
                            psc.t[:, (hp_ % 4) * 128:(hp_ % 4 + 1) * 128], lhsT=qp.t[:], rhs=kT.t[:, hp_, :],
                            start=True, stop=True), r=[qp.r, kT.r], w=[psc.r])
                    for q4 in range(4):
                        S.op("dve", lambda e, q4=q4: e.tensor_copy(
                            out=sc.t[:, q4 * 4:(q4 + 1) * 4, :],
                            in_=PF[q4].t[:, 0:512].rearrange("p (a k) -> p a k", a=4)), r=[PF[q4].r], w=[sc.r])
                    for hp_ in range(16):
                        S.op("dve", lambda e, hp_=hp_: e.max(out=tops.t[:, hp_, 0:8], in_=sc.t[:, hp_, :]),
                             r=[sc.r], w=[tops.r])
                        S.op("dve", lambda e, hp_=hp_: e.max_index(out=topi.t[:, hp_, 0:8], in_max=tops.t[:, hp_, 0:8],
                                                                   in_values=sc.t[:, hp_, :]),
                             r=[sc.r, tops.r], w=[topi.r])
                        S.op("dve", lambda e, hp_=hp_: e.match_replace(out=scw.t[:, hp_, :],
                                                                       in_to_replace=tops.t[:, hp_, 0:8],
                                                                       in_values=sc.t[:, hp_, :], imm_value=-1e30),
                             r=[sc.r, tops.r], w=[scw.r])
                        S.op("dve", lambda e, hp_=hp_: e.max(out=tops.t[:, hp_, 8:16], in_=scw.t[:, hp_, :]),
                             r=[scw.r], w=[tops.r])
                        S.op("dve", lambda e, hp_=hp_: e.max_index(out=topi.t[:, hp_, 8:16],
                                                                   in_max=tops.t[:, hp_, 8:16],
                                                                   in_values=scw.t[:, hp_, :]),
                             r=[scw.r, tops.r], w=[topi.r])
                    S.op("dve", lambda e: e.tensor_copy(out=topif.t[:], in_=topi.t[:]), r=[topi.r], w=[topif.r])
                    for hd in range(8):
                        S.op("dve", lambda e, hd=hd: e.tensor_tensor(
                            out=cand.t[:, hd, :].rearrange("p (a b) -> p a b", a=16),
                            in0=tops.t[:, 2 * hd, :].unsqueeze(2).to_broadcast([128, 16, 16]),
                            in1=tops.t[:, 2 * hd + 1, :].unsqueeze(1).to_broadcast([128, 16, 16]), op=ALU.add),
                            r=[tops.r], w=[cand.r])
                    for hd in range(8):
                        S.op("dve", lambda e, hd=hd: e.max(out=best.t[:, hd, 0:8], in_=cand.t[:, hd, :]),
                             r=[cand.r], w=[best.r])
                        S.op("dve", lambda e, hd=hd: e.max_index(out=bpos.t[:, hd, 0:8], in_max=best.t[:, hd, 0:8],
                                                                 in_values=cand.t[:, hd, :]),
                             r=[cand.r, best.r], w=[bpos.r])
                        S.op("dve", lambda e, hd=hd: e.match_replace(out=candw.t[:, hd, :],
                                                                     in_to_replace=best.t[:, hd, 0:8],
                                                                     in_values=cand.t[:, hd, :], imm_value=-1e30),
                             r=[cand.r, best.r], w=[candw.r])
                        S.op("dve", lambda e, hd=hd: e.max(out=best.t[:, hd, 8:16], in_=candw.t[:, hd, :]),
                             r=[candw.r], w=[best.r])
                        S.op("dve", lambda e, hd=hd: e.max_index(out=bpos.t[:, hd, 8:16], in_max=best.t[:, hd, 8:16],
                                                                 in_values=candw.t[:, hd, :]),
                             r=[candw.r, best.r], w=[bpos.r])
                    S.op("dve", lambda e: e.tensor_single_scalar(out=k1u.t[:], in_=bpos.t[:], scalar=4,
                                                                 op=ALU.logical_shift_right), r=[bpos.r], w=[k1u.r])
                    S.op("dve", lambda e: e.tensor_single_scalar(out=k2u.t[:], in_=bpos.t[:], scalar=15,
                                                                 op=ALU.bitwise_and), r=[bpos.r], w=[k2u.r])
                    S.op("dve", lambda e: e.tensor_copy(out=k1f.t[:], in_=k1u.t[:]), r=[k1u.r], w=[k1f.r])
                    S.op("dve", lambda e: e.tensor_copy(out=k2f.t[:], in_=k2u.t[:]), r=[k2u.r], w=[k2f.r])
                    for hd in range(8):
                        for side, kf_, io_ in ((0, k1f, i1), (1, k2f, i2)):
                            S.op("dve", lambda e, hd=hd, kf_=kf_: e.tensor_tensor(
                                out=oh.t[:], in0=iota16.t[:].unsqueeze(1).to_broadcast([128, 16, 16]),
                                in1=kf_.t[:, hd, :].unsqueeze(2).to_broadcast([128, 16, 16]), op=ALU.is_equal),
                                r=[iota16.r, kf_.r], w=[oh.r])
                            S.op("dve", lambda e, hd=hd, side=side: e.tensor_tensor(
                                out=oh.t[:], in0=oh.t[:],
                                in1=topif.t[:, 2 * hd + side, :].unsqueeze(1).to_broadcast([128, 16, 16]),
                                op=ALU.mult), r=[oh.r, topif.r], w=[oh.r])
                            S.op("dve", lambda e, hd=hd, io_=io_: e.tensor_reduce(out=io_.t[:, hd, :], in_=oh.t[:],
                                                                                  axis=AX.X, op=ALU.add),
                                 r=[oh.r], w=[io_.r])
                    S.op("dve", lambda e: e.scalar_tensor_tensor(out=i1.t[:], in0=i1.t[:], scalar=128.0, in1=i2.t[:],
                                                                 op0=ALU.mult, op1=ALU.add),
                         r=[i1.r, i2.r], w=[i1.r])
                    S.op("dve", lambda e, j=j: e.tensor_copy(
                        out=idx_all.t[:, j, :].rearrange("p (a b) -> p a b", a=8), in_=i1.t[:]),
                        r=[i1.r], w=[r_idx[j]])
                    S.op("dve", lambda e: e.tensor_reduce(out=bm.t[:], in_=best.t[:], axis=AX.X, op=ALU.max),
                         r=[best.r], w=[bm.r])
                    S.op("dve", lambda e: e.tensor_tensor(out=best.t[:], in0=best.t[:],
                                                          in1=bm.t[:].unsqueeze(2).to_broadcast([128, 8, 16]),
                                                          op=ALU.subtract), r=[best.r, bm.r], w=[best.r])
                    S.op("act", lambda e: e.activation(out=best.t[:], in_=best.t[:], func=AF.Exp),
                         r=[best.r], w=[best.r])
                    S.op("dve", lambda e: e.tensor_reduce(out=bs.t[:], in_=best.t[:], axis=AX.X, op=ALU.add),
                         r=[best.r], w=[bs.r])
                    S.op("dve", lambda e: e.reciprocal(out=bs.t[:], in_=bs.t[:]), r=[bs.r], w=[bs.r])
                    S.op("dve", lambda e, j=j: e.tensor_tensor(
                        out=gate_all.t[:, j, :].rearrange("p (a b) -> p a b", a=8), in0=best.t[:],
                        in1=bs.t[:].unsqueeze(2).to_broadcast([128, 8, 16]), op=ALU.mult),
                        r=[best.r, bs.r], w=[r_idx[j]])
                S.flush()

            with ExitStack() as P7:
                gfin = sb(P7, "gfin", [128, D])
                S.dma("sp", lambda e: e.dma_start(out=gfin.t[:], in_=g_fin_bc), w=[gfin.r])
                NB = 8
                GS = 4
                gb = [sb(P7, f"gb{i}", [128, 2 * D], BF16) for i in range(NB)]
                hn7 = [sb(P7, f"hn7_{i}", [128, D]) for i in range(2)]
                hnb7 = [sb(P7, f"hnb7_{i}", [128, D], BF16) for i in range(2)]
                x7 = [sb(P7, f"x7_{i}", [128, D]) for i in range(2)]
                acc = sb(P7, "acc7", [128, D])
                junkb = sb(P7, "junk7b", [128, D], BF16)
                diag = [sb(P7, f"diag{i}", [128, 128], BF16) for i in range(8)]
                avs = [sb(P7, f"av7_{i}", [128, GS]) for i in range(4)]
                wvs = [sb(P7, f"wv7_{i}", [128, GS]) for i in range(4)]
                ss = sb(P7, "ss7", [128, 1])
                rstd = sb(P7, "rstd7", [128, 1])
                bi = 0
                gi = 0
                for j in range(NJ):
                    hn_ = hn7[j % 2]
                    hb_ = hnb7[j % 2]
                    x_ = x7[j % 2]
                    S.dma("sp", lambda e, j=j, hn_=hn_: e.dma_start(out=hn_.t[:], in_=hn2_d[j * 128:(j + 1) * 128, :]),
                          r=[r_hn2[j]], w=[hn_.r])
                    S.dma("sp", lambda e, j=j, x_=x_: e.dma_start(out=x_.t[:], in_=x1_d[j * 128:(j + 1) * 128, :]),
                          r=[r_x1[j]], w=[x_.r])
                    S.op("act", lambda e, hn_=hn_, hb_=hb_: e.activation(out=hb_.t[:], in_=hn_.t[:], func=AF.Copy),
                         r=[hn_.r], w=[hb_.r])
                    for s0 in range(0, 128, GS):
                        av = avs[gi % 4]
                        wv7 = wvs[gi % 4]
                        gi += 1
                        gl = []
                        for k in range(GS):
                            s_ = s0 + k
                            g_ = gb[bi % NB]
                            bi += 1
                            gl.append(g_)
                            S.dma("pool", lambda e, g_=g_, j=j, s_=s_: e.indirect_dma_start(
                                out=g_.t[:], out_offset=None, in_=uv_b,
                                in_offset=bass.IndirectOffsetOnAxis(ap=idx_all.t[:, j, s_:s_ + 1], axis=0)),
                                r=[r_idx[j], r_uvb], w=[g_.r])
                            S.op("dve", lambda e, g_=g_, hb_=hb_, k=k, av=av: e.scalar_tensor_tensor(
                                out=junkb.t[:], in0=g_.t[:, 0:D], scalar=1.0, in1=hb_.t[:], op0=ALU.mult, op1=ALU.mult,
                                accum_out=av.t[:, k:k + 1]), r=[g_.r, hb_.r], w=[junkb.r, av.r])
                        S.op("act", lambda e, av=av, wv7=wv7: e.activation(out=wv7.t[:], in_=av.t[:], func=AF.Gelu),
                             r=[av.r], w=[wv7.r])
                        S.op("dve", lambda e, j=j, s0=s0, wv7=wv7: e.tensor_tensor(
                            out=wv7.t[:], in0=wv7.t[:], in1=gate_all.t[:, j, s0:s0 + GS], op=ALU.mult),
                            r=[wv7.r, r_idx[j]], w=[wv7.r])
                        for k in range(GS):
                            s_ = s0 + k
                            g_ = gl[k]
                            dg = diag[s_ % 8]
                            S.op("act", lambda e, dg=dg, k=k, wv7=wv7: e.activation(
                                out=dg.t[:], in_=ident.t[:], func=AF.Copy, scale=wv7.t[:, k:k + 1]),
                                r=[ident.r, wv7.r], w=[dg.r])
                            for q4 in range(4):
                                S.op("pe", lambda e, dg=dg, g_=g_, q4=q4, s_=s_: e.matmul(
                                    PF[q4].t[:, 0:512], lhsT=dg.t[:], rhs=g_.t[:, D + q4 * 512:D + (q4 + 1) * 512],
                                    start=(s_ == 0), stop=(s_ == 127)), r=[dg.r, g_.r], w=[PF[q4].r])
                    for q4 in range(4):
                        S.op("dve", lambda e, q4=q4, x_=x_: e.tensor_tensor(
                            out=x_.t[:, q4 * 512:(q4 + 1) * 512], in0=PF[q4].t[:, 0:512],
                            in1=x_.t[:, q4 * 512:(q4 + 1) * 512], op=ALU.add), r=[PF[q4].r, x_.r], w=[x_.r])
                    S.op("act", lambda e, x_=x_: e.activation(out=junkb.t[:], in_=x_.t[:], func=AF.Square,
                                                              accum_out=ss.t[:]), r=[x_.r], w=[junkb.r, ss.r])
                    rstd_from_ss(ss, rstd, D)
                    S.op("dve", lambda e, x_=x_: e.scalar_tensor_tensor(
                        out=acc.t[:], in0=x_.t[:], scalar=rstd.t[:, 0:1], in1=gfin.t[:], op0=ALU.mult, op1=ALU.mult),
                        r=[x_.r, rstd.r, gfin.r], w=[acc.r])
                    S.dma("sp", lambda e, j=j: e.dma_start(out=out_d[j * 128:(j + 1) * 128, :], in_=acc.t[:]),
                          r=[acc.r], w=[Res("out")])
                S.flush()
    return nc


def make_in_maps(inp, NJ, cores):
    NS = 4 * NJ
    SQ = 128 * NS
    f32 = np.float32
    x = np.asarray(inp["x"], f32)
    pos = np.asarray(inp["positions"], np.int32)
    w_in = np.asarray(inp["w_in"], f32)[0]
    offs = np.cumsum([0, 1024, 256, 256, 512, 256, 64, 2048, 2048])
    q_a, k_a, v_a, c_q, c_kv, k_r, g_a, g_b = [slice(int(offs[i]), int(offs[i + 1])) for i in range(8)]
    C = np.ascontiguousarray

    def bc(v, n=128):
        return C(np.broadcast_to(np.asarray(v, f32)[None, :], (n, len(v))))

    w_ukv = np.asarray(inp["w_ukv"], f32)[0].reshape(256, 8, 256)
    shared = {
        "invf_bc": bc(1.0 / (10000.0 ** (np.arange(0, 64, 2, dtype=np.float32) / 64)).astype(f32)),
        "g_mix_pc": C(np.asarray(inp["g_mix"], f32)[0].reshape(NKC, 128).T),
        "g_ffn_bc": bc(np.asarray(inp["g_ffn"], f32)[0]),
        "g_fin_bc": bc(np.asarray(inp["g_final"], f32)),
        "g_ckv_bc": bc(np.asarray(inp["g_ckv"], f32)[0]),
        "g_cq_bc": bc(np.asarray(inp["g_cq"], f32)[0]),
        "sinks_bc": bc(np.asarray(inp["sinks"], f32)[0]),
        "w_kv": C(np.concatenate([w_in[:, c_kv], w_in[:, k_r]], axis=1)),
        "w_kva": C(np.concatenate([w_in[:, k_a], w_in[:, v_a]], axis=1)),
        "w_cq": C(w_in[:, c_q]),
        "w_qa": C(w_in[:, q_a]),
        "w_gate": C(np.concatenate([w_in[:, g_a], w_in[:, g_b]], axis=1)),
        "w_uq": C(np.asarray(inp["w_uq"], f32)[0]),
        "w_ukv_k": C(w_ukv[:, :, :128].reshape(256, 1024)),
        "w_ukv_v": C(w_ukv[:, :, 128:].reshape(256, 1024)),
        "w_a": C(np.asarray(inp["w_a_proj"], f32)[0]),
        "w_b": C(np.asarray(inp["w_b_proj"], f32)[0]),
        "w_o": C(np.asarray(inp["w_o"], f32)[0]),
        "w_pq": C(np.asarray(inp["w_peer_q"], f32)[0]),
        "keysT": C(np.asarray(inp["peer_keys"], f32)[0].reshape(16, 128, 128).transpose(2, 0, 1)),
        "u_tab": C(np.asarray(inp["peer_u"], f32)[0]),
        "v_tab": C(np.asarray(inp["peer_v"], f32)[0]),
        "ident": np.eye(128, dtype=f32),
    }
    qi = np.arange(128)[:, None]
    kj = np.arange(256)[None, :]
    rel = 128 + qi - kj
    band = (rel >= 0) & (rel < 128)
    maps = []
    for c in cores:
        b, r = c // 4, c % 4
        xs = x[b, :SQ]
        ps = pos[b, :SQ]
        own = [4 * j + r for j in range(NJ)]
        m = dict(shared)
        m["x_seq"] = C(xs)
        m["x_own"] = C(np.concatenate([xs[t * 128:(t + 1) * 128] for t in own], axis=0))
        xp = []
        pk = []
        for t in own:
            if t == 0:
                xp.append(np.zeros((128, D), f32))
                pprev = np.zeros(128, np.int32)
            else:
                xp.append(xs[(t - 1) * 128:t * 128])
                pprev = ps[(t - 1) * 128:t * 128]
            row = np.concatenate([pprev, ps[t * 128:(t + 1) * 128]])
            pk.append(np.broadcast_to(row[None, :], (128, 256)))
        m["x_prev"] = C(np.concatenate(xp, axis=0))
        m["pos_kv_bc"] = C(np.stack(pk, axis=0).astype(np.int32))
        m["pos_seq"] = C(ps.reshape(NS, 128).T)
        m["pos_own"] = C(np.stack([ps[t * 128:(t + 1) * 128] for t in own], axis=1))
        mb0 = np.where(band & ((kj >= 128) | (own[0] > 0)), 0.0, -1e30).astype(f32)
        mb1 = np.where(band, 0.0, -1e30).astype(f32)
        m["swa_maskb"] = C(np.stack([mb0, mb1], axis=0))
        mm = np.zeros((128, 512), f32)
        for kk in range(4):
            if kk < r:
                mm[:, kk * 128:(kk + 1) * 128] = 1.0
            elif kk == r:
                mm[:, kk * 128:(kk + 1) * 128] = (np.arange(128)[None, :] <= np.arange(128)[:, None])
        m["mla_mask"] = mm
        maps.append(m)
    return maps


def run(inputs, NJ, cores):
    nc = build(NJ)
    maps = make_in_maps(inputs, NJ, cores)
    res = run_bass_kernel_spmd(nc, maps, core_ids=list(range(len(cores))))
    return [r["out"] for r in res.results]


def kernel(**inputs):
    NJ = 16
    cores = list(range(8))
    outs = run(inputs, NJ, cores)
    B, SQ = 2, 8192
    out = np.empty((B, SQ, D), np.float32)
    for c, o in zip(cores, outs):
        b, r = c // 4, c % 4
        for j in range(NJ):
            t = 4 * j + r
            out[b, t * 128:(t + 1) * 128] = o[j * 128:(j + 1) * 128]
    return out
```

```python
import math
from contextlib import ExitStack

import numpy as np
import concourse.bass as bass
import concourse.mybir as mybir
from concourse.bass_utils import run_bass_kernel_spmd

F32 = mybir.dt.float32
BF16 = mybir.dt.bfloat16
I32 = mybir.dt.int32
U32 = mybir.dt.uint32
AF = mybir.ActivationFunctionType
ALU = mybir.AluOpType
AX = mybir.AxisListType

D = 2048
EPS = 1e-6
NKC = D // 128
PI = math.pi
TWO_PI = 2.0 * math.pi
MLA_SCALE = 192 ** -0.5
N_EXP = 16384


class Res:
    __slots__ = ("name", "w", "rd")

    def __init__(self, name=""):
        self.name = name
        self.w = None
        self.rd = {}


class Sched:
    ENGS = ("pe", "act", "dve", "pool", "sp")
    NDMA = {"sp": 8, "act": 4, "pool": 8}

    def __init__(self, nc, es):
        self.nc = nc
        self.sem = {}
        for e in ("pe", "act", "dve", "pool"):
            self.sem[e] = es.enter_context(nc.semaphore("c_" + e))
        for q, n in self.NDMA.items():
            for k in range(n):
                self.sem[("d", q, k)] = es.enter_context(nc.semaphore(f"d_{q}{k}"))
        self.cnt = {k: 0 for k in self.sem}
        self.seen = {e: {} for e in self.ENGS}
        self.items = {e: [] for e in self.ENGS}
        self.drr = {q: 0 for q in self.NDMA}

    def _deps(self, eng, reads, writes):
        need = {}

        def add(kv):
            if kv is None:
                return
            k, v = kv
            if k == "pe" and eng == "pe":
                return
            if need.get(k, 0) < v:
                need[k] = v
        for r in reads:
            add(r.w)
        for w in writes:
            add(w.w)
            for k, v in w.rd.items():
                add((k, v))
        waits = []
        sn = self.seen[eng]
        for k, v in need.items():
            if sn.get(k, 0) < v:
                sn[k] = v
                waits.append((k, v))
        return waits

    @staticmethod
    def _mark(key, val, reads, writes):
        for r in reads:
            if r.rd.get(key, 0) < val:
                r.rd[key] = val
        for w in writes:
            w.w = (key, val)
            w.rd = {}

    def op(self, eng, fn, r=(), w=()):
        waits = self._deps(eng, r, w)
        self.cnt[eng] += 1
        self._mark(eng, self.cnt[eng], r, w)
        self.items[eng].append((waits, fn, (eng, 1)))

    def dma(self, q, fn, r=(), w=()):
        k = self.drr[q]
        self.drr[q] = (k + 1) % self.NDMA[q]
        key = ("d", q, k)
        waits = self._deps(q, r, w)
        prev = self.cnt[key]
        if prev and self.seen[q].get(key, 0) < prev:
            self.seen[q][key] = prev
            waits.append((key, prev))
        self.cnt[key] += 16
        self._mark(key, self.cnt[key], r, w)
        self.items[q].append((waits, fn, (key, 16)))

    def barrier(self):
        for e in self.ENGS:
            waits = []
            for k, v in self.cnt.items():
                if v and self.seen[e].get(k, 0) < v:
                    self.seen[e][k] = v
                    waits.append((k, v))
            if waits:
                self.items[e].append((waits, None, None))

    def flush(self):
        self.barrier()
        nc, items, sem = self.nc, self.items, self.sem

        def emit(h, lst):
            for waits, fn, inc in lst:
                for k, v in waits:
                    h.wait_ge(sem[k], v)
                if fn is not None:
                    fn(h).then_inc(sem[inc[0]], inc[1])

        with nc.Block() as block:
            @block.tensor
            def _(e):
                emit(e, items["pe"])

            @block.scalar
            def _(e):
                emit(e, items["act"])

            @block.vector
            def _(e):
                emit(e, items["dve"])

            @block.gpsimd
            def _(e):
                emit(e, items["pool"])

            @block.sync
            def _(e):
                emit(e, items["sp"])
        self.items = {e: [] for e in self.ENGS}


class Tl:
    __slots__ = ("t", "r")

    def __init__(self, t, name):
        self.t = t
        self.r = Res(name)


def build(NJ):
    NS = 4 * NJ
    SQ = 128 * NS
    TK = 128 * NJ
    nc = bass.Bass("TRN2", target_bir_lowering=False)

    def din(name, shape, dt=F32):
        return nc.dram_tensor(name, list(shape), dt, kind="ExternalInput").ap()

    def dscr(name, shape, dt):
        return nc.dram_tensor(name, list(shape), dt, kind="Internal").ap()

    x_seq = din("x_seq", [SQ, D])
    x_own = din("x_own", [TK, D])
    x_prev = din("x_prev", [TK, D])
    pos_seq = din("pos_seq", [128, NS], I32)
    pos_own = din("pos_own", [128, NJ], I32)
    pos_kv_bc = din("pos_kv_bc", [NJ, 128, 256], I32)
    swa_maskb = din("swa_maskb", [2, 128, 256])
    mla_mask = din("mla_mask", [128, 512])
    invf_bc = din("invf_bc", [128, 32])
    g_mix_pc = din("g_mix_pc", [128, NKC])
    g_ffn_bc = din("g_ffn_bc", [128, D])
    g_fin_bc = din("g_fin_bc", [128, D])
    g_ckv_bc = din("g_ckv_bc", [128, 256])
    g_cq_bc = din("g_cq_bc", [128, 512])
    sinks_bc = din("sinks_bc", [128, 16])
    w_kv = din("w_kv", [D, 320])
    w_kva = din("w_kva", [D, 512])
    w_cq = din("w_cq", [D, 512])
    w_qa = din("w_qa", [D, 1024])
    w_gate = din("w_gate", [D, 4096])
    w_uq = din("w_uq", [512, 1536])
    w_ukv_k = din("w_ukv_k", [256, 1024])
    w_ukv_v = din("w_ukv_v", [256, 1024])
    w_a = din("w_a", [1024, D])
    w_b = din("w_b", [1024, D])
    w_o = din("w_o", [D, D])
    w_pq = din("w_pq", [D, D])
    keysT = din("keysT", [128, 16, 128])
    u_tab = din("u_tab", [N_EXP, D])
    v_tab = din("v_tab", [N_EXP, D])
    ident_in = din("ident", [128, 128])
    out_d = nc.dram_tensor("out", [TK, D], F32, kind="ExternalOutput").ap()

    qnT_d = dscr("qnT_d", [8, 128, TK], BF16)
    qrT_d = dscr("qrT_d", [8, 64, TK], BF16)
    swa_d = dscr("swa_d", [TK, 1024], BF16)
    mla_d = dscr("mla_d", [TK, 1024], BF16)
    zg_d = dscr("zg_d", [TK, 4096], F32)
    x1_d = dscr("x1_d", [TK, D], F32)
    mixed_d = dscr("mixed_d", [TK, D], BF16)
    hn2_d = dscr("hn2_d", [TK, D], F32)
    uv_b = dscr("uv_b", [N_EXP, 2 * D], BF16)
    r_uvb = Res("uv_b")
    r_qT = Res("qT_d")
    r_swa = [Res(f"swa_d{j}") for j in range(NJ)]
    r_mla = [Res(f"mla_d{j}") for j in range(NJ)]
    r_zg = [Res(f"zg_d{j}") for j in range(NJ)]
    r_x1 = [Res(f"x1_d{j}") for j in range(NJ)]
    r_hn2 = [Res(f"hn2_d{j}") for j in range(NJ)]

    slopes = [2.0 ** (-8.0 * (h + 1) / 16) for h in range(16)]

    with ExitStack() as G:
        S = Sched(nc, G)

        def sb(es, name, shape, dt=F32):
            return Tl(es.enter_context(nc.sbuf_tensor(name, list(shape), dt)), name)

        def psum(es, name, shape, dt=F32):
            return Tl(es.enter_context(nc.psum_tensor(name, list(shape), dt)), name)

        PF = [psum(G, f"pf{i}", [128, 512], F32) for i in range(6)]
        PB = [psum(G, f"pb{i}", [128, 1024], BF16) for i in range(2)]

        ident_f = sb(G, "ident_f", [128, 128])
        ident = sb(G, "ident_b", [128, 128], BF16)
        S.dma("sp", lambda e: e.dma_start(out=ident_f.t[:], in_=ident_in), w=[ident_f.r])
        S.op("dve", lambda e: e.tensor_copy(out=ident.t[:], in_=ident_f.t[:]), r=[ident_f.r], w=[ident.r])
        gmix = sb(G, "gmix", [128, NKC])
        S.dma("sp", lambda e: e.dma_start(out=gmix.t[:], in_=g_mix_pc), w=[gmix.r])
        invf = sb(G, "invf", [128, 32])
        S.dma("sp", lambda e: e.dma_start(out=invf.t[:], in_=invf_bc), w=[invf.r])
        GK = ExitStack()
        ckvT = sb(GK, "ckvT", [128, 2, SQ], BF16)
        kropeT = sb(GK, "kropeT", [128, SQ], BF16)
        S.op("dve", lambda e: e.memset(kropeT.t[64:128, :], 0.0), w=[kropeT.r])
        r_ckvT = [Res(f"ckvT{t}") for t in range(NS)]

        def load_w(es, name, w_ap, kchunks, ncols, dst=None, col0=0, ncol_dst=None):
            t = dst if dst is not None else sb(es, name, [128, kchunks, ncols], BF16)
            src = w_ap.rearrange("(c p) n -> p c n", p=128)
            step = max(1, 4096 // ncols)
            for c0 in range(0, kchunks, step):
                c1 = min(kchunks, c0 + step)
                S.dma("pool", lambda e, c0=c0, c1=c1: e.dma_start(out=t.t[:, c0:c1, :], in_=src[:, c0:c1, :]),
                      w=[t.r])
            return t

        def rstd_from_ss(ss, rstd, n):
            S.op("dve", lambda e: e.tensor_scalar(out=rstd.t[:], in0=ss.t[:], scalar1=1.0 / n, scalar2=EPS,
                                                  op0=ALU.mult, op1=ALU.add), r=[ss.r], w=[rstd.r])
            S.op("act", lambda e: e.activation(out=rstd.t[:], in_=rstd.t[:], func=AF.Sqrt), r=[rstd.r], w=[rstd.r])
            S.op("dve", lambda e: e.reciprocal(out=rstd.t[:], in_=rstd.t[:]), r=[rstd.r], w=[rstd.r])

        def range_reduce(src, dst, ki, kf, tmp):
            S.op("dve", lambda e: e.tensor_scalar(out=ki.t[:], in0=src.t[:], scalar1=1.0 / TWO_PI, scalar2=None,
                                                  op0=ALU.mult), r=[src.r], w=[ki.r])
            S.op("dve", lambda e: e.tensor_copy(out=kf.t[:], in_=ki.t[:]), r=[ki.r], w=[kf.r])
            S.op("dve", lambda e: e.scalar_tensor_tensor(out=dst.t[:], in0=kf.t[:], scalar=-TWO_PI, in1=src.t[:],
                                                         op0=ALU.mult, op1=ALU.add), r=[kf.r, src.r], w=[dst.r])
            S.op("dve", lambda e: e.tensor_scalar(out=tmp.t[:], in0=dst.t[:], scalar1=PI, scalar2=-TWO_PI,
                                                  op0=ALU.is_gt, op1=ALU.mult), r=[dst.r], w=[tmp.r])
            S.op("dve", lambda e: e.tensor_tensor(out=dst.t[:], in0=dst.t[:], in1=tmp.t[:], op=ALU.add),
                 r=[dst.r, tmp.r], w=[dst.r])
            S.op("dve", lambda e: e.tensor_scalar(out=tmp.t[:], in0=dst.t[:], scalar1=-PI, scalar2=TWO_PI,
                                                  op0=ALU.is_lt, op1=ALU.mult), r=[dst.r], w=[tmp.r])
            S.op("dve", lambda e: e.tensor_tensor(out=dst.t[:], in0=dst.t[:], in1=tmp.t[:], op=ALU.add),
                 r=[dst.r, tmp.r], w=[dst.r])
            S.op("dve", lambda e: e.tensor_scalar(out=dst.t[:], in0=dst.t[:], scalar1=-3.1415925, scalar2=3.1415925,
                                                  op0=ALU.max, op1=ALU.min), r=[dst.r], w=[dst.r])

        def cos_sin(es, name, pos_ap, n):
            cs = sb(es, name + "_cos", [128, n, 32])
            sn = sb(es, name + "_sin", [128, n, 32])
            with ExitStack() as L:
                pi_ = sb(L, name + "_pi", [128, n], I32)
                pf_ = sb(L, name + "_pf", [128, n])
                ang = sb(L, name + "_ang", [128, n, 32])
                red = sb(L, name + "_red", [128, n, 32])
                ki = sb(L, name + "_ki", [128, n, 32], I32)
                kf = sb(L, name + "_kf", [128, n, 32])
                tmp = sb(L, name + "_tmp", [128, n, 32])
                S.dma("sp", lambda e: e.dma_start(out=pi_.t[:], in_=pos_ap), w=[pi_.r])
                S.op("dve", lambda e: e.tensor_copy(out=pf_.t[:], in_=pi_.t[:]), r=[pi_.r], w=[pf_.r])
                S.op("dve", lambda e: e.tensor_tensor(out=ang.t[:],
                                                      in0=pf_.t[:].unsqueeze(2).to_broadcast([128, n, 32]),
                                                      in1=invf.t[:].unsqueeze(1).to_broadcast([128, n, 32]),
                                                      op=ALU.mult), r=[pf_.r, invf.r], w=[ang.r])
                range_reduce(ang, red, ki, kf, tmp)
                S.op("act", lambda e: e.activation(out=sn.t[:], in_=red.t[:], func=AF.Sin), r=[red.r], w=[sn.r])
                S.op("dve", lambda e: e.tensor_scalar(out=ang.t[:], in0=ang.t[:], scalar1=PI / 2, scalar2=None,
                                                      op0=ALU.add), r=[ang.r], w=[ang.r])
                range_reduce(ang, red, ki, kf, tmp)
                S.op("act", lambda e: e.activation(out=cs.t[:], in_=red.t[:], func=AF.Sin), r=[red.r], w=[cs.r])
                S.flush()
            return cs, sn

        class NormCtx:
            def __init__(self, es, name, gpc):
                self.x = [sb(es, f"{name}_x{i}", [128, D]) for i in range(2)]
                self.junk = sb(es, name + "_junk", [128, D], BF16)
                self.ss = sb(es, name + "_ss", [128, 1])
                self.rstd = sb(es, name + "_rstd", [128, 1])
                self.xn = sb(es, name + "_xn", [128, D], BF16)
                self.gpc = gpc
                self.i = 0

            def run(self, x_ap, hT, hT_slice, extra_r=()):
                xt = self.x[self.i % 2]
                self.i += 1
                S.dma("sp", lambda e: e.dma_start(out=xt.t[:], in_=x_ap), r=list(extra_r), w=[xt.r])
                S.op("act", lambda e: e.activation(out=self.junk.t[:], in_=xt.t[:], func=AF.Square,
                                                   accum_out=self.ss.t[:]), r=[xt.r], w=[self.junk.r, self.ss.r])
                rstd_from_ss(self.ss, self.rstd, D)
                S.op("dve", lambda e: e.tensor_scalar(out=self.xn.t[:], in0=xt.t[:], scalar1=self.rstd.t[:, 0:1],
                                                      scalar2=None, op0=ALU.mult), r=[xt.r, self.rstd.r], w=[self.xn.r])
                for half in range(2):
                    pb = PB[half]
                    for cc in range(8):
                        c = half * 8 + cc
                        S.op("pe", lambda e, c=c, cc=cc, pb=pb: e.transpose(
                            out=pb.t[:, cc * 128:(cc + 1) * 128], in_=self.xn.t[:, c * 128:(c + 1) * 128],
                            identity=ident.t[:]), r=[self.xn.r, ident.r], w=[pb.r])
                    S.op("dve", lambda e, half=half, pb=pb: e.tensor_tensor(
                        out=hT_slice(slice(half * 8, half * 8 + 8)),
                        in0=pb.t[:, 0:1024].rearrange("p (c k) -> p c k", c=8),
                        in1=self.gpc.t[:, half * 8:half * 8 + 8].unsqueeze(2).to_broadcast([128, 8, 128]),
                        op=ALU.mult), r=[pb.r, self.gpc.r], w=[hT.r])
                return xt

        def proj(hT, hT_slice, w, col0, ncols, pf):
            for c in range(NKC):
                S.op("pe", lambda e, c=c: e.matmul(pf.t[:, 0:ncols], lhsT=hT_slice(c), rhs=w.t[:, c, col0:col0 + ncols],
                                                   start=(c == 0), stop=(c == NKC - 1)), r=[hT.r, w.r], w=[pf.r])

        def rope(src_ap_fn, dst_ap_fn, cs_ap, sn_ap, tmps, rsrc, rdst, rcs):
            a, b_ = tmps
            S.op("dve", lambda e: e.tensor_tensor(out=a.t[:], in0=src_ap_fn(0, 32), in1=cs_ap, op=ALU.mult),
                 r=[rsrc, rcs], w=[a.r])
            S.op("dve", lambda e: e.tensor_tensor(out=b_.t[:], in0=src_ap_fn(32, 64), in1=sn_ap, op=ALU.mult),
                 r=[rsrc, rcs], w=[b_.r])
            S.op("dve", lambda e: e.tensor_tensor(out=dst_ap_fn(0, 32), in0=a.t[:], in1=b_.t[:], op=ALU.subtract),
                 r=[a.r, b_.r], w=[rdst])
            S.op("dve", lambda e: e.tensor_tensor(out=a.t[:], in0=src_ap_fn(0, 32), in1=sn_ap, op=ALU.mult),
                 r=[rsrc, rcs], w=[a.r])
            S.op("dve", lambda e: e.tensor_tensor(out=b_.t[:], in0=src_ap_fn(32, 64), in1=cs_ap, op=ALU.mult),
                 r=[rsrc, rcs], w=[b_.r])
            S.op("dve", lambda e: e.tensor_tensor(out=dst_ap_fn(32, 64), in0=a.t[:], in1=b_.t[:], op=ALU.add),
                 r=[a.r, b_.r], w=[rdst])

        S.flush()

        with ExitStack() as P:
            cs_s, sn_s = cos_sin(P, "cs_seq", pos_seq, NS)
            wkv = load_w(P, "wkv", w_kv, NKC, 320)
            gckv = sb(P, "gckv", [128, 256])
            S.dma("sp", lambda e: e.dma_start(out=gckv.t[:], in_=g_ckv_bc), w=[gckv.r])
            nctx = NormCtx(P, "n1", gmix)
            hT = [sb(P, f"hT1_{i}", [128, NKC, 128], BF16) for i in range(2)]
            ss2 = sb(P, "ss2", [128, 1])
            rstd2 = sb(P, "rstd2", [128, 1])
            junk2 = sb(P, "junk2", [128, 256], BF16)
            ckvn = sb(P, "ckvn", [128, 256], BF16)
            krs = sb(P, "krs", [128, 64])
            kro = sb(P, "kro", [128, 64], BF16)
            ta = sb(P, "ropa", [128, 32])
            tb = sb(P, "ropb", [128, 32])
            for t in range(NS):
                h = hT[t % 2]
                nctx.run(x_seq[t * 128:(t + 1) * 128, :], h, lambda c, h=h: h.t[:, c, :])
                pf = PF[t % 2]
                proj(h, lambda c, h=h: h.t[:, c, :], wkv, 0, 320, pf)
                S.op("act", lambda e, pf=pf: e.activation(out=junk2.t[:], in_=pf.t[:, 0:256], func=AF.Square,
                                                          accum_out=ss2.t[:]), r=[pf.r], w=[junk2.r, ss2.r])
                rstd_from_ss(ss2, rstd2, 256)
                S.op("dve", lambda e, pf=pf: e.scalar_tensor_tensor(out=ckvn.t[:], in0=pf.t[:, 0:256],
                                                                    scalar=rstd2.t[:, 0:1], in1=gckv.t[:],
                                                                    op0=ALU.mult, op1=ALU.mult),
                     r=[pf.r, rstd2.r, gckv.r], w=[ckvn.r])
                S.op("act", lambda e, pf=pf: e.activation(out=krs.t[:], in_=pf.t[:, 256:320], func=AF.Copy),
                     r=[pf.r], w=[krs.r])
                rope(lambda lo, hi: krs.t[:, lo:hi], lambda lo, hi: kro.t[:, lo:hi],
                     cs_s.t[:, t, :], sn_s.t[:, t, :], (ta, tb), krs.r, kro.r, cs_s.r)
                pb = PB[t % 2]
                for cc in range(2):
                    S.op("pe", lambda e, cc=cc, pb=pb: e.transpose(out=pb.t[:, cc * 128:(cc + 1) * 128],
                                                                   in_=ckvn.t[:, cc * 128:(cc + 1) * 128],
                                                                   identity=ident.t[:]),
                         r=[ckvn.r, ident.r], w=[pb.r])
                S.op("pe", lambda e, pb=pb: e.transpose(out=pb.t[0:64, 256:384], in_=kro.t[:, 0:64],
                                                        identity=ident.t[:]), r=[kro.r, ident.r], w=[pb.r])
                S.op("act", lambda e, pb=pb, t=t: e.activation(
                    out=ckvT.t[:, :, t * 128:(t + 1) * 128],
                    in_=pb.t[:, 0:256].rearrange("p (c k) -> p c k", c=2), func=AF.Copy),
                    r=[pb.r], w=[r_ckvT[t]])
                S.op("dve", lambda e, pb=pb, t=t: e.tensor_copy(out=kropeT.t[0:64, t * 128:(t + 1) * 128],
                                                                in_=pb.t[0:64, 256:384]),
                     r=[pb.r, kropeT.r], w=[r_ckvT[t]])
            S.flush()

        with ExitStack() as P:
            cs_o, sn_o = cos_sin(P, "cs_own", pos_own, NJ)
            kv_prev = sb(P, "kv_prev", [128, NJ, 512], BF16)
            kv_own = sb(P, "kv_own", [128, NJ, 512], BF16)
            r_kvp = [Res(f"kvp{j}") for j in range(NJ)]
            r_kvo = [Res(f"kvo{j}") for j in range(NJ)]
            nctx = NormCtx(P, "n2", gmix)
            hT = [sb(P, f"hT2_{i}", [128, NKC, 128], BF16) for i in range(2)]
            with ExitStack() as P2:
                wkva = load_w(P2, "wkva", w_kva, NKC, 512)
                wcq = load_w(P2, "wcq", w_cq, NKC, 512)
                wuq = load_w(P2, "wuq", w_uq, 4, 1536)
                gcq = sb(P2, "gcq", [128, 512])
                S.dma("sp", lambda e: e.dma_start(out=gcq.t[:], in_=g_cq_bc), w=[gcq.r])
                for j in range(NJ):
                    h = hT[j % 2]
                    nctx.run(x_prev[j * 128:(j + 1) * 128, :], h, lambda c, h=h: h.t[:, c, :])
                    pf = PF[j % 2]
                    proj(h, lambda c, h=h: h.t[:, c, :], wkva, 0, 512, pf)
                    S.op("act", lambda e, pf=pf, j=j: e.activation(out=kv_prev.t[:, j, :], in_=pf.t[:, 0:512],
                                                                  func=AF.Copy), r=[pf.r], w=[r_kvp[j]])
                ss3 = sb(P2, "ss3", [128, 1])
                rstd3 = sb(P2, "rstd3", [128, 1])
                junk3 = sb(P2, "junk3", [128, 512], BF16)
                cqn = sb(P2, "cqn", [128, 512], BF16)
                cqT = sb(P2, "cqT", [128, 4, 128], BF16)
                qf = sb(P2, "qf", [128, 1536])
                qb = sb(P2, "qb", [128, 1536], BF16)
                ta = sb(P2, "ropa2", [128, 32])
                tb = sb(P2, "ropb2", [128, 32])
                qnT = [sb(P2, f"qnT{i}", [128, 8, 128], BF16) for i in range(2)]
                qrT = [sb(P2, f"qrT{i}", [64, 8, 128], BF16) for i in range(2)]
                for j in range(NJ):
                    h = hT[j % 2]
                    nctx.run(x_own[j * 128:(j + 1) * 128, :], h, lambda c, h=h: h.t[:, c, :])
                    hs = (lambda c, h=h: h.t[:, c, :])
                    proj(h, hs, wkva, 0, 512, PF[0])
                    S.op("act", lambda e, j=j: e.activation(out=kv_own.t[:, j, :], in_=PF[0].t[:, 0:512],
                                                           func=AF.Copy), r=[PF[0].r], w=[r_kvo[j]])
                    proj(h, hs, wcq, 0, 512, PF[1])
                    S.op("act", lambda e: e.activation(out=junk3.t[:], in_=PF[1].t[:, 0:512], func=AF.Square,
                                                       accum_out=ss3.t[:]), r=[PF[1].r], w=[junk3.r, ss3.r])
                    rstd_from_ss(ss3, rstd3, 512)
                    S.op("dve", lambda e: e.scalar_tensor_tensor(out=cqn.t[:], in0=PF[1].t[:, 0:512],
                                                                 scalar=rstd3.t[:, 0:1], in1=gcq.t[:],
                                                                 op0=ALU.mult, op1=ALU.mult),
                         r=[PF[1].r, rstd3.r, gcq.r], w=[cqn.r])
                    for cc in range(4):
                        S.op("pe", lambda e, cc=cc: e.transpose(out=PB[0].t[:, cc * 128:(cc + 1) * 128],
                                                                in_=cqn.t[:, cc * 128:(cc + 1) * 128],
                                                                identity=ident.t[:]),
                             r=[cqn.r, ident.r], w=[PB[0].r])
                    S.op("dve", lambda e: e.tensor_copy(out=cqT.t[:],
                                                        in_=PB[0].t[:, 0:512].rearrange("p (c k) -> p c k", c=4)),
                         r=[PB[0].r], w=[cqT.r])
                    for g3 in range(3):
                        pf = PF[2 + g3]
                        for cc in range(4):
                            S.op("pe", lambda e, cc=cc, g3=g3, pf=pf: e.matmul(
                                pf.t[:, 0:512], lhsT=cqT.t[:, cc, :], rhs=wuq.t[:, cc, g3 * 512:(g3 + 1) * 512],
                                start=(cc == 0), stop=(cc == 3)), r=[cqT.r, wuq.r], w=[pf.r])
                        S.op("act", lambda e, g3=g3, pf=pf: e.activation(out=qf.t[:, g3 * 512:(g3 + 1) * 512],
                                                                        in_=pf.t[:, 0:512], func=AF.Copy),
                             r=[pf.r], w=[qf.r])
                    S.op("act", lambda e: e.activation(out=qb.t[:], in_=qf.t[:], func=AF.Copy), r=[qf.r], w=[qb.r])
                    for hh in range(8):
                        base = hh * 192 + 128
                        rope(lambda lo, hi, base=base: qf.t[:, base + lo:base + hi],
                             lambda lo, hi, base=base: qb.t[:, base + lo:base + hi],
                             cs_o.t[:, j, :], sn_o.t[:, j, :], (ta, tb), qf.r, qb.r, cs_o.r)
                    qn = qnT[j % 2]
                    qr = qrT[j % 2]
                    for hh in range(8):
                        S.op("pe", lambda e, hh=hh: e.transpose(out=PB[1].t[:, hh * 128:(hh + 1) * 128],
                                                                in_=qb.t[:, hh * 192:hh * 192 + 128],
                                                                identity=ident.t[:]),
                             r=[qb.r, ident.r], w=[PB[1].r])
                    S.op("dve", lambda e, qn=qn: e.tensor_copy(
                        out=qn.t[:], in_=PB[1].t[:, 0:1024].rearrange("p (c k) -> p c k", c=8)),
                        r=[PB[1].r], w=[qn.r])
                    for hh in range(8):
                        S.op("pe", lambda e, hh=hh: e.transpose(out=PB[0].t[0:64, hh * 128:(hh + 1) * 128],
                                                                in_=qb.t[:, hh * 192 + 128:hh * 192 + 192],
                                                                identity=ident.t[:]),
                             r=[qb.r, ident.r], w=[PB[0].r])
                    S.op("act", lambda e, qr=qr: e.activation(
                        out=qr.t[:], in_=PB[0].t[0:64, 0:1024].rearrange("p (c k) -> p c k", c=8), func=AF.Copy),
                        r=[PB[0].r], w=[qr.r])
                    S.dma("sp", lambda e, qn=qn, j=j: e.dma_start(
                        out=qnT_d[:, :, j * 128:(j + 1) * 128].rearrange("h p k -> p h k"), in_=qn.t[:]),
                        r=[qn.r], w=[r_qT])
                    S.dma("sp", lambda e, qr=qr, j=j: e.dma_start(
                        out=qrT_d[:, :, j * 128:(j + 1) * 128].rearrange("h p k -> p h k"), in_=qr.t[:]),
                        r=[qr.r], w=[r_qT])
                S.flush()

            with ExitStack() as P3:
                wqa = load_w(P3, "wqa", w_qa, NKC, 1024)
                sinks = sb(P3, "sinks", [128, 16])
                S.dma("sp", lambda e: e.dma_start(out=sinks.t[:], in_=sinks_bc), w=[sinks.r])
                maskb = sb(P3, "maskb", [128, 2, 256])
                S.dma("sp", lambda e: e.dma_start(out=maskb.t[:], in_=swa_maskb.rearrange("a p k -> p a k")),
                      w=[maskb.r])
                posq_i = sb(P3, "posq_i", [128, NJ], I32)
                posq = sb(P3, "posq", [128, NJ])
                S.dma("sp", lambda e: e.dma_start(out=posq_i.t[:], in_=pos_own), w=[posq_i.r])
                S.op("dve", lambda e: e.tensor_copy(out=posq.t[:], in_=posq_i.t[:]), r=[posq_i.r], w=[posq.r])
                posk_i = sb(P3, "posk_i", [128, 256], I32)
                dist = sb(P3, "dist", [128, 256])
                bias = sb(P3, "bias", [128, 16, 256])
                qa = sb(P3, "qa", [128, 1024], BF16)
                qaT = sb(P3, "qaT", [64, 16, 128], BF16)
                kaT = sb(P3, "kaT", [64, 4, 256], BF16)
                L = sb(P3, "swaL", [128, 4, 256])
                mx = sb(P3, "swamx", [128, 4])
                dd = sb(P3, "swadd", [128, 4])
                rsum = sb(P3, "swarsum", [128, 4])
                den = sb(P3, "swaden", [128, 4])
                pexp = sb(P3, "swap", [128, 4, 256], BF16)
                pT = sb(P3, "swapT", [128, 8, 128], BF16)
                so = [sb(P3, f"swao{i}", [128, 1024], BF16) for i in range(2)]
                for j in range(NJ):
                    h = hT[j % 2]
                    hs = (lambda c, h=h: h.t[:, c, :])
                    nctx.run(x_own[j * 128:(j + 1) * 128, :], h, hs)
                    for g2 in range(2):
                        proj(h, hs, wqa, g2 * 512, 512, PF[g2])
                        S.op("act", lambda e, g2=g2: e.activation(out=qa.t[:, g2 * 512:(g2 + 1) * 512],
                                                                 in_=PF[g2].t[:, 0:512], func=AF.Copy),
                             r=[PF[g2].r], w=[qa.r])
                    for half in range(2):
                        for hh in range(8):
                            hd = half * 8 + hh
                            S.op("pe", lambda e, hd=hd, hh=hh, half=half: e.transpose(
                                out=PB[half].t[0:64, hh * 128:(hh + 1) * 128], in_=qa.t[:, hd * 64:(hd + 1) * 64],
                                identity=ident.t[:]), r=[qa.r, ident.r], w=[PB[half].r])
                        S.op("dve", lambda e, half=half: e.tensor_copy(
                            out=qaT.t[:, half * 8:(half + 1) * 8, :],
                            in_=PB[half].t[0:64, 0:1024].rearrange("p (c k) -> p c k", c=8)),
                            r=[PB[half].r], w=[qaT.r])
                    for kvh in range(4):
                        S.op("pe", lambda e, kvh=kvh, j=j: e.transpose(
                            out=PB[0].t[0:64, kvh * 256:kvh * 256 + 128], in_=kv_prev.t[:, j, kvh * 64:(kvh + 1) * 64],
                            identity=ident.t[:]), r=[r_kvp[j], ident.r], w=[PB[0].r])
                        S.op("pe", lambda e, kvh=kvh, j=j: e.transpose(
                            out=PB[0].t[0:64, kvh * 256 + 128:kvh * 256 + 256],
                            in_=kv_own.t[:, j, kvh * 64:(kvh + 1) * 64],
                            identity=ident.t[:]), r=[r_kvo[j], ident.r], w=[PB[0].r])
                    S.op("dve", lambda e: e.tensor_copy(
                        out=kaT.t[:], in_=PB[0].t[0:64, 0:1024].rearrange("p (c k) -> p c k", c=4)),
                        r=[PB[0].r], w=[kaT.r])
                    S.dma("sp", lambda e, j=j: e.dma_start(out=posk_i.t[:], in_=pos_kv_bc[j]), w=[posk_i.r])
                    S.op("dve", lambda e: e.tensor_copy(out=dist.t[:], in_=posk_i.t[:]), r=[posk_i.r], w=[dist.r])
                    S.op("dve", lambda e, j=j: e.tensor_scalar(out=dist.t[:], in0=dist.t[:], scalar1=posq.t[:, j:j + 1],
                                                               scalar2=None, op0=ALU.subtract),
                         r=[dist.r, posq.r], w=[dist.r])
                    S.op("dve", lambda e: e.scalar_tensor_tensor(out=dist.t[:], in0=dist.t[:], scalar=-1.0,
                                                                 in1=dist.t[:], op0=ALU.mult, op1=ALU.max),
                         r=[dist.r], w=[dist.r])
                    mi = 0 if j == 0 else 1
                    for hd in range(16):
                        S.op("pool" if hd % 2 else "dve", lambda e, hd=hd: e.tensor_scalar(
                            out=bias.t[:, hd, :], in0=dist.t[:], scalar1=-slopes[hd], scalar2=None, op0=ALU.mult),
                            r=[dist.r], w=[bias.r])
                    S.op("dve", lambda e, mi=mi: e.tensor_tensor(
                        out=bias.t[:], in0=bias.t[:], in1=maskb.t[:, mi, :].unsqueeze(1).to_broadcast([128, 16, 256]),
                        op=ALU.add), r=[bias.r, maskb.r], w=[bias.r])
                    sot = so[j % 2]
                    for kvh in range(4):
                        for gq in range(4):
                            hd = kvh * 4 + gq
                            pfl = PF[gq // 2]
                            S.op("pe", lambda e, hd=hd, gq=gq, kvh=kvh, pfl=pfl: e.matmul(
                                pfl.t[:, (gq % 2) * 256:(gq % 2) * 256 + 256], lhsT=qaT.t[:, hd, :],
                                rhs=kaT.t[:, kvh, :], start=True, stop=True), r=[qaT.r, kaT.r], w=[pfl.r])
                        for half in range(2):
                            S.op("dve", lambda e, half=half, kvh=kvh: e.scalar_tensor_tensor(
                                out=L.t[:, half * 2:half * 2 + 2, :],
                                in0=PF[half].t[:, 0:512].rearrange("p (a k) -> p a k", a=2), scalar=0.125,
                                in1=bias.t[:, kvh * 4 + half * 2:kvh * 4 + half * 2 + 2, :],
                                op0=ALU.mult, op1=ALU.add), r=[PF[half].r, bias.r], w=[L.r])
                        S.op("dve", lambda e: e.tensor_reduce(out=mx.t[:], in_=L.t[:], axis=AX.X, op=ALU.max),
                             r=[L.r], w=[mx.r])
                        S.op("dve", lambda e, kvh=kvh: e.tensor_tensor(out=mx.t[:], in0=mx.t[:],
                                                                       in1=sinks.t[:, kvh * 4:kvh * 4 + 4], op=ALU.max),
                             r=[mx.r, sinks.r], w=[mx.r])
                        S.op("dve", lambda e, kvh=kvh: e.tensor_tensor(out=dd.t[:], in0=sinks.t[:, kvh * 4:kvh * 4 + 4],
                                                                       in1=mx.t[:], op=ALU.subtract),
                             r=[mx.r, sinks.r], w=[dd.r])
                        S.op("dve", lambda e: e.tensor_scalar(out=mx.t[:], in0=mx.t[:], scalar1=-1.0, scalar2=None,
                                                              op0=ALU.mult), r=[mx.r], w=[mx.r])
                        for gq in range(4):
                            S.op("act", lambda e, gq=gq: e.activation(out=pexp.t[:, gq, :], in_=L.t[:, gq, :],
                                                                      func=AF.Exp, bias=mx.t[:, gq:gq + 1], scale=1.0,
                                                                      accum_out=rsum.t[:, gq:gq + 1]),
                                 r=[L.r, mx.r], w=[pexp.r, rsum.r])
                        S.op("act", lambda e: e.activation(out=dd.t[:], in_=dd.t[:], func=AF.Exp), r=[dd.r], w=[dd.r])
                        S.op("dve", lambda e: e.tensor_tensor(out=den.t[:], in0=rsum.t[:], in1=dd.t[:], op=ALU.add),
                             r=[rsum.r, dd.r], w=[den.r])
                        S.op("dve", lambda e: e.reciprocal(out=den.t[:], in_=den.t[:]), r=[den.r], w=[den.r])
                        for gq in range(4):
                            for kb in range(2):
                                S.op("pe", lambda e, gq=gq, kb=kb: e.transpose(
                                    out=PB[1].t[:, (gq * 2 + kb) * 128:(gq * 2 + kb + 1) * 128],
                                    in_=pexp.t[:, gq, kb * 128:(kb + 1) * 128], identity=ident.t[:]),
                                    r=[pexp.r, ident.r], w=[PB[1].r])
                        S.op("act", lambda e: e.activation(
                            out=pT.t[:], in_=PB[1].t[:, 0:1024].rearrange("p (c k) -> p c k", c=8), func=AF.Copy),
                            r=[PB[1].r], w=[pT.r])
                        po = PF[2 + kvh % 2]
                        for gq in range(4):
                            S.op("pe", lambda e, gq=gq, kvh=kvh, j=j, po=po: e.matmul(
                                po.t[:, gq * 64:(gq + 1) * 64], lhsT=pT.t[:, gq * 2, :],
                                rhs=kv_prev.t[:, j, 256 + kvh * 64:256 + (kvh + 1) * 64], start=True, stop=False),
                                r=[pT.r, r_kvp[j]], w=[po.r])
                            S.op("pe", lambda e, gq=gq, kvh=kvh, j=j, po=po: e.matmul(
                                po.t[:, gq * 64:(gq + 1) * 64], lhsT=pT.t[:, gq * 2 + 1, :],
                                rhs=kv_own.t[:, j, 256 + kvh * 64:256 + (kvh + 1) * 64], start=False, stop=True),
                                r=[pT.r, r_kvo[j]], w=[po.r])
                        for gq in range(4):
                            hd = kvh * 4 + gq
                            S.op("dve", lambda e, gq=gq, hd=hd, po=po, sot=sot: e.tensor_scalar(
                                out=sot.t[:, hd * 64:(hd + 1) * 64], in0=po.t[:, gq * 64:(gq + 1) * 64],
                                scalar1=den.t[:, gq:gq + 1], scalar2=None, op0=ALU.mult),
                                r=[po.r, den.r], w=[sot.r])
                    S.dma("sp", lambda e, j=j, sot=sot: e.dma_start(out=swa_d[j * 128:(j + 1) * 128, :], in_=sot.t[:]),
                          r=[sot.r], w=[r_swa[j]])
                S.flush()

        with ExitStack() as P:
            nctx = NormCtx(P, "n3", gmix)
            hTo = sb(P, "hTo", [128, NKC, TK], BF16)
            r_hTo = [Res(f"hTo{j}") for j in range(NJ)]
            for j in range(NJ):
                hj = Tl(hTo.t, "x")
                hj.r = r_hTo[j]
                nctx.run(x_own[j * 128:(j + 1) * 128, :], hj, lambda c, j=j: hTo.t[:, c, j * 128:(j + 1) * 128])
            wg = [sb(P, f"wg{i}", [128, NKC, 512], BF16) for i in range(2)]
            stg = [sb(P, f"stg{i}", [128, 512]) for i in range(3)]
            si = 0
            for g8 in range(8):
                w = wg[g8 % 2]
                srcw = w_gate[:, g8 * 512:(g8 + 1) * 512].rearrange("(c p) n -> p c n", p=128)
                for c0 in range(0, NKC, 8):
                    S.dma("pool", lambda e, w=w, srcw=srcw, c0=c0: e.dma_start(out=w.t[:, c0:c0 + 8, :],
                                                                              in_=srcw[:, c0:c0 + 8, :]), w=[w.r])
                for j in range(NJ):
                    pf = PF[(g8 * NJ + j) % 4]
                    hj = Tl(hTo.t, "x")
                    hj.r = r_hTo[j]
                    proj(hj, lambda c, j=j: hTo.t[:, c, j * 128:(j + 1) * 128], w, 0, 512, pf)
                    st = stg[si % 3]
                    si += 1
                    if si % 2:
                        S.op("act", lambda e, st=st, pf=pf: e.activation(out=st.t[:], in_=pf.t[:, 0:512], func=AF.Copy),
                             r=[pf.r], w=[st.r])
                    else:
                        S.op("dve", lambda e, st=st, pf=pf: e.tensor_copy(out=st.t[:], in_=pf.t[:, 0:512]),
                             r=[pf.r], w=[st.r])
                    S.dma("sp", lambda e, st=st, j=j, g8=g8: e.dma_start(
                        out=zg_d[j * 128:(j + 1) * 128, g8 * 512:(g8 + 1) * 512], in_=st.t[:]),
                        r=[st.r], w=[r_zg[j]])
            S.flush()

        with ExitStack() as P:
            wk = load_w(P, "wukvk", w_ukv_k, 2, 1024)
            wv = load_w(P, "wukvv", w_ukv_v, 2, 1024)
            mmask = sb(P, "mmask", [128, 512])
            S.dma("sp", lambda e: e.dma_start(out=mmask.t[:], in_=mla_mask), w=[mmask.r])
            for tab, c0 in ((u_tab, 0), (v_tab, D)):
                for i in range(16):
                    S.dma("pool", lambda e, tab=tab, c0=c0, i=i: e.dma_start(
                        out=uv_b[i * 1024:(i + 1) * 1024, c0:c0 + D], in_=tab[i * 1024:(i + 1) * 1024, :]),
                        w=[r_uvb])
            knT = sb(P, "knT", [128, 2, SQ], BF16)
            Vsb = sb(P, "Vsb", [128, NS, 256], BF16)
            qn2 = sb(P, "qn2", [128, 2, TK], BF16)
            qr2 = sb(P, "qr2", [128, 2, TK], BF16)
            qr2z = Res("qr2z")
            S.op("dve", lambda e: e.memset(qr2.t[64:128, :, :], 0.0), w=[qr2z])
            SETS = []
            for bs in range(2):
                SETS.append(dict(
                    S1=PF[0 + 3 * bs], S2=PF[1 + 3 * bs], O=PF[2 + 3 * bs], PT=PB[bs],
                    rmax=sb(P, f"rmax{bs}", [128, NJ]), rsm=sb(P, f"rsm{bs}", [128, NJ]),
                    mrow=sb(P, f"mrow{bs}", [128, 1]), lrow=sb(P, f"lrow{bs}", [128, 1]),
                    Pb=[sb(P, f"Pb{bs}_{i}", [128, 512], BF16) for i in range(2)],
                    Pf=sb(P, f"Pf{bs}", [128, 512]),
                    PTs=[sb(P, f"PTs{bs}_{i}", [128, 4, 128], BF16) for i in range(2)]))
            mo = [sb(P, f"mo{i}", [128, 256], BF16) for i in range(2)]
            allk = list(r_ckvT)

            def interleave(gens):
                gens = list(gens)
                while gens:
                    for g in list(gens):
                        try:
                            next(g)
                        except StopIteration:
                            gens.remove(g)

            def mla_unit(j, hh, B, mot):
                qs = slice(j * 128, (j + 1) * 128)
                rmax, rsm, mrow, lrow = B["rmax"], B["rsm"], B["mrow"], B["lrow"]

                def scores(c, pf):
                    S.op("pe", lambda e: e.matmul(pf.t[:, 0:512], lhsT=qn2.t[:, hh, qs],
                                                  rhs=knT.t[:, hh, c * 512:(c + 1) * 512], start=True,
                                                  stop=False), r=[qn2.r, knT.r], w=[pf.r])
                    S.op("pe", lambda e: e.matmul(pf.t[:, 0:512], lhsT=qr2.t[:, hh, qs],
                                                  rhs=kropeT.t[:, c * 512:(c + 1) * 512], start=False,
                                                  stop=True), r=[qr2.r, qr2z, kropeT.r] + allk[c * 4:c * 4 + 4], w=[pf.r])
                for c in range(j + 1):
                    pf = B["S1"]
                    scores(c, pf)
                    yield
                    S.op("dve", lambda e, c=c, pf=pf: e.tensor_reduce(out=rmax.t[:, c:c + 1], in_=pf.t[:, 0:512],
                                                                     axis=AX.X, op=ALU.max), r=[pf.r], w=[rmax.r])
                S.op("dve", lambda e: e.tensor_reduce(out=mrow.t[:], in_=rmax.t[:, 0:j + 1], axis=AX.X,
                                                      op=ALU.max), r=[rmax.r], w=[mrow.r])
                S.op("dve", lambda e: e.tensor_scalar(out=mrow.t[:], in0=mrow.t[:], scalar1=-MLA_SCALE,
                                                      scalar2=None, op0=ALU.mult), r=[mrow.r], w=[mrow.r])
                po = B["O"]
                for c in range(j + 1):
                    pf = B["S2"]
                    scores(c, pf)
                    yield
                    pbt = B["Pb"][c % 2]
                    if c < j:
                        S.op("act", lambda e, c=c, pf=pf, pbt=pbt: e.activation(
                            out=pbt.t[:], in_=pf.t[:, 0:512], func=AF.Exp, bias=mrow.t[:, 0:1],
                            scale=MLA_SCALE, accum_out=rsm.t[:, c:c + 1]),
                            r=[pf.r, mrow.r], w=[pbt.r, rsm.r])
                    else:
                        Pf = B["Pf"]
                        S.op("act", lambda e, pf=pf, Pf=Pf: e.activation(
                            out=Pf.t[:], in_=pf.t[:, 0:512], func=AF.Exp, bias=mrow.t[:, 0:1],
                            scale=MLA_SCALE), r=[pf.r, mrow.r], w=[Pf.r])
                        S.op("dve", lambda e, Pf=Pf: e.tensor_tensor(out=Pf.t[:], in0=Pf.t[:], in1=mmask.t[:],
                                                                     op=ALU.mult), r=[Pf.r, mmask.r], w=[Pf.r])
                        S.op("dve", lambda e, c=c, Pf=Pf: e.tensor_reduce(out=rsm.t[:, c:c + 1], in_=Pf.t[:],
                                                                          axis=AX.X, op=ALU.add),
                             r=[Pf.r], w=[rsm.r])
                        S.op("act", lambda e, pbt=pbt, Pf=Pf: e.activation(out=pbt.t[:], in_=Pf.t[:], func=AF.Copy),
                             r=[Pf.r], w=[pbt.r])
                    ptp = B["PT"]
                    for kk in range(4):
                        S.op("pe", lambda e, kk=kk, pbt=pbt, ptp=ptp: e.transpose(
                            out=ptp.t[:, kk * 128:(kk + 1) * 128], in_=pbt.t[:, kk * 128:(kk + 1) * 128],
                            identity=ident.t[:]), r=[pbt.r, ident.r], w=[ptp.r])
                    yield
                    pts = B["PTs"][c % 2]
                    if c % 2:
                        S.op("act", lambda e, pts=pts, ptp=ptp: e.activation(
                            out=pts.t[:], in_=ptp.t[:, 0:512].rearrange("p (c k) -> p c k", c=4), func=AF.Copy),
                            r=[ptp.r], w=[pts.r])
                    else:
                        S.op("dve", lambda e, pts=pts, ptp=ptp: e.tensor_copy(
                            out=pts.t[:], in_=ptp.t[:, 0:512].rearrange("p (c k) -> p c k", c=4)),
                            r=[ptp.r], w=[pts.r])
                    for kk in range(4):
                        S.op("pe", lambda e, kk=kk, c=c, pts=pts: e.matmul(
                            po.t[:, 0:128], lhsT=pts.t[:, kk, :],
                            rhs=Vsb.t[:, c * 4 + kk, hh * 128:(hh + 1) * 128],
                            start=(c == 0 and kk == 0), stop=(c == j and kk == 3)),
                            r=[pts.r, Vsb.r], w=[po.r])
                    yield
                S.op("dve", lambda e: e.tensor_reduce(out=lrow.t[:], in_=rsm.t[:, 0:j + 1], axis=AX.X,
                                                      op=ALU.add), r=[rsm.r], w=[lrow.r])
                S.op("dve", lambda e: e.reciprocal(out=lrow.t[:], in_=lrow.t[:]), r=[lrow.r], w=[lrow.r])
                S.op("dve", lambda e: e.tensor_scalar(
                    out=mot.t[:, hh * 128:(hh + 1) * 128], in0=po.t[:, 0:128], scalar1=lrow.t[:, 0:1],
                    scalar2=None, op0=ALU.mult), r=[po.r, lrow.r], w=[mot.r])

            for hp in range(4):
                for hh in range(2):
                    hd = hp * 2 + hh
                    for kc in range(NS // 4):
                        pf = PF[(kc % 2) * 3 + 1]
                        for cc in range(2):
                            S.op("pe", lambda e, cc=cc, hd=hd, kc=kc, pf=pf: e.matmul(
                                pf.t[:, 0:512], lhsT=wk.t[:, cc, hd * 128:(hd + 1) * 128],
                                rhs=ckvT.t[:, cc, kc * 512:(kc + 1) * 512], start=(cc == 0), stop=(cc == 1)),
                                r=[wk.r] + allk[kc * 4:kc * 4 + 4], w=[pf.r])
                        if kc % 2:
                            S.op("act", lambda e, hh=hh, kc=kc, pf=pf: e.activation(
                                out=knT.t[:, hh, kc * 512:(kc + 1) * 512], in_=pf.t[:, 0:512], func=AF.Copy),
                                r=[pf.r], w=[knT.r])
                        else:
                            S.op("dve", lambda e, hh=hh, kc=kc, pf=pf: e.tensor_copy(
                                out=knT.t[:, hh, kc * 512:(kc + 1) * 512], in_=pf.t[:, 0:512]),
                                r=[pf.r], w=[knT.r])
                for kt in range(NS):
                    pf = PF[(kt % 2) * 3 + 2]
                    for cc in range(2):
                        S.op("pe", lambda e, cc=cc, kt=kt, pf=pf, hp=hp: e.matmul(
                            pf.t[:, 0:256], lhsT=ckvT.t[:, cc, kt * 128:(kt + 1) * 128],
                            rhs=wv.t[:, cc, hp * 256:(hp + 1) * 256], start=(cc == 0), stop=(cc == 1)),
                            r=[wv.r, allk[kt]], w=[pf.r])
                    if kt % 2:
                        S.op("act", lambda e, kt=kt, pf=pf: e.activation(
                            out=Vsb.t[:, kt, :], in_=pf.t[:, 0:256], func=AF.Copy), r=[pf.r], w=[Vsb.r])
                    else:
                        S.op("dve", lambda e, kt=kt, pf=pf: e.tensor_copy(out=Vsb.t[:, kt, :], in_=pf.t[:, 0:256]),
                             r=[pf.r], w=[Vsb.r])
                S.dma("sp", lambda e, hp=hp: e.dma_start(out=qn2.t[:], in_=qnT_d[hp * 2:hp * 2 + 2].rearrange(
                    "h p k -> p h k")), r=[r_qT], w=[qn2.r])
                S.dma("sp", lambda e, hp=hp: e.dma_start(out=qr2.t[0:64, :, :], in_=qrT_d[hp * 2:hp * 2 + 2].rearrange(
                    "h p k -> p h k")), r=[r_qT, qr2z], w=[qr2.r])
                for j in range(NJ):
                    mot = mo[j % 2]
                    interleave([mla_unit(j, 0, SETS[0], mot), mla_unit(j, 1, SETS[1], mot)])
                    S.dma("sp", lambda e, j=j, hp=hp, mot=mot: e.dma_start(
                        out=mla_d[j * 128:(j + 1) * 128, hp * 256:(hp + 1) * 256], in_=mot.t[:]),
                        r=[mot.r], w=[r_mla[j]])
                S.flush()

        GK.close()
        with ExitStack() as P:
            r_mixd = [Res(f"mixd{j}") for j in range(NJ)]
            with ExitStack() as P5:
                wa = load_w(P5, "wa", w_a, 8, D)
                wb = load_w(P5, "wb", w_b, 8, D)
                at = [sb(P5, f"at{i}", [128, 1024], BF16) for i in range(2)]
                bt = [sb(P5, f"bt{i}", [128, 1024], BF16) for i in range(2)]
                aT = sb(P5, "aT", [128, 8, 128], BF16)
                bT = sb(P5, "bT", [128, 8, 128], BF16)
                gt = [sb(P5, f"gt{i}", [128, 4096]) for i in range(2)]
                t1 = sb(P5, "mix_t1", [128, 512])
                mixeds = [sb(P5, f"mixed{i}", [128, D], BF16) for i in range(2)]
                for j in range(NJ):
                    mixed = mixeds[j % 2]
                    a_, b_, g_ = at[j % 2], bt[j % 2], gt[j % 2]
                    S.dma("sp", lambda e, j=j, a_=a_: e.dma_start(out=a_.t[:], in_=swa_d[j * 128:(j + 1) * 128, :]),
                          r=[r_swa[j]], w=[a_.r])
                    S.dma("sp", lambda e, j=j, b_=b_: e.dma_start(out=b_.t[:], in_=mla_d[j * 128:(j + 1) * 128, :]),
                          r=[r_mla[j]], w=[b_.r])
                    S.dma("sp", lambda e, j=j, g_=g_: e.dma_start(out=g_.t[:], in_=zg_d[j * 128:(j + 1) * 128, :]),
                          r=[r_zg[j]], w=[g_.r])
                    S.op("act", lambda e, g_=g_: e.activation(out=g_.t[:], in_=g_.t[:], func=AF.Sigmoid),
                         r=[g_.r], w=[g_.r])
                    for src, dstT, pb in ((a_, aT, PB[0]), (b_, bT, PB[1])):
                        for cc in range(8):
                            S.op("pe", lambda e, cc=cc, src=src, pb=pb: e.transpose(
                                out=pb.t[:, cc * 128:(cc + 1) * 128], in_=src.t[:, cc * 128:(cc + 1) * 128],
                                identity=ident.t[:]), r=[src.r, ident.r], w=[pb.r])
                        S.op("dve", lambda e, dstT=dstT, pb=pb: e.tensor_copy(
                            out=dstT.t[:], in_=pb.t[:, 0:1024].rearrange("p (c k) -> p c k", c=8)),
                            r=[pb.r], w=[dstT.r])
                    for g4 in range(4):
                        pa, pbb = PF[(g4 % 2) * 2], PF[(g4 % 2) * 2 + 1]
                        for cc in range(8):
                            S.op("pe", lambda e, cc=cc, g4=g4, pa=pa: e.matmul(
                                pa.t[:, 0:512], lhsT=aT.t[:, cc, :], rhs=wa.t[:, cc, g4 * 512:(g4 + 1) * 512],
                                start=(cc == 0), stop=(cc == 7)), r=[aT.r, wa.r], w=[pa.r])
                        for cc in range(8):
                            S.op("pe", lambda e, cc=cc, g4=g4, pbb=pbb: e.matmul(
                                pbb.t[:, 0:512], lhsT=bT.t[:, cc, :], rhs=wb.t[:, cc, g4 * 512:(g4 + 1) * 512],
                                start=(cc == 0), stop=(cc == 7)), r=[bT.r, wb.r], w=[pbb.r])
                        S.op("dve", lambda e, g4=g4, pa=pa, g_=g_: e.tensor_tensor(
                            out=t1.t[:], in0=pa.t[:, 0:512], in1=g_.t[:, g4 * 512:(g4 + 1) * 512], op=ALU.mult),
                            r=[pa.r, g_.r], w=[t1.r])
                        S.op("dve", lambda e, g4=g4, pbb=pbb, g_=g_: e.tensor_tensor(
                            out=g_.t[:, 2048 + g4 * 512:2048 + (g4 + 1) * 512], in0=pbb.t[:, 0:512],
                            in1=g_.t[:, 2048 + g4 * 512:2048 + (g4 + 1) * 512], op=ALU.mult),
                            r=[pbb.r, g_.r], w=[g_.r])
                        S.op("dve", lambda e, g4=g4, g_=g_, mixed=mixed: e.tensor_tensor(
                            out=mixed.t[:, g4 * 512:(g4 + 1) * 512], in0=t1.t[:],
                            in1=g_.t[:, 2048 + g4 * 512:2048 + (g4 + 1) * 512], op=ALU.add),
                            r=[t1.r, g_.r], w=[mixed.r])
                    S.dma("sp", lambda e, j=j, mixed=mixed: e.dma_start(out=mixed_d[j * 128:(j + 1) * 128, :],
                                                                        in_=mixed.t[:]), r=[mixed.r], w=[r_mixd[j]])
                S.flush()
            with ExitStack() as P5:
                wo = load_w(P5, "wo", w_o, NKC, D)
                xt = [sb(P5, f"x5_{i}", [128, D]) for i in range(2)]
                mxs = [sb(P5, f"mx5_{i}", [128, D], BF16) for i in range(2)]
                mixTs = [sb(P5, f"mixT5_{i}", [128, NKC, 128], BF16) for i in range(2)]
                for j in range(NJ):
                    x_ = xt[j % 2]
                    mx_ = mxs[j % 2]
                    mixT = mixTs[j % 2]
                    S.dma("sp", lambda e, j=j, x_=x_: e.dma_start(out=x_.t[:], in_=x_own[j * 128:(j + 1) * 128, :]),
                          w=[x_.r])
                    S.dma("sp", lambda e, j=j, mx_=mx_: e.dma_start(out=mx_.t[:], in_=mixed_d[j * 128:(j + 1) * 128, :]),
                          r=[r_mixd[j]], w=[mx_.r])
                    for half in range(2):
                        pb = PB[half]
                        for cc in range(8):
                            c = half * 8 + cc
                            S.op("pe", lambda e, c=c, cc=cc, pb=pb, mx_=mx_: e.transpose(
                                out=pb.t[:, cc * 128:(cc + 1) * 128], in_=mx_.t[:, c * 128:(c + 1) * 128],
                                identity=ident.t[:]), r=[mx_.r, ident.r], w=[pb.r])
                        S.op("act", lambda e, half=half, pb=pb, mixT=mixT: e.activation(
                            out=mixT.t[:, half * 8:(half + 1) * 8, :],
                            in_=pb.t[:, 0:1024].rearrange("p (c k) -> p c k", c=8), func=AF.Copy),
                            r=[pb.r], w=[mixT.r])
                    for g4 in range(4):
                        pf = PF[g4]
                        for c in range(NKC):
                            S.op("pe", lambda e, c=c, g4=g4, pf=pf, mixT=mixT: e.matmul(
                                pf.t[:, 0:512], lhsT=mixT.t[:, c, :],
                                rhs=wo.t[:, c, g4 * 512:(g4 + 1) * 512], start=(c == 0), stop=(c == NKC - 1)),
                                r=[mixT.r, wo.r], w=[pf.r])
                        S.op("dve", lambda e, g4=g4, pf=pf, x_=x_: e.tensor_tensor(
                            out=x_.t[:, g4 * 512:(g4 + 1) * 512], in0=pf.t[:, 0:512],
                            in1=x_.t[:, g4 * 512:(g4 + 1) * 512], op=ALU.add), r=[pf.r, x_.r], w=[x_.r])
                    S.dma("sp", lambda e, j=j, x_=x_: e.dma_start(out=x1_d[j * 128:(j + 1) * 128, :], in_=x_.t[:]),
                          r=[x_.r], w=[r_x1[j]])
                S.flush()

        with ExitStack() as P:
            idx_all = sb(P, "idx_all", [128, NJ, 128], I32)
            gate_all = sb(P, "gate_all", [128, NJ, 128])
            r_idx = [Res(f"idx{j}") for j in range(NJ)]
            with ExitStack() as P6:
                wpq = load_w(P6, "wpq", w_pq, NKC, D)
                kT_f = sb(P6, "kT_f", [128, 16, 128])
                kT = sb(P6, "kT_b", [128, 16, 128], BF16)
                S.dma("sp", lambda e: e.dma_start(out=kT_f.t[:], in_=keysT), w=[kT_f.r])
                S.op("dve", lambda e: e.tensor_copy(out=kT.t[:], in_=kT_f.t[:]), r=[kT_f.r], w=[kT.r])
                gffn = sb(P6, "gffn", [128, D])
                S.dma("sp", lambda e: e.dma_start(out=gffn.t[:], in_=g_ffn_bc), w=[gffn.r])
                iota16 = sb(P6, "iota16", [128, 16])
                S.op("pool", lambda e: e.iota(iota16.t[:], pattern=[[1, 16]], base=0, channel_multiplier=0,
                                              allow_small_or_imprecise_dtypes=True), w=[iota16.r])
                xt = [sb(P6, f"x6_{i}", [128, D]) for i in range(2)]
                junk = sb(P6, "junk6", [128, D], BF16)
                ss = sb(P6, "ss6", [128, 1])
                rstd = sb(P6, "rstd6", [128, 1])
                hn = [sb(P6, f"hn6_{i}", [128, D]) for i in range(2)]
                hnb = sb(P6, "hnb6", [128, D], BF16)
                hT6 = sb(P6, "hT6", [128, NKC, 128], BF16)
                qpT = [sb(P6, f"qpT{i}", [128, 128], BF16) for i in range(2)]
                sc = sb(P6, "sc6", [128, 16, 128])
                scw = sb(P6, "scw6", [128, 16, 128])
                tops = sb(P6, "tops", [128, 16, 16])
                topi = sb(P6, "topi", [128, 16, 16], U32)
                topif = sb(P6, "topif", [128, 16, 16])
                cand = sb(P6, "cand", [128, 8, 256])
                candw = sb(P6, "candw", [128, 8, 256])
                best = sb(P6, "best", [128, 8, 16])
                bpos = sb(P6, "bpos", [128, 8, 16], U32)
                k1u = sb(P6, "k1u", [128, 8, 16], U32)
                k2u = sb(P6, "k2u", [128, 8, 16], U32)
                k1f = sb(P6, "k1f", [128, 8, 16])
                k2f = sb(P6, "k2f", [128, 8, 16])
                oh = sb(P6, "oh6", [128, 16, 16])
                i1 = sb(P6, "i1", [128, 8, 16])
                i2 = sb(P6, "i2", [128, 8, 16])
                bm = sb(P6, "bm", [128, 8])
                bs = sb(P6, "bs", [128, 8])
                for j in range(NJ):
                    x_ = xt[j % 2]
                    hn_ = hn[j % 2]
                    S.dma("sp", lambda e, j=j, x_=x_: e.dma_start(out=x_.t[:], in_=x1_d[j * 128:(j + 1) * 128, :]),
                          r=[r_x1[j]], w=[x_.r])
                    S.op("act", lambda e, x_=x_: e.activation(out=junk.t[:], in_=x_.t[:], func=AF.Square,
                                                              accum_out=ss.t[:]), r=[x_.r], w=[junk.r, ss.r])
                    rstd_from_ss(ss, rstd, D)
                    S.op("dve", lambda e, x_=x_, hn_=hn_: e.scalar_tensor_tensor(
                        out=hn_.t[:], in0=x_.t[:], scalar=rstd.t[:, 0:1], in1=gffn.t[:], op0=ALU.mult, op1=ALU.mult),
                        r=[x_.r, rstd.r, gffn.r], w=[hn_.r])
                    S.dma("sp", lambda e, j=j, hn_=hn_: e.dma_start(out=hn2_d[j * 128:(j + 1) * 128, :], in_=hn_.t[:]),
                          r=[hn_.r], w=[r_hn2[j]])
                    S.op("act", lambda e, hn_=hn_: e.activation(out=hnb.t[:], in_=hn_.t[:], func=AF.Copy),
                         r=[hn_.r], w=[hnb.r])
                    for half in range(2):
                        pb = PB[half]
                        for cc in range(8):
                            c = half * 8 + cc
                            S.op("pe", lambda e, c=c, cc=cc, pb=pb: e.transpose(
                                out=pb.t[:, cc * 128:(cc + 1) * 128], in_=hnb.t[:, c * 128:(c + 1) * 128],
                                identity=ident.t[:]), r=[hnb.r, ident.r], w=[pb.r])
                        S.op("dve" if half else "act", (lambda e, half=half, pb=pb: e.tensor_copy(
                            out=hT6.t[:, half * 8:(half + 1) * 8, :],
                            in_=pb.t[:, 0:1024].rearrange("p (c k) -> p c k", c=8))) if half else
                            (lambda e, half=half, pb=pb: e.activation(
                                out=hT6.t[:, half * 8:(half + 1) * 8, :],
                                in_=pb.t[:, 0:1024].rearrange("p (c k) -> p c k", c=8), func=AF.Copy)),
                            r=[pb.r], w=[hT6.r])
                    for hp_ in range(16):
                        pq = PF[4 + hp_ % 2]
                        for c in range(NKC):
                            S.op("pe", lambda e, c=c, hp_=hp_, pq=pq: e.matmul(
                                pq.t[:, 0:128], lhsT=wpq.t[:, c, hp_ * 128:(hp_ + 1) * 128], rhs=hT6.t[:, c, :],
                                start=(c == 0), stop=(c == NKC - 1)), r=[wpq.r, hT6.r], w=[pq.r])
                        qp = qpT[hp_ % 2]
                        S.op("act", lambda e, pq=pq, qp=qp: e.activation(out=qp.t[:], in_=pq.t[:, 0:128], func=AF.Copy),
                             r=[pq.r], w=[qp.r])
                        psc = PF[hp_ // 4]
                        S.op("pe", lambda e, hp_=hp_, qp=qp, psc=psc: e.matmul(
                            psc.t[:, (hp_ % 4) * 128:(hp_ % 4 + 1) * 128], lhsT=qp.t[:], rhs=kT.t[:, hp_, :],
                            start=True, stop=True), r=[qp.r, kT.r], w=[psc.r])
                    r_sc = [Res() for _ in range(16)]
                    r_scw = [Res() for _ in range(16)]
                    r_tops = [Res() for _ in range(16)]
                    r_topi = [Res() for _ in range(16)]
                    for q4 in range(4):
                        S.op("dve", lambda e, q4=q4: e.tensor_copy(
                            out=sc.t[:, q4 * 4:(q4 + 1) * 4, :],
                            in_=PF[q4].t[:, 0:512].rearrange("p (a k) -> p a k", a=4)),
                            r=[PF[q4].r, sc.r], w=[r_sc[q4 * 4 + i] for i in range(4)])
                    for hp_ in range(16):
                        S.op("dve", lambda e, hp_=hp_: e.max(out=tops.t[:, hp_, 0:8], in_=sc.t[:, hp_, :]),
                             r=[r_sc[hp_], tops.r], w=[r_tops[hp_]])
                    for hp_ in range(16):
                        S.op("dve", lambda e, hp_=hp_: e.max_index(out=topi.t[:, hp_, 0:8], in_max=tops.t[:, hp_, 0:8],
                                                                   in_values=sc.t[:, hp_, :]),
                             r=[r_sc[hp_], r_tops[hp_], topi.r], w=[r_topi[hp_]])
                    for hp_ in range(16):
                        S.op("dve", lambda e, hp_=hp_: e.match_replace(out=scw.t[:, hp_, :],
                                                                       in_to_replace=tops.t[:, hp_, 0:8],
                                                                       in_values=sc.t[:, hp_, :], imm_value=-1e30),
                             r=[r_sc[hp_], r_tops[hp_], scw.r], w=[r_scw[hp_]])
                    for hp_ in range(16):
                        S.op("dve", lambda e, hp_=hp_: e.max(out=tops.t[:, hp_, 8:16], in_=scw.t[:, hp_, :]),
                             r=[r_scw[hp_]], w=[r_tops[hp_]])
                    for hp_ in range(16):
                        S.op("dve", lambda e, hp_=hp_: e.max_index(out=topi.t[:, hp_, 8:16],
                                                                   in_max=tops.t[:, hp_, 8:16],
                                                                   in_values=scw.t[:, hp_, :]),
                             r=[r_scw[hp_], r_tops[hp_]], w=[r_topi[hp_]])
                    S.op("dve", lambda e: e.tensor_copy(out=topif.t[:], in_=topi.t[:]), r=r_topi, w=[topif.r, topi.r])
                    r_cand = [Res() for _ in range(8)]
                    r_candw = [Res() for _ in range(8)]
                    r_best = [Res() for _ in range(8)]
                    r_bpos = [Res() for _ in range(8)]
                    for hd in range(8):
                        S.op("dve", lambda e, hd=hd: e.tensor_tensor(
                            out=cand.t[:, hd, :].rearrange("p (a b) -> p a b", a=16),
                            in0=tops.t[:, 2 * hd, :].unsqueeze(2).to_broadcast([128, 16, 16]),
                            in1=tops.t[:, 2 * hd + 1, :].unsqueeze(1).to_broadcast([128, 16, 16]), op=ALU.add),
                            r=[r_tops[2 * hd], r_tops[2 * hd + 1], cand.r], w=[r_cand[hd]])
                    for hd in range(8):
                        S.op("dve", lambda e, hd=hd: e.max(out=best.t[:, hd, 0:8], in_=cand.t[:, hd, :]),
                             r=[r_cand[hd], best.r], w=[r_best[hd]])
                    for hd in range(8):
                        S.op("dve", lambda e, hd=hd: e.max_index(out=bpos.t[:, hd, 0:8], in_max=best.t[:, hd, 0:8],
                                                                 in_values=cand.t[:, hd, :]),
                             r=[r_cand[hd], r_best[hd], bpos.r], w=[r_bpos[hd]])
                    for hd in range(8):
                        S.op("dve", lambda e, hd=hd: e.match_replace(out=candw.t[:, hd, :],
                                                                     in_to_replace=best.t[:, hd, 0:8],
                                                                     in_values=cand.t[:, hd, :], imm_value=-1e30),
                             r=[r_cand[hd], r_best[hd], candw.r], w=[r_candw[hd]])
                    for hd in range(8):
                        S.op("dve", lambda e, hd=hd: e.max(out=best.t[:, hd, 8:16], in_=candw.t[:, hd, :]),
                             r=[r_candw[hd]], w=[r_best[hd]])
                    for hd in range(8):
                        S.op("dve", lambda e, hd=hd: e.max_index(out=bpos.t[:, hd, 8:16], in_max=best.t[:, hd, 8:16],
                                                                 in_values=candw.t[:, hd, :]),
                             r=[r_candw[hd], r_best[hd]], w=[r_bpos[hd]])
                    S.op("dve", lambda e: e.tensor_copy(out=k1f.t[:, 0, 0:1], in_=k1f.t[:, 0, 0:1]),
                         r=r_bpos + r_best + r_tops + r_cand + r_candw + r_sc + r_scw,
                         w=[bpos.r, best.r, tops.r, cand.r, candw.r, sc.r, scw.r, k1f.r])
                    S.op("dve", lambda e: e.tensor_single_scalar(out=k1u.t[:], in_=bpos.t[:], scalar=4,
                                                                 op=ALU.logical_shift_right), r=[bpos.r], w=[k1u.r])
                    S.op("dve", lambda e: e.tensor_single_scalar(out=k2u.t[:], in_=bpos.t[:], scalar=15,
                                                                 op=ALU.bitwise_and), r=[bpos.r], w=[k2u.r])
                    S.op("dve", lambda e: e.tensor_copy(out=k1f.t[:], in_=k1u.t[:]), r=[k1u.r], w=[k1f.r])
                    S.op("dve", lambda e: e.tensor_copy(out=k2f.t[:], in_=k2u.t[:]), r=[k2u.r], w=[k2f.r])
                    for hd in range(8):
                        for side, kf_, io_ in ((0, k1f, i1), (1, k2f, i2)):
                            S.op("dve", lambda e, hd=hd, kf_=kf_: e.tensor_tensor(
                                out=oh.t[:], in0=iota16.t[:].unsqueeze(1).to_broadcast([128, 16, 16]),
                                in1=kf_.t[:, hd, :].unsqueeze(2).to_broadcast([128, 16, 16]), op=ALU.is_equal),
                                r=[iota16.r, kf_.r], w=[oh.r])
                            S.op("dve", lambda e, hd=hd, side=side: e.tensor_tensor(
                                out=oh.t[:], in0=oh.t[:],
                                in1=topif.t[:, 2 * hd + side, :].unsqueeze(1).to_broadcast([128, 16, 16]),
                                op=ALU.mult), r=[oh.r, topif.r], w=[oh.r])
                            S.op("dve", lambda e, hd=hd, io_=io_: e.tensor_reduce(out=io_.t[:, hd, :], in_=oh.t[:],
                                                                                  axis=AX.X, op=ALU.add),
                                 r=[oh.r], w=[io_.r])
                    S.op("dve", lambda e: e.scalar_tensor_tensor(out=i1.t[:], in0=i1.t[:], scalar=128.0, in1=i2.t[:],
                                                                 op0=ALU.mult, op1=ALU.add),
                         r=[i1.r, i2.r], w=[i1.r])
                    S.op("dve", lambda e, j=j: e.tensor_copy(
                        out=idx_all.t[:, j, :].rearrange("p (a b) -> p a b", a=8), in_=i1.t[:]),
                        r=[i1.r], w=[r_idx[j]])
                    S.op("dve", lambda e: e.tensor_reduce(out=bm.t[:], in_=best.t[:], axis=AX.X, op=ALU.max),
                         r=[best.r], w=[bm.r])
                    S.op("dve", lambda e: e.tensor_tensor(out=best.t[:], in0=best.t[:],
                                                          in1=bm.t[:].unsqueeze(2).to_broadcast([128, 8, 16]),
                                                          op=ALU.subtract), r=[best.r, bm.r], w=[best.r])
                    S.op("act", lambda e: e.activation(out=best.t[:], in_=best.t[:], func=AF.Exp),
                         r=[best.r], w=[best.r])
                    S.op("dve", lambda e: e.tensor_reduce(out=bs.t[:], in_=best.t[:], axis=AX.X, op=ALU.add),
                         r=[best.r], w=[bs.r])
                    S.op("dve", lambda e: e.reciprocal(out=bs.t[:], in_=bs.t[:]), r=[bs.r], w=[bs.r])
                    S.op("dve", lambda e, j=j: e.tensor_tensor(
                        out=gate_all.t[:, j, :].rearrange("p (a b) -> p a b", a=8), in0=best.t[:],
                        in1=bs.t[:].unsqueeze(2).to_broadcast([128, 8, 16]), op=ALU.mult),
                        r=[best.r, bs.r], w=[r_idx[j]])
                S.flush()

            with ExitStack() as P7:
                gfin = sb(P7, "gfin", [128, D])
                S.dma("sp", lambda e: e.dma_start(out=gfin.t[:], in_=g_fin_bc), w=[gfin.r])
                NB = 8
                GS = 4
                gb = [sb(P7, f"gb{i}", [128, 2 * D], BF16) for i in range(NB)]
                hn7 = [sb(P7, f"hn7_{i}", [128, D]) for i in range(2)]
                hnb7 = [sb(P7, f"hnb7_{i}", [128, D], BF16) for i in range(2)]
                x7 = [sb(P7, f"x7_{i}", [128, D]) for i in range(2)]
                acc = sb(P7, "acc7", [128, D])
                junkb = sb(P7, "junk7b", [128, D], BF16)
                diag = [sb(P7, f"diag{i}", [128, 128], BF16) for i in range(8)]
                avs = [sb(P7, f"av7_{i}", [128, GS]) for i in range(4)]
                wvs = [sb(P7, f"wv7_{i}", [128, GS]) for i in range(4)]
                ss = sb(P7, "ss7", [128, 1])
                rstd = sb(P7, "rstd7", [128, 1])
                bi = 0
                gi = 0
                for j in range(NJ):
                    hn_ = hn7[j % 2]
                    hb_ = hnb7[j % 2]
                    x_ = x7[j % 2]
                    S.dma("sp", lambda e, j=j, hn_=hn_: e.dma_start(out=hn_.t[:], in_=hn2_d[j * 128:(j + 1) * 128, :]),
                          r=[r_hn2[j]], w=[hn_.r])
                    S.dma("sp", lambda e, j=j, x_=x_: e.dma_start(out=x_.t[:], in_=x1_d[j * 128:(j + 1) * 128, :]),
                          r=[r_x1[j]], w=[x_.r])
                    S.op("act", lambda e, hn_=hn_, hb_=hb_: e.activation(out=hb_.t[:], in_=hn_.t[:], func=AF.Copy),
                         r=[hn_.r], w=[hb_.r])
                    for s0 in range(0, 128, GS):
                        av = avs[gi % 4]
                        wv7 = wvs[gi % 4]
                        gi += 1
                        gl = []
                        for k in range(GS):
                            s_ = s0 + k
                            g_ = gb[bi % NB]
                            bi += 1
                            gl.append(g_)
                            S.dma("pool", lambda e, g_=g_, j=j, s_=s_: e.indirect_dma_start(
                                out=g_.t[:], out_offset=None, in_=uv_b,
                                in_offset=bass.IndirectOffsetOnAxis(ap=idx_all.t[:, j, s_:s_ + 1], axis=0)),
                                r=[r_idx[j], r_uvb], w=[g_.r])
                            S.op("dve", lambda e, g_=g_, hb_=hb_, k=k, av=av: e.scalar_tensor_tensor(
                                out=junkb.t[:], in0=g_.t[:, 0:D], scalar=1.0, in1=hb_.t[:], op0=ALU.mult, op1=ALU.mult,
                                accum_out=av.t[:, k:k + 1]), r=[g_.r, hb_.r], w=[junkb.r, av.r])
                        S.op("act", lambda e, av=av, wv7=wv7: e.activation(out=wv7.t[:], in_=av.t[:], func=AF.Gelu),
                             r=[av.r], w=[wv7.r])
                        S.op("dve", lambda e, j=j, s0=s0, wv7=wv7: e.tensor_tensor(
                            out=wv7.t[:], in0=wv7.t[:], in1=gate_all.t[:, j, s0:s0 + GS], op=ALU.mult),
                            r=[wv7.r, r_idx[j]], w=[wv7.r])
                        for k in range(GS):
                            s_ = s0 + k
                            g_ = gl[k]
                            dg = diag[s_ % 8]
                            S.op("act", lambda e, dg=dg, k=k, wv7=wv7: e.activation(
                                out=dg.t[:], in_=ident.t[:], func=AF.Copy, scale=wv7.t[:, k:k + 1]),
                                r=[ident.r, wv7.r], w=[dg.r])
                            for q4 in range(4):
                                S.op("pe", lambda e, dg=dg, g_=g_, q4=q4, s_=s_: e.matmul(
                                    PF[q4].t[:, 0:512], lhsT=dg.t[:], rhs=g_.t[:, D + q4 * 512:D + (q4 + 1) * 512],
                                    start=(s_ == 0), stop=(s_ == 127)), r=[dg.r, g_.r], w=[PF[q4].r])
                    for q4 in range(4):
                        S.op("dve", lambda e, q4=q4, x_=x_: e.tensor_tensor(
                            out=x_.t[:, q4 * 512:(q4 + 1) * 512], in0=PF[q4].t[:, 0:512],
                            in1=x_.t[:, q4 * 512:(q4 + 1) * 512], op=ALU.add), r=[PF[q4].r, x_.r], w=[x_.r])
                    S.op("act", lambda e, x_=x_: e.activation(out=junkb.t[:], in_=x_.t[:], func=AF.Square,
                                                              accum_out=ss.t[:]), r=[x_.r], w=[junkb.r, ss.r])
                    rstd_from_ss(ss, rstd, D)
                    S.op("dve", lambda e, x_=x_: e.scalar_tensor_tensor(
                        out=acc.t[:], in0=x_.t[:], scalar=rstd.t[:, 0:1], in1=gfin.t[:], op0=ALU.mult, op1=ALU.mult),
                        r=[x_.r, rstd.r, gfin.r], w=[acc.r])
                    S.dma("sp", lambda e, j=j: e.dma_start(out=out_d[j * 128:(j + 1) * 128, :], in_=acc.t[:]),
                          r=[acc.r], w=[Res("out")])
                S.flush()
    return nc


def make_in_maps(inp, NJ, cores):
    NS = 4 * NJ
    SQ = 128 * NS
    f32 = np.float32
    x = np.asarray(inp["x"], f32)
    pos = np.asarray(inp["positions"], np.int32)
    w_in = np.asarray(inp["w_in"], f32)[0]
    offs = np.cumsum([0, 1024, 256, 256, 512, 256, 64, 2048, 2048])
    q_a, k_a, v_a, c_q, c_kv, k_r, g_a, g_b = [slice(int(offs[i]), int(offs[i + 1])) for i in range(8)]
    C = np.ascontiguousarray

    def bc(v, n=128):
        return C(np.broadcast_to(np.asarray(v, f32)[None, :], (n, len(v))))

    w_ukv = np.asarray(inp["w_ukv"], f32)[0].reshape(256, 8, 256)
    shared = {
        "invf_bc": bc(1.0 / (10000.0 ** (np.arange(0, 64, 2, dtype=np.float32) / 64)).astype(f32)),
        "g_mix_pc": C(np.asarray(inp["g_mix"], f32)[0].reshape(NKC, 128).T),
        "g_ffn_bc": bc(np.asarray(inp["g_ffn"], f32)[0]),
        "g_fin_bc": bc(np.asarray(inp["g_final"], f32)),
        "g_ckv_bc": bc(np.asarray(inp["g_ckv"], f32)[0]),
        "g_cq_bc": bc(np.asarray(inp["g_cq"], f32)[0]),
        "sinks_bc": bc(np.asarray(inp["sinks"], f32)[0]),
        "w_kv": C(np.concatenate([w_in[:, c_kv], w_in[:, k_r]], axis=1)),
        "w_kva": C(np.concatenate([w_in[:, k_a], w_in[:, v_a]], axis=1)),
        "w_cq": C(w_in[:, c_q]),
        "w_qa": C(w_in[:, q_a]),
        "w_gate": C(np.concatenate([w_in[:, g_a], w_in[:, g_b]], axis=1)),
        "w_uq": C(np.asarray(inp["w_uq"], f32)[0]),
        "w_ukv_k": C(w_ukv[:, :, :128].reshape(256, 1024)),
        "w_ukv_v": C(w_ukv[:, :, 128:].reshape(256, 1024)),
        "w_a": C(np.asarray(inp["w_a_proj"], f32)[0]),
        "w_b": C(np.asarray(inp["w_b_proj"], f32)[0]),
        "w_o": C(np.asarray(inp["w_o"], f32)[0]),
        "w_pq": C(np.asarray(inp["w_peer_q"], f32)[0]),
        "keysT": C(np.asarray(inp["peer_keys"], f32)[0].reshape(16, 128, 128).transpose(2, 0, 1)),
        "u_tab": C(np.asarray(inp["peer_u"], f32)[0]),
        "v_tab": C(np.asarray(inp["peer_v"], f32)[0]),
        "ident": np.eye(128, dtype=f32),
    }
    qi = np.arange(128)[:, None]
    kj = np.arange(256)[None, :]
    rel = 128 + qi - kj
    band = (rel >= 0) & (rel < 128)
    maps = []
    for c in cores:
        b, r = c // 4, c % 4
        xs = x[b, :SQ]
        ps = pos[b, :SQ]
        own = [4 * j + r for j in range(NJ)]
        m = dict(shared)
        m["x_seq"] = C(xs)
        m["x_own"] = C(np.concatenate([xs[t * 128:(t + 1) * 128] for t in own], axis=0))
        xp = []
        pk = []
        for t in own:
            if t == 0:
                xp.append(np.zeros((128, D), f32))
                pprev = np.zeros(128, np.int32)
            else:
                xp.append(xs[(t - 1) * 128:t * 128])
                pprev = ps[(t - 1) * 128:t * 128]
            row = np.concatenate([pprev, ps[t * 128:(t + 1) * 128]])
            pk.append(np.broadcast_to(row[None, :], (128, 256)))
        m["x_prev"] = C(np.concatenate(xp, axis=0))
        m["pos_kv_bc"] = C(np.stack(pk, axis=0).astype(np.int32))
        m["pos_seq"] = C(ps.reshape(NS, 128).T)
        m["pos_own"] = C(np.stack([ps[t * 128:(t + 1) * 128] for t in own], axis=1))
        mb0 = np.where(band & ((kj >= 128) | (own[0] > 0)), 0.0, -1e30).astype(f32)
        mb1 = np.where(band, 0.0, -1e30).astype(f32)
        m["swa_maskb"] = C(np.stack([mb0, mb1], axis=0))
        mm = np.zeros((128, 512), f32)
        for kk in range(4):
            if kk < r:
                mm[:, kk * 128:(kk + 1) * 128] = 1.0
            elif kk == r:
                mm[:, kk * 128:(kk + 1) * 128] = (np.arange(128)[None, :] <= np.arange(128)[:, None])
        m["mla_mask"] = mm
        maps.append(m)
    return maps


def run(inputs, NJ, cores):
    nc = build(NJ)
    maps = make_in_maps(inputs, NJ, cores)
    res = run_bass_kernel_spmd(nc, maps, core_ids=list(range(len(cores))))
    return [r["out"] for r in res.results]


def kernel(**inputs):
    NJ = 16
    cores = list(range(8))
    outs = run(inputs, NJ, cores)
    B, SQ = 2, 8192
    out = np.empty((B, SQ, D), np.float32)
    for c, o in zip(cores, outs):
        b, r = c // 4, c % 4
        for j in range(NJ):
            t = 4 * j + r
            out[b, t * 128:(t + 1) * 128] = o[j * 128:(j + 1) * 128]
    return out
```

```python
import math
from contextlib import ExitStack

import numpy as np
import concourse.bass as bass
import concourse.mybir as mybir
from concourse.bass_utils import run_bass_kernel_spmd

F32 = mybir.dt.float32
BF16 = mybir.dt.bfloat16
I32 = mybir.dt.int32
U32 = mybir.dt.uint32
AF = mybir.ActivationFunctionType
ALU = mybir.AluOpType
AX = mybir.AxisListType

D = 2048
EPS = 1e-6
NKC = D // 128
PI = math.pi
TWO_PI = 2.0 * math.pi
MLA_SCALE = 192 ** -0.5
N_EXP = 16384


class Res:
    __slots__ = ("name", "w", "rd")

    def __init__(self, name=""):
        self.name = name
        self.w = None
        self.rd = {}


class Sched:
    ENGS = ("pe", "act", "dve", "pool", "sp")
    NDMA = {"sp": 8, "act": 4, "pool": 8}

    def __init__(self, nc, es):
        self.nc = nc
        self.sem = {}
        for e in ("pe", "act", "dve", "pool"):
            self.sem[e] = es.enter_context(nc.semaphore("c_" + e))
        for q, n in self.NDMA.items():
            for k in range(n):
                self.sem[("d", q, k)] = es.enter_context(nc.semaphore(f"d_{q}{k}"))
        self.cnt = {k: 0 for k in self.sem}
        self.seen = {e: {} for e in self.ENGS}
        self.items = {e: [] for e in self.ENGS}
        self.drr = {q: 0 for q in self.NDMA}

    def _deps(self, eng, reads, writes):
        need = {}

        def add(kv):
            if kv is None:
                return
            k, v = kv
            if k == "pe" and eng == "pe":
                return
            if need.get(k, 0) < v:
                need[k] = v
        for r in reads:
            add(r.w)
        for w in writes:
            add(w.w)
            for k, v in w.rd.items():
                add((k, v))
        waits = []
        sn = self.seen[eng]
        for k, v in need.items():
            if sn.get(k, 0) < v:
                sn[k] = v
                waits.append((k, v))
        return waits

    @staticmethod
    def _mark(key, val, reads, writes):
        for r in reads:
            if r.rd.get(key, 0) < val:
                r.rd[key] = val
        for w in writes:
            w.w = (key, val)
            w.rd = {}

    def op(self, eng, fn, r=(), w=()):
        waits = self._deps(eng, r, w)
        self.cnt[eng] += 1
        self._mark(eng, self.cnt[eng], r, w)
        self.items[eng].append((waits, fn, (eng, 1)))

    def dma(self, q, fn, r=(), w=()):
        k = self.drr[q]
        self.drr[q] = (k + 1) % self.NDMA[q]
        key = ("d", q, k)
        waits = self._deps(q, r, w)
        prev = self.cnt[key]
        if prev and self.seen[q].get(key, 0) < prev:
            self.seen[q][key] = prev
            waits.append((key, prev))
        self.cnt[key] += 16
        self._mark(key, self.cnt[key], r, w)
        self.items[q].append((waits, fn, (key, 16)))

    def barrier(self):
        for e in self.ENGS:
            waits = []
            for k, v in self.cnt.items():
                if v and self.seen[e].get(k, 0) < v:
                    self.seen[e][k] = v
                    waits.append((k, v))
            if waits:
                self.items[e].append((waits, None, None))

    def flush(self):
        self.barrier()
        nc, items, sem = self.nc, self.items, self.sem

        def emit(h, lst):
            for waits, fn, inc in lst:
                for k, v in waits:
                    h.wait_ge(sem[k], v)
                if fn is not None:
                    fn(h).then_inc(sem[inc[0]], inc[1])

        with nc.Block() as block:
            @block.tensor
            def _(e):
                emit(e, items["pe"])

            @block.scalar
            def _(e):
                emit(e, items["act"])

            @block.vector
            def _(e):
                emit(e, items["dve"])

            @block.gpsimd
            def _(e):
                emit(e, items["pool"])

            @block.sync
            def _(e):
                emit(e, items["sp"])
        self.items = {e: [] for e in self.ENGS}


class Tl:
    __slots__ = ("t", "r")

    def __init__(self, t, name):
        self.t = t
        self.r = Res(name)


def build(NJ):
    NS = 4 * NJ
    SQ = 128 * NS
    TK = 128 * NJ
    nc = bass.Bass("TRN2", target_bir_lowering=False)

    def din(name, shape, dt=F32):
        return nc.dram_tensor(name, list(shape), dt, kind="ExternalInput").ap()

    def dscr(name, shape, dt):
        return nc.dram_tensor(name, list(shape), dt, kind="Internal").ap()

    x_seq = din("x_seq", [SQ, D])
    x_own = din("x_own", [TK, D])
    x_prev = din("x_prev", [TK, D])
    pos_seq = din("pos_seq", [128, NS], I32)
    pos_own = din("pos_own", [128, NJ], I32)
    pos_kv_bc = din("pos_kv_bc", [NJ, 128, 256], I32)
    swa_maskb = din("swa_maskb", [2, 128, 256])
    mla_mask = din("mla_mask", [128, 512])
    invf_bc = din("invf_bc", [128, 32])
    g_mix_pc = din("g_mix_pc", [128, NKC])
    g_ffn_bc = din("g_ffn_bc", [128, D])
    g_fin_bc = din("g_fin_bc", [128, D])
    g_ckv_bc = din("g_ckv_bc", [128, 256])
    g_cq_bc = din("g_cq_bc", [128, 512])
    sinks_bc = din("sinks_bc", [128, 16])
    w_kv = din("w_kv", [D, 320])
    w_kva = din("w_kva", [D, 512])
    w_cq = din("w_cq", [D, 512])
    w_qa = din("w_qa", [D, 1024])
    w_gate = din("w_gate", [D, 4096])
    w_uq = din("w_uq", [512, 1536])
    w_ukv_k = din("w_ukv_k", [256, 1024])
    w_ukv_v = din("w_ukv_v", [256, 1024])
    w_a = din("w_a", [1024, D])
    w_b = din("w_b", [1024, D])
    w_o = din("w_o", [D, D])
    w_pq = din("w_pq", [D, D])
    keysT = din("keysT", [128, 16, 128])
    u_tab = din("u_tab", [N_EXP, D])
    v_tab = din("v_tab", [N_EXP, D])
    ident_in = din("ident", [128, 128])
    out_d = nc.dram_tensor("out", [TK, D], F32, kind="ExternalOutput").ap()

    qnT_d = dscr("qnT_d", [8, 128, TK], BF16)
    qrT_d = dscr("qrT_d", [8, 64, TK], BF16)
    swa_d = dscr("swa_d", [TK, 1024], BF16)
    mla_d = dscr("mla_d", [TK, 1024], BF16)
    zg_d = dscr("zg_d", [TK, 4096], F32)
    x1_d = dscr("x1_d", [TK, D], F32)
    mixed_d = dscr("mixed_d", [TK, D], BF16)
    hn2_d = dscr("hn2_d", [TK, D], F32)
    uv_b = dscr("uv_b", [N_EXP, 2 * D], BF16)
    r_uvb = Res("uv_b")
    r_qT = Res("qT_d")
    r_swa = [Res(f"swa_d{j}") for j in range(NJ)]
    r_mla = [Res(f"mla_d{j}") for j in range(NJ)]
    r_zg = [Res(f"zg_d{j}") for j in range(NJ)]
    r_x1 = [Res(f"x1_d{j}") for j in range(NJ)]
    r_hn2 = [Res(f"hn2_d{j}") for j in range(NJ)]

    slopes = [2.0 ** (-8.0 * (h + 1) / 16) for h in range(16)]

    with ExitStack() as G:
        S = Sched(nc, G)

        def sb(es, name, shape, dt=F32):
            return Tl(es.enter_context(nc.sbuf_tensor(name, list(shape), dt)), name)

        def psum(es, name, shape, dt=F32):
            return Tl(es.enter_context(nc.psum_tensor(name, list(shape), dt)), name)

        PF = [psum(G, f"pf{i}", [128, 512], F32) for i in range(6)]
        PB = [psum(G, f"pb{i}", [128, 1024], BF16) for i in range(2)]

        ident_f = sb(G, "ident_f", [128, 128])
        ident = sb(G, "ident_b", [128, 128], BF16)
        S.dma("sp", lambda e: e.dma_start(out=ident_f.t[:], in_=ident_in), w=[ident_f.r])
        S.op("dve", lambda e: e.tensor_copy(out=ident.t[:], in_=ident_f.t[:]), r=[ident_f.r], w=[ident.r])
        gmix = sb(G, "gmix", [128, NKC])
        S.dma("sp", lambda e: e.dma_start(out=gmix.t[:], in_=g_mix_pc), w=[gmix.r])
        invf = sb(G, "invf", [128, 32])
        S.dma("sp", lambda e: e.dma_start(out=invf.t[:], in_=invf_bc), w=[invf.r])
        GK = ExitStack()
        ckvT = sb(GK, "ckvT", [128, 2, SQ], BF16)
        kropeT = sb(GK, "kropeT", [128, SQ], BF16)
        S.op("dve", lambda e: e.memset(kropeT.t[64:128, :], 0.0), w=[kropeT.r])
        r_ckvT = [Res(f"ckvT{t}") for t in range(NS)]

        def load_w(es, name, w_ap, kchunks, ncols, dst=None, col0=0, ncol_dst=None):
            t = dst if dst is not None else sb(es, name, [128, kchunks, ncols], BF16)
            src = w_ap.rearrange("(c p) n -> p c n", p=128)
            step = max(1, 4096 // ncols)
            for c0 in range(0, kchunks, step):
                c1 = min(kchunks, c0 + step)
                S.dma("pool", lambda e, c0=c0, c1=c1: e.dma_start(out=t.t[:, c0:c1, :], in_=src[:, c0:c1, :]),
                      w=[t.r])
            return t

        def rstd_from_ss(ss, rstd, n):
            S.op("dve", lambda e: e.tensor_scalar(out=rstd.t[:], in0=ss.t[:], scalar1=1.0 / n, scalar2=EPS,
                                                  op0=ALU.mult, op1=ALU.add), r=[ss.r], w=[rstd.r])
            S.op("act", lambda e: e.activation(out=rstd.t[:], in_=rstd.t[:], func=AF.Sqrt), r=[rstd.r], w=[rstd.r])
            S.op("dve", lambda e: e.reciprocal(out=rstd.t[:], in_=rstd.t[:]), r=[rstd.r], w=[rstd.r])

        def range_reduce(src, dst, ki, kf, tmp):
            S.op("dve", lambda e: e.tensor_scalar(out=ki.t[:], in0=src.t[:], scalar1=1.0 / TWO_PI, scalar2=None,
                                                  op0=ALU.mult), r=[src.r], w=[ki.r])
            S.op("dve", lambda e: e.tensor_copy(out=kf.t[:], in_=ki.t[:]), r=[ki.r], w=[kf.r])
            S.op("dve", lambda e: e.scalar_tensor_tensor(out=dst.t[:], in0=kf.t[:], scalar=-TWO_PI, in1=src.t[:],
                                                         op0=ALU.mult, op1=ALU.add), r=[kf.r, src.r], w=[dst.r])
            S.op("dve", lambda e: e.tensor_scalar(out=tmp.t[:], in0=dst.t[:], scalar1=PI, scalar2=-TWO_PI,
                                                  op0=ALU.is_gt, op1=ALU.mult), r=[dst.r], w=[tmp.r])
            S.op("dve", lambda e: e.tensor_tensor(out=dst.t[:], in0=dst.t[:], in1=tmp.t[:], op=ALU.add),
                 r=[dst.r, tmp.r], w=[dst.r])
            S.op("dve", lambda e: e.tensor_scalar(out=tmp.t[:], in0=dst.t[:], scalar1=-PI, scalar2=TWO_PI,
                                                  op0=ALU.is_lt, op1=ALU.mult), r=[dst.r], w=[tmp.r])
            S.op("dve", lambda e: e.tensor_tensor(out=dst.t[:], in0=dst.t[:], in1=tmp.t[:], op=ALU.add),
                 r=[dst.r, tmp.r], w=[dst.r])
            S.op("dve", lambda e: e.tensor_scalar(out=dst.t[:], in0=dst.t[:], scalar1=-3.1415925, scalar2=3.1415925,
                                                  op0=ALU.max, op1=ALU.min), r=[dst.r], w=[dst.r])

        def cos_sin(es, name, pos_ap, n):
            cs = sb(es, name + "_cos", [128, n, 32])
            sn = sb(es, name + "_sin", [128, n, 32])
            with ExitStack() as L:
                pi_ = sb(L, name + "_pi", [128, n], I32)
                pf_ = sb(L, name + "_pf", [128, n])
                ang = sb(L, name + "_ang", [128, n, 32])
                red = sb(L, name + "_red", [128, n, 32])
                ki = sb(L, name + "_ki", [128, n, 32], I32)
                kf = sb(L, name + "_kf", [128, n, 32])
                tmp = sb(L, name + "_tmp", [128, n, 32])
                S.dma("sp", lambda e: e.dma_start(out=pi_.t[:], in_=pos_ap), w=[pi_.r])
                S.op("dve", lambda e: e.tensor_copy(out=pf_.t[:], in_=pi_.t[:]), r=[pi_.r], w=[pf_.r])
                S.op("dve", lambda e: e.tensor_tensor(out=ang.t[:],
                                                      in0=pf_.t[:].unsqueeze(2).to_broadcast([128, n, 32]),
                                                      in1=invf.t[:].unsqueeze(1).to_broadcast([128, n, 32]),
                                                      op=ALU.mult), r=[pf_.r, invf.r], w=[ang.r])
                range_reduce(ang, red, ki, kf, tmp)
                S.op("act", lambda e: e.activation(out=sn.t[:], in_=red.t[:], func=AF.Sin), r=[red.r], w=[sn.r])
                S.op("dve", lambda e: e.tensor_scalar(out=ang.t[:], in0=ang.t[:], scalar1=PI / 2, scalar2=None,
                                                      op0=ALU.add), r=[ang.r], w=[ang.r])
                range_reduce(ang, red, ki, kf, tmp)
                S.op("act", lambda e: e.activation(out=cs.t[:], in_=red.t[:], func=AF.Sin), r=[red.r], w=[cs.r])
                S.flush()
            return cs, sn

        class NormCtx:
            def __init__(self, es, name, gpc):
                self.x = [sb(es, f"{name}_x{i}", [128, D]) for i in range(2)]
                self.junk = sb(es, name + "_junk", [128, D], BF16)
                self.ss = sb(es, name + "_ss", [128, 1])
                self.rstd = sb(es, name + "_rstd", [128, 1])
                self.xn = sb(es, name + "_xn", [128, D], BF16)
                self.gpc = gpc
                self.i = 0

            def run(self, x_ap, hT, hT_slice, extra_r=()):
                xt = self.x[self.i % 2]
                self.i += 1
                S.dma("sp", lambda e: e.dma_start(out=xt.t[:], in_=x_ap), r=list(extra_r), w=[xt.r])
                S.op("act", lambda e: e.activation(out=self.junk.t[:], in_=xt.t[:], func=AF.Square,
                                                   accum_out=self.ss.t[:]), r=[xt.r], w=[self.junk.r, self.ss.r])
                rstd_from_ss(self.ss, self.rstd, D)
                S.op("dve", lambda e: e.tensor_scalar(out=self.xn.t[:], in0=xt.t[:], scalar1=self.rstd.t[:, 0:1],
                                                      scalar2=None, op0=ALU.mult), r=[xt.r, self.rstd.r], w=[self.xn.r])
                for half in range(2):
                    pb = PB[half]
                    for cc in range(8):
                        c = half * 8 + cc
                        S.op("pe", lambda e, c=c, cc=cc, pb=pb: e.transpose(
                            out=pb.t[:, cc * 128:(cc + 1) * 128], in_=self.xn.t[:, c * 128:(c + 1) * 128],
                            identity=ident.t[:]), r=[self.xn.r, ident.r], w=[pb.r])
                    S.op("dve", lambda e, half=half, pb=pb: e.tensor_tensor(
                        out=hT_slice(slice(half * 8, half * 8 + 8)),
                        in0=pb.t[:, 0:1024].rearrange("p (c k) -> p c k", c=8),
                        in1=self.gpc.t[:, half * 8:half * 8 + 8].unsqueeze(2).to_broadcast([128, 8, 128]),
                        op=ALU.mult), r=[pb.r, self.gpc.r], w=[hT.r])
                return xt

        def proj(hT, hT_slice, w, col0, ncols, pf):
            for c in range(NKC):
                S.op("pe", lambda e, c=c: e.matmul(pf.t[:, 0:ncols], lhsT=hT_slice(c), rhs=w.t[:, c, col0:col0 + ncols],
                                                   start=(c == 0), stop=(c == NKC - 1)), r=[hT.r, w.r], w=[pf.r])

        def rope(src_ap_fn, dst_ap_fn, cs_ap, sn_ap, tmps, rsrc, rdst, rcs):
            a, b_ = tmps
            S.op("dve", lambda e: e.tensor_tensor(out=a.t[:], in0=src_ap_fn(0, 32), in1=cs_ap, op=ALU.mult),
                 r=[rsrc, rcs], w=[a.r])
            S.op("dve", lambda e: e.tensor_tensor(out=b_.t[:], in0=src_ap_fn(32, 64), in1=sn_ap, op=ALU.mult),
                 r=[rsrc, rcs], w=[b_.r])
            S.op("dve", lambda e: e.tensor_tensor(out=dst_ap_fn(0, 32), in0=a.t[:], in1=b_.t[:], op=ALU.subtract),
                 r=[a.r, b_.r], w=[rdst])
            S.op("dve", lambda e: e.tensor_tensor(out=a.t[:], in0=src_ap_fn(0, 32), in1=sn_ap, op=ALU.mult),
                 r=[rsrc, rcs], w=[a.r])
            S.op("dve", lambda e: e.tensor_tensor(out=b_.t[:], in0=src_ap_fn(32, 64), in1=cs_ap, op=ALU.mult),
                 r=[rsrc, rcs], w=[b_.r])
            S.op("dve", lambda e: e.tensor_tensor(out=dst_ap_fn(32, 64), in0=a.t[:], in1=b_.t[:], op=ALU.add),
                 r=[a.r, b_.r], w=[rdst])

        S.flush()

        with ExitStack() as P:
            cs_s, sn_s = cos_sin(P, "cs_seq", pos_seq, NS)
            wkv = load_w(P, "wkv", w_kv, NKC, 320)
            gckv = sb(P, "gckv", [128, 256])
            S.dma("sp", lambda e: e.dma_start(out=gckv.t[:], in_=g_ckv_bc), w=[gckv.r])
            nctx = NormCtx(P, "n1", gmix)
            hT = [sb(P, f"hT1_{i}", [128, NKC, 128], BF16) for i in range(2)]
            ss2 = sb(P, "ss2", [128, 1])
            rstd2 = sb(P, "rstd2", [128, 1])
            junk2 = sb(P, "junk2", [128, 256], BF16)
            ckvn = sb(P, "ckvn", [128, 256], BF16)
            krs = sb(P, "krs", [128, 64])
            kro = sb(P, "kro", [128, 64], BF16)
            ta = sb(P, "ropa", [128, 32])
            tb = sb(P, "ropb", [128, 32])
            for t in range(NS):
                h = hT[t % 2]
                nctx.run(x_seq[t * 128:(t + 1) * 128, :], h, lambda c, h=h: h.t[:, c, :])
                pf = PF[t % 2]
                proj(h, lambda c, h=h: h.t[:, c, :], wkv, 0, 320, pf)
                S.op("act", lambda e, pf=pf: e.activation(out=junk2.t[:], in_=pf.t[:, 0:256], func=AF.Square,
                                                          accum_out=ss2.t[:]), r=[pf.r], w=[junk2.r, ss2.r])
                rstd_from_ss(ss2, rstd2, 256)
                S.op("dve", lambda e, pf=pf: e.scalar_tensor_tensor(out=ckvn.t[:], in0=pf.t[:, 0:256],
                                                                    scalar=rstd2.t[:, 0:1], in1=gckv.t[:],
                                                                    op0=ALU.mult, op1=ALU.mult),
                     r=[pf.r, rstd2.r, gckv.r], w=[ckvn.r])
                S.op("act", lambda e, pf=pf: e.activation(out=krs.t[:], in_=pf.t[:, 256:320], func=AF.Copy),
                     r=[pf.r], w=[krs.r])
                rope(lambda lo, hi: krs.t[:, lo:hi], lambda lo, hi: kro.t[:, lo:hi],
                     cs_s.t[:, t, :], sn_s.t[:, t, :], (ta, tb), krs.r, kro.r, cs_s.r)
                pb = PB[t % 2]
                for cc in range(2):
                    S.op("pe", lambda e, cc=cc, pb=pb: e.transpose(out=pb.t[:, cc * 128:(cc + 1) * 128],
                                                                   in_=ckvn.t[:, cc * 128:(cc + 1) * 128],
                                                                   identity=ident.t[:]),
                         r=[ckvn.r, ident.r], w=[pb.r])
                S.op("pe", lambda e, pb=pb: e.transpose(out=pb.t[0:64, 256:384], in_=kro.t[:, 0:64],
                                                        identity=ident.t[:]), r=[kro.r, ident.r], w=[pb.r])
                S.op("act", lambda e, pb=pb, t=t: e.activation(
                    out=ckvT.t[:, :, t * 128:(t + 1) * 128],
                    in_=pb.t[:, 0:256].rearrange("p (c k) -> p c k", c=2), func=AF.Copy),
                    r=[pb.r], w=[r_ckvT[t]])
                S.op("dve", lambda e, pb=pb, t=t: e.tensor_copy(out=kropeT.t[0:64, t * 128:(t + 1) * 128],
                                                                in_=pb.t[0:64, 256:384]),
                     r=[pb.r, kropeT.r], w=[r_ckvT[t]])
            S.flush()

        with ExitStack() as P:
            cs_o, sn_o = cos_sin(P, "cs_own", pos_own, NJ)
            kv_prev = sb(P, "kv_prev", [128, NJ, 512], BF16)
            kv_own = sb(P, "kv_own", [128, NJ, 512], BF16)
            r_kvp = [Res(f"kvp{j}") for j in range(NJ)]
            r_kvo = [Res(f"kvo{j}") for j in range(NJ)]
            nctx = NormCtx(P, "n2", gmix)
            hT = [sb(P, f"hT2_{i}", [128, NKC, 128], BF16) for i in range(2)]
            with ExitStack() as P2:
                wkva = load_w(P2, "wkva", w_kva, NKC, 512)
                wcq = load_w(P2, "wcq", w_cq, NKC, 512)
                wuq = load_w(P2, "wuq", w_uq, 4, 1536)
                gcq = sb(P2, "gcq", [128, 512])
                S.dma("sp", lambda e: e.dma_start(out=gcq.t[:], in_=g_cq_bc), w=[gcq.r])
                for j in range(NJ):
                    h = hT[j % 2]
                    nctx.run(x_prev[j * 128:(j + 1) * 128, :], h, lambda c, h=h: h.t[:, c, :])
                    pf = PF[j % 2]
                    proj(h, lambda c, h=h: h.t[:, c, :], wkva, 0, 512, pf)
                    S.op("act", lambda e, pf=pf, j=j: e.activation(out=kv_prev.t[:, j, :], in_=pf.t[:, 0:512],
                                                                  func=AF.Copy), r=[pf.r], w=[r_kvp[j]])
                ss3 = sb(P2, "ss3", [128, 1])
                rstd3 = sb(P2, "rstd3", [128, 1])
                junk3 = sb(P2, "junk3", [128, 512], BF16)
                cqn = sb(P2, "cqn", [128, 512], BF16)
                cqT = sb(P2, "cqT", [128, 4, 128], BF16)
                qf = sb(P2, "qf", [128, 1536])
                qb = sb(P2, "qb", [128, 1536], BF16)
                ta = sb(P2, "ropa2", [128, 32])
                tb = sb(P2, "ropb2", [128, 32])
                qnT = [sb(P2, f"qnT{i}", [128, 8, 128], BF16) for i in range(2)]
                qrT = [sb(P2, f"qrT{i}", [64, 8, 128], BF16) for i in range(2)]
                for j in range(NJ):
                    h = hT[j % 2]
                    nctx.run(x_own[j * 128:(j + 1) * 128, :], h, lambda c, h=h: h.t[:, c, :])
                    hs = (lambda c, h=h: h.t[:, c, :])
                    proj(h, hs, wkva, 0, 512, PF[0])
                    S.op("act", lambda e, j=j: e.activation(out=kv_own.t[:, j, :], in_=PF[0].t[:, 0:512],
                                                           func=AF.Copy), r=[PF[0].r], w=[r_kvo[j]])
                    proj(h, hs, wcq, 0, 512, PF[1])
                    S.op("act", lambda e: e.activation(out=junk3.t[:], in_=PF[1].t[:, 0:512], func=AF.Square,
                                                       accum_out=ss3.t[:]), r=[PF[1].r], w=[junk3.r, ss3.r])
                    rstd_from_ss(ss3, rstd3, 512)
                    S.op("dve", lambda e: e.scalar_tensor_tensor(out=cqn.t[:], in0=PF[1].t[:, 0:512],
                                                                 scalar=rstd3.t[:, 0:1], in1=gcq.t[:],
                                                                 op0=ALU.mult, op1=ALU.mult),
                         r=[PF[1].r, rstd3.r, gcq.r], w=[cqn.r])
                    for cc in range(4):
                        S.op("pe", lambda e, cc=cc: e.transpose(out=PB[0].t[:, cc * 128:(cc + 1) * 128],
                                                                in_=cqn.t[:, cc * 128:(cc + 1) * 128],
                                                                identity=ident.t[:]),
                             r=[cqn.r, ident.r], w=[PB[0].r])
                    S.op("dve", lambda e: e.tensor_copy(out=cqT.t[:],
                                                        in_=PB[0].t[:, 0:512].rearrange("p (c k) -> p c k", c=4)),
                         r=[PB[0].r], w=[cqT.r])
                    for g3 in range(3):
                        pf = PF[2 + g3]
                        for cc in range(4):
                            S.op("pe", lambda e, cc=cc, g3=g3, pf=pf: e.matmul(
                                pf.t[:, 0:512], lhsT=cqT.t[:, cc, :], rhs=wuq.t[:, cc, g3 * 512:(g3 + 1) * 512],
                                start=(cc == 0), stop=(cc == 3)), r=[cqT.r, wuq.r], w=[pf.r])
                        S.op("act", lambda e, g3=g3, pf=pf: e.activation(out=qf.t[:, g3 * 512:(g3 + 1) * 512],
                                                                        in_=pf.t[:, 0:512], func=AF.Copy),
                             r=[pf.r], w=[qf.r])
                    S.op("act", lambda e: e.activation(out=qb.t[:], in_=qf.t[:], func=AF.Copy), r=[qf.r], w=[qb.r])
                    for hh in range(8):
                        base = hh * 192 + 128
                        rope(lambda lo, hi, base=base: qf.t[:, base + lo:base + hi],
                             lambda lo, hi, base=base: qb.t[:, base + lo:base + hi],
                             cs_o.t[:, j, :], sn_o.t[:, j, :], (ta, tb), qf.r, qb.r, cs_o.r)
                    qn = qnT[j % 2]
                    qr = qrT[j % 2]
                    for hh in range(8):
                        S.op("pe", lambda e, hh=hh: e.transpose(out=PB[1].t[:, hh * 128:(hh + 1) * 128],
                                                                in_=qb.t[:, hh * 192:hh * 192 + 128],
                                                                identity=ident.t[:]),
                             r=[qb.r, ident.r], w=[PB[1].r])
                    S.op("dve", lambda e, qn=qn: e.tensor_copy(
                        out=qn.t[:], in_=PB[1].t[:, 0:1024].rearrange("p (c k) -> p c k", c=8)),
                        r=[PB[1].r], w=[qn.r])
                    for hh in range(8):
                        S.op("pe", lambda e, hh=hh: e.transpose(out=PB[0].t[0:64, hh * 128:(hh + 1) * 128],
                                                                in_=qb.t[:, hh * 192 + 128:hh * 192 + 192],
                                                                identity=ident.t[:]),
                             r=[qb.r, ident.r], w=[PB[0].r])
                    S.op("act", lambda e, qr=qr: e.activation(
                        out=qr.t[:], in_=PB[0].t[0:64, 0:1024].rearrange("p (c k) -> p c k", c=8), func=AF.Copy),
                        r=[PB[0].r], w=[qr.r])
                    S.dma("sp", lambda e, qn=qn, j=j: e.dma_start(
                        out=qnT_d[:, :, j * 128:(j + 1) * 128].rearrange("h p k -> p h k"), in_=qn.t[:]),
                        r=[qn.r], w=[r_qT])
                    S.dma("sp", lambda e, qr=qr, j=j: e.dma_start(
                        out=qrT_d[:, :, j * 128:(j + 1) * 128].rearrange("h p k -> p h k"), in_=qr.t[:]),
                        r=[qr.r], w=[r_qT])
                S.flush()

            with ExitStack() as P3:
                wqa = load_w(P3, "wqa", w_qa, NKC, 1024)
                sinks = sb(P3, "sinks", [128, 16])
                S.dma("sp", lambda e: e.dma_start(out=sinks.t[:], in_=sinks_bc), w=[sinks.r])
                maskb = sb(P3, "maskb", [128, 2, 256])
                S.dma("sp", lambda e: e.dma_start(out=maskb.t[:], in_=swa_maskb.rearrange("a p k -> p a k")),
                      w=[maskb.r])
                posq_i = sb(P3, "posq_i", [128, NJ], I32)
                posq = sb(P3, "posq", [128, NJ])
                S.dma("sp", lambda e: e.dma_start(out=posq_i.t[:], in_=pos_own), w=[posq_i.r])
                S.op("dve", lambda e: e.tensor_copy(out=posq.t[:], in_=posq_i.t[:]), r=[posq_i.r], w=[posq.r])
                posk_i = sb(P3, "posk_i", [128, 256], I32)
                dist = sb(P3, "dist", [128, 256])
                bias = sb(P3, "bias", [128, 16, 256])
                qa = sb(P3, "qa", [128, 1024], BF16)
                qaT = sb(P3, "qaT", [64, 16, 128], BF16)
                kaT = sb(P3, "kaT", [64, 4, 256], BF16)
                L = sb(P3, "swaL", [128, 4, 256])
                mx = sb(P3, "swamx", [128, 4])
                dd = sb(P3, "swadd", [128, 4])
                rsum = sb(P3, "swarsum", [128, 4])
                den = sb(P3, "swaden", [128, 4])
                pexp = sb(P3, "swap", [128, 4, 256], BF16)
                pT = sb(P3, "swapT", [128, 8, 128], BF16)
                so = [sb(P3, f"swao{i}", [128, 1024], BF16) for i in range(2)]
                for j in range(NJ):
                    h = hT[j % 2]
                    hs = (lambda c, h=h: h.t[:, c, :])
                    nctx.run(x_own[j * 128:(j + 1) * 128, :], h, hs)
                    for g2 in range(2):
                        proj(h, hs, wqa, g2 * 512, 512, PF[g2])
                        S.op("act", lambda e, g2=g2: e.activation(out=qa.t[:, g2 * 512:(g2 + 1) * 512],
                                                                 in_=PF[g2].t[:, 0:512], func=AF.Copy),
                             r=[PF[g2].r], w=[qa.r])
                    for half in range(2):
                        for hh in range(8):
                            hd = half * 8 + hh
                            S.op("pe", lambda e, hd=hd, hh=hh, half=half: e.transpose(
                                out=PB[half].t[0:64, hh * 128:(hh + 1) * 128], in_=qa.t[:, hd * 64:(hd + 1) * 64],
                                identity=ident.t[:]), r=[qa.r, ident.r], w=[PB[half].r])
                        S.op("dve", lambda e, half=half: e.tensor_copy(
                            out=qaT.t[:, half * 8:(half + 1) * 8, :],
                            in_=PB[half].t[0:64, 0:1024].rearrange("p (c k) -> p c k", c=8)),
                            r=[PB[half].r], w=[qaT.r])
                    for kvh in range(4):
                        S.op("pe", lambda e, kvh=kvh, j=j: e.transpose(
                            out=PB[0].t[0:64, kvh * 256:kvh * 256 + 128], in_=kv_prev.t[:, j, kvh * 64:(kvh + 1) * 64],
                            identity=ident.t[:]), r=[r_kvp[j], ident.r], w=[PB[0].r])
                        S.op("pe", lambda e, kvh=kvh, j=j: e.transpose(
                            out=PB[0].t[0:64, kvh * 256 + 128:kvh * 256 + 256],
                            in_=kv_own.t[:, j, kvh * 64:(kvh + 1) * 64],
                            identity=ident.t[:]), r=[r_kvo[j], ident.r], w=[PB[0].r])
                    S.op("dve", lambda e: e.tensor_copy(
                        out=kaT.t[:], in_=PB[0].t[0:64, 0:1024].rearrange("p (c k) -> p c k", c=4)),
                        r=[PB[0].r], w=[kaT.r])
                    S.dma("sp", lambda e, j=j: e.dma_start(out=posk_i.t[:], in_=pos_kv_bc[j]), w=[posk_i.r])
                    S.op("dve", lambda e: e.tensor_copy(out=dist.t[:], in_=posk_i.t[:]), r=[posk_i.r], w=[dist.r])
                    S.op("dve", lambda e, j=j: e.tensor_scalar(out=dist.t[:], in0=dist.t[:], scalar1=posq.t[:, j:j + 1],
                                                               scalar2=None, op0=ALU.subtract),
                         r=[dist.r, posq.r], w=[dist.r])
                    S.op("dve", lambda e: e.scalar_tensor_tensor(out=dist.t[:], in0=dist.t[:], scalar=-1.0,
                                                                 in1=dist.t[:], op0=ALU.mult, op1=ALU.max),
                         r=[dist.r], w=[dist.r])
                    mi = 0 if j == 0 else 1
                    r_bh = [Res() for _ in range(16)]
                    for hd in range(16):
                        S.op("dve", lambda e, hd=hd: e.tensor_scalar(
                            out=bias.t[:, hd, :], in0=dist.t[:], scalar1=-slopes[hd], scalar2=None, op0=ALU.mult),
                            r=[dist.r, bias.r], w=[r_bh[hd]])
                    S.op("dve", lambda e, mi=mi: e.tensor_tensor(
                        out=bias.t[:], in0=bias.t[:], in1=maskb.t[:, mi, :].unsqueeze(1).to_broadcast([128, 16, 256]),
                        op=ALU.add), r=r_bh + [maskb.r], w=[bias.r])
                    sot = so[j % 2]
                    for kvh in range(4):
                        for gq in range(4):
                            hd = kvh * 4 + gq
                            pfl = PF[gq // 2]
                            S.op("pe", lambda e, hd=hd, gq=gq, kvh=kvh, pfl=pfl: e.matmul(
                                pfl.t[:, (gq % 2) * 256:(gq % 2) * 256 + 256], lhsT=qaT.t[:, hd, :],
                                rhs=kaT.t[:, kvh, :], start=True, stop=True), r=[qaT.r, kaT.r], w=[pfl.r])
                        for half in range(2):
                            S.op("dve", lambda e, half=half, kvh=kvh: e.scalar_tensor_tensor(
                                out=L.t[:, half * 2:half * 2 + 2, :],
                                in0=PF[half].t[:, 0:512].rearrange("p (a k) -> p a k", a=2), scalar=0.125,
                                in1=bias.t[:, kvh * 4 + half * 2:kvh * 4 + half * 2 + 2, :],
                                op0=ALU.mult, op1=ALU.add), r=[PF[half].r, bias.r], w=[L.r])
                        S.op("dve", lambda e: e.tensor_reduce(out=mx.t[:], in_=L.t[:], axis=AX.X, op=ALU.max),
                             r=[L.r], w=[mx.r])
                        S.op("dve", lambda e, kvh=kvh: e.tensor_tensor(out=mx.t[:], in0=mx.t[:],
                                                                       in1=sinks.t[:, kvh * 4:kvh * 4 + 4], op=ALU.max),
                             r=[mx.r, sinks.r], w=[mx.r])
                        S.op("dve", lambda e, kvh=kvh: e.tensor_tensor(out=dd.t[:], in0=sinks.t[:, kvh * 4:kvh * 4 + 4],
                                                                       in1=mx.t[:], op=ALU.subtract),
                             r=[mx.r, sinks.r], w=[dd.r])
                        S.op("dve", lambda e: e.tensor_scalar(out=mx.t[:], in0=mx.t[:], scalar1=-1.0, scalar2=None,
                                                              op0=ALU.mult), r=[mx.r], w=[mx.r])
                        for gq in range(4):
                            S.op("act", lambda e, gq=gq: e.activation(out=pexp.t[:, gq, :], in_=L.t[:, gq, :],
                                                                      func=AF.Exp, bias=mx.t[:, gq:gq + 1], scale=1.0,
                                                                      accum_out=rsum.t[:, gq:gq + 1]),
                                 r=[L.r, mx.r], w=[pexp.r, rsum.r])
                        S.op("act", lambda e: e.activation(out=dd.t[:], in_=dd.t[:], func=AF.Exp), r=[dd.r], w=[dd.r])
                        S.op("dve", lambda e: e.tensor_tensor(out=den.t[:], in0=rsum.t[:], in1=dd.t[:], op=ALU.add),
                             r=[rsum.r, dd.r], w=[den.r])
                        S.op("dve", lambda e: e.reciprocal(out=den.t[:], in_=den.t[:]), r=[den.r], w=[den.r])
                        for gq in range(4):
                            for kb in range(2):
                                S.op("pe", lambda e, gq=gq, kb=kb: e.transpose(
                                    out=PB[1].t[:, (gq * 2 + kb) * 128:(gq * 2 + kb + 1) * 128],
                                    in_=pexp.t[:, gq, kb * 128:(kb + 1) * 128], identity=ident.t[:]),
                                    r=[pexp.r, ident.r], w=[PB[1].r])
                        S.op("act", lambda e: e.activation(
                            out=pT.t[:], in_=PB[1].t[:, 0:1024].rearrange("p (c k) -> p c k", c=8), func=AF.Copy),
                            r=[PB[1].r], w=[pT.r])
                        po = PF[2 + kvh % 2]
                        for gq in range(4):
                            S.op("pe", lambda e, gq=gq, kvh=kvh, j=j, po=po: e.matmul(
                                po.t[:, gq * 64:(gq + 1) * 64], lhsT=pT.t[:, gq * 2, :],
                                rhs=kv_prev.t[:, j, 256 + kvh * 64:256 + (kvh + 1) * 64], start=True, stop=False),
                                r=[pT.r, r_kvp[j]], w=[po.r])
                            S.op("pe", lambda e, gq=gq, kvh=kvh, j=j, po=po: e.matmul(
                                po.t[:, gq * 64:(gq + 1) * 64], lhsT=pT.t[:, gq * 2 + 1, :],
                                rhs=kv_own.t[:, j, 256 + kvh * 64:256 + (kvh + 1) * 64], start=False, stop=True),
                                r=[pT.r, r_kvo[j]], w=[po.r])
                        for gq in range(4):
                            hd = kvh * 4 + gq
                            S.op("dve", lambda e, gq=gq, hd=hd, po=po, sot=sot: e.tensor_scalar(
                                out=sot.t[:, hd * 64:(hd + 1) * 64], in0=po.t[:, gq * 64:(gq + 1) * 64],
                                scalar1=den.t[:, gq:gq + 1], scalar2=None, op0=ALU.mult),
                                r=[po.r, den.r], w=[sot.r])
                    S.dma("sp", lambda e, j=j, sot=sot: e.dma_start(out=swa_d[j * 128:(j + 1) * 128, :], in_=sot.t[:]),
                          r=[sot.r], w=[r_swa[j]])
                S.flush()

        with ExitStack() as P:
            nctx = NormCtx(P, "n3", gmix)
            hTo = sb(P, "hTo", [128, NKC, TK], BF16)
            r_hTo = [Res(f"hTo{j}") for j in range(NJ)]
            for j in range(NJ):
                hj = Tl(hTo.t, "x")
                hj.r = r_hTo[j]
                nctx.run(x_own[j * 128:(j + 1) * 128, :], hj, lambda c, j=j: hTo.t[:, c, j * 128:(j + 1) * 128])
            wg = [sb(P, f"wg{i}", [128, NKC, 512], BF16) for i in range(2)]
            stg = [sb(P, f"stg{i}", [128, 512]) for i in range(3)]
            si = 0
            for g8 in range(8):
                w = wg[g8 % 2]
                srcw = w_gate[:, g8 * 512:(g8 + 1) * 512].rearrange("(c p) n -> p c n", p=128)
                for c0 in range(0, NKC, 8):
                    S.dma("pool", lambda e, w=w, srcw=srcw, c0=c0: e.dma_start(out=w.t[:, c0:c0 + 8, :],
                                                                              in_=srcw[:, c0:c0 + 8, :]), w=[w.r])
                for j in range(NJ):
                    pf = PF[(g8 * NJ + j) % 4]
                    hj = Tl(hTo.t, "x")
                    hj.r = r_hTo[j]
                    proj(hj, lambda c, j=j: hTo.t[:, c, j * 128:(j + 1) * 128], w, 0, 512, pf)
                    st = stg[si % 3]
                    si += 1
                    if si % 2:
                        S.op("act", lambda e, st=st, pf=pf: e.activation(out=st.t[:], in_=pf.t[:, 0:512], func=AF.Copy),
                             r=[pf.r], w=[st.r])
                    else:
                        S.op("dve", lambda e, st=st, pf=pf: e.tensor_copy(out=st.t[:], in_=pf.t[:, 0:512]),
                             r=[pf.r], w=[st.r])
                    S.dma("sp", lambda e, st=st, j=j, g8=g8: e.dma_start(
                        out=zg_d[j * 128:(j + 1) * 128, g8 * 512:(g8 + 1) * 512], in_=st.t[:]),
                        r=[st.r], w=[r_zg[j]])
            S.flush()

        with ExitStack() as P:
            wk = load_w(P, "wukvk", w_ukv_k, 2, 1024)
            wv = load_w(P, "wukvv", w_ukv_v, 2, 1024)
            mmask = sb(P, "mmask", [128, 512])
            S.dma("sp", lambda e: e.dma_start(out=mmask.t[:], in_=mla_mask), w=[mmask.r])
            for tab, c0 in ((u_tab, 0), (v_tab, D)):
                for i in range(16):
                    S.dma("pool", lambda e, tab=tab, c0=c0, i=i: e.dma_start(
                        out=uv_b[i * 1024:(i + 1) * 1024, c0:c0 + D], in_=tab[i * 1024:(i + 1) * 1024, :]),
                        w=[r_uvb])
            knT = sb(P, "knT", [128, 2, SQ], BF16)
            Vsb = sb(P, "Vsb", [128, NS, 256], BF16)
            qn2 = sb(P, "qn2", [128, 2, TK], BF16)
            qr2 = sb(P, "qr2", [128, 2, TK], BF16)
            qr2z = Res("qr2z")
            S.op("dve", lambda e: e.memset(qr2.t[64:128, :, :], 0.0), w=[qr2z])
            SETS = []
            for bs in range(2):
                SETS.append(dict(
                    S1=PF[0 + 3 * bs], S2=PF[1 + 3 * bs], O=PF[2 + 3 * bs], PT=PB[bs],
                    rmax=sb(P, f"rmax{bs}", [128, NJ]), rsm=sb(P, f"rsm{bs}", [128, NJ]),
                    mrow=sb(P, f"mrow{bs}", [128, 1]), lrow=sb(P, f"lrow{bs}", [128, 1]),
                    Pb=[sb(P, f"Pb{bs}_{i}", [128, 512], BF16) for i in range(2)],
                    Pf=sb(P, f"Pf{bs}", [128, 512]),
                    PTs=[sb(P, f"PTs{bs}_{i}", [128, 4, 128], BF16) for i in range(2)]))
            mo = [sb(P, f"mo{i}", [128, 256], BF16) for i in range(2)]
            allk = list(r_ckvT)

            def interleave(gens):
                gens = list(gens)
                while gens:
                    for g in list(gens):
                        try:
                            next(g)
                        except StopIteration:
                            gens.remove(g)

            def mla_unit(j, hh, B, mot):
                qs = slice(j * 128, (j + 1) * 128)
                rmax, rsm, mrow, lrow = B["rmax"], B["rsm"], B["mrow"], B["lrow"]

                def scores(c, pf):
                    S.op("pe", lambda e: e.matmul(pf.t[:, 0:512], lhsT=qn2.t[:, hh, qs],
                                                  rhs=knT.t[:, hh, c * 512:(c + 1) * 512], start=True,
                                                  stop=False), r=[qn2.r, knT.r], w=[pf.r])
                    S.op("pe", lambda e: e.matmul(pf.t[:, 0:512], lhsT=qr2.t[:, hh, qs],
                                                  rhs=kropeT.t[:, c * 512:(c + 1) * 512], start=False,
                                                  stop=True), r=[qr2.r, qr2z, kropeT.r] + allk[c * 4:c * 4 + 4], w=[pf.r])
                for c in range(j + 1):
                    pf = B["S1"]
                    scores(c, pf)
                    yield
                    S.op("dve", lambda e, c=c, pf=pf: e.tensor_reduce(out=rmax.t[:, c:c + 1], in_=pf.t[:, 0:512],
                                                                     axis=AX.X, op=ALU.max), r=[pf.r], w=[rmax.r])
                S.op("dve", lambda e: e.tensor_reduce(out=mrow.t[:], in_=rmax.t[:, 0:j + 1], axis=AX.X,
                                                      op=ALU.max), r=[rmax.r], w=[mrow.r])
                S.op("dve", lambda e: e.tensor_scalar(out=mrow.t[:], in0=mrow.t[:], scalar1=-MLA_SCALE,
                                                      scalar2=None, op0=ALU.mult), r=[mrow.r], w=[mrow.r])
                po = B["O"]
                for c in range(j + 1):
                    pf = B["S2"]
                    scores(c, pf)
                    yield
                    pbt = B["Pb"][c % 2]
                    if c < j:
                        S.op("act", lambda e, c=c, pf=pf, pbt=pbt: e.activation(
                            out=pbt.t[:], in_=pf.t[:, 0:512], func=AF.Exp, bias=mrow.t[:, 0:1],
                            scale=MLA_SCALE, accum_out=rsm.t[:, c:c + 1]),
                            r=[pf.r, mrow.r], w=[pbt.r, rsm.r])
                    else:
                        Pf = B["Pf"]
                        S.op("act", lambda e, pf=pf, Pf=Pf: e.activation(
                            out=Pf.t[:], in_=pf.t[:, 0:512], func=AF.Exp, bias=mrow.t[:, 0:1],
                            scale=MLA_SCALE), r=[pf.r, mrow.r], w=[Pf.r])
                        S.op("dve", lambda e, Pf=Pf: e.tensor_tensor(out=Pf.t[:], in0=Pf.t[:], in1=mmask.t[:],
                                                                     op=ALU.mult), r=[Pf.r, mmask.r], w=[Pf.r])
                        S.op("dve", lambda e, c=c, Pf=Pf: e.tensor_reduce(out=rsm.t[:, c:c + 1], in_=Pf.t[:],
                                                                          axis=AX.X, op=ALU.add),
                             r=[Pf.r], w=[rsm.r])
                        S.op("act", lambda e, pbt=pbt, Pf=Pf: e.activation(out=pbt.t[:], in_=Pf.t[:], func=AF.Copy),
                             r=[Pf.r], w=[pbt.r])
                    ptp = B["PT"]
                    for kk in range(4):
                        S.op("pe", lambda e, kk=kk, pbt=pbt, ptp=ptp: e.transpose(
                            out=ptp.t[:, kk * 128:(kk + 1) * 128], in_=pbt.t[:, kk * 128:(kk + 1) * 128],
                            identity=ident.t[:]), r=[pbt.r, ident.r], w=[ptp.r])
                    yield
                    pts = B["PTs"][c % 2]
                    if c % 2:
                        S.op("act", lambda e, pts=pts, ptp=ptp: e.activation(
                            out=pts.t[:], in_=ptp.t[:, 0:512].rearrange("p (c k) -> p c k", c=4), func=AF.Copy),
                            r=[ptp.r], w=[pts.r])
                    else:
                        S.op("dve", lambda e, pts=pts, ptp=ptp: e.tensor_copy(
                            out=pts.t[:], in_=ptp.t[:, 0:512].rearrange("p (c k) -> p c k", c=4)),
                            r=[ptp.r], w=[pts.r])
                    for kk in range(4):
                        S.op("pe", lambda e, kk=kk, c=c, pts=pts: e.matmul(
                            po.t[:, 0:128], lhsT=pts.t[:, kk, :],
                            rhs=Vsb.t[:, c * 4 + kk, hh * 128:(hh + 1) * 128],
                            start=(c == 0 and kk == 0), stop=(c == j and kk == 3)),
                            r=[pts.r, Vsb.r], w=[po.r])
                    yield
                S.op("dve", lambda e: e.tensor_reduce(out=lrow.t[:], in_=rsm.t[:, 0:j + 1], axis=AX.X,
                                                      op=ALU.add), r=[rsm.r], w=[lrow.r])
                S.op("dve", lambda e: e.reciprocal(out=lrow.t[:], in_=lrow.t[:]), r=[lrow.r], w=[lrow.r])
                S.op("dve", lambda e: e.tensor_scalar(
                    out=mot.t[:, hh * 128:(hh + 1) * 128], in0=po.t[:, 0:128], scalar1=lrow.t[:, 0:1],
                    scalar2=None, op0=ALU.mult), r=[po.r, lrow.r], w=[mot.r])

            for hp in range(4):
                for hh in range(2):
                    hd = hp * 2 + hh
                    for kc in range(NS // 4):
                        pf = PF[(kc % 2) * 3 + 1]
                        for cc in range(2):
                            S.op("pe", lambda e, cc=cc, hd=hd, kc=kc, pf=pf: e.matmul(
                                pf.t[:, 0:512], lhsT=wk.t[:, cc, hd * 128:(hd + 1) * 128],
                                rhs=ckvT.t[:, cc, kc * 512:(kc + 1) * 512], start=(cc == 0), stop=(cc == 1)),
                                r=[wk.r] + allk[kc * 4:kc * 4 + 4], w=[pf.r])
                        if kc % 2:
                            S.op("act", lambda e, hh=hh, kc=kc, pf=pf: e.activation(
                                out=knT.t[:, hh, kc * 512:(kc + 1) * 512], in_=pf.t[:, 0:512], func=AF.Copy),
                                r=[pf.r], w=[knT.r])
                        else:
                            S.op("dve", lambda e, hh=hh, kc=kc, pf=pf: e.tensor_copy(
                                out=knT.t[:, hh, kc * 512:(kc + 1) * 512], in_=pf.t[:, 0:512]),
                                r=[pf.r], w=[knT.r])
                for kt in range(NS):
                    pf = PF[(kt % 2) * 3 + 2]
                    for cc in range(2):
                        S.op("pe", lambda e, cc=cc, kt=kt, pf=pf, hp=hp: e.matmul(
                            pf.t[:, 0:256], lhsT=ckvT.t[:, cc, kt * 128:(kt + 1) * 128],
                            rhs=wv.t[:, cc, hp * 256:(hp + 1) * 256], start=(cc == 0), stop=(cc == 1)),
                            r=[wv.r, allk[kt]], w=[pf.r])
                    if kt % 2:
                        S.op("act", lambda e, kt=kt, pf=pf: e.activation(
                            out=Vsb.t[:, kt, :], in_=pf.t[:, 0:256], func=AF.Copy), r=[pf.r], w=[Vsb.r])
                    else:
                        S.op("dve", lambda e, kt=kt, pf=pf: e.tensor_copy(out=Vsb.t[:, kt, :], in_=pf.t[:, 0:256]),
                             r=[pf.r], w=[Vsb.r])
                S.dma("sp", lambda e, hp=hp: e.dma_start(out=qn2.t[:], in_=qnT_d[hp * 2:hp * 2 + 2].rearrange(
                    "h p k -> p h k")), r=[r_qT], w=[qn2.r])
                S.dma("sp", lambda e, hp=hp: e.dma_start(out=qr2.t[0:64, :, :], in_=qrT_d[hp * 2:hp * 2 + 2].rearrange(
                    "h p k -> p h k")), r=[r_qT, qr2z], w=[qr2.r])
                for j in range(NJ):
                    mot = mo[j % 2]
                    interleave([mla_unit(j, 0, SETS[0], mot), mla_unit(j, 1, SETS[1], mot)])
                    S.dma("sp", lambda e, j=j, hp=hp, mot=mot: e.dma_start(
                        out=mla_d[j * 128:(j + 1) * 128, hp * 256:(hp + 1) * 256], in_=mot.t[:]),
                        r=[mot.r], w=[r_mla[j]])
                S.flush()

        GK.close()
        with ExitStack() as P:
            r_mixd = [Res(f"mixd{j}") for j in range(NJ)]
            with ExitStack() as P5:
                wa = load_w(P5, "wa", w_a, 8, D)
                wb = load_w(P5, "wb", w_b, 8, D)
                at = [sb(P5, f"at{i}", [128, 1024], BF16) for i in range(2)]
                bt = [sb(P5, f"bt{i}", [128, 1024], BF16) for i in range(2)]
                aT = sb(P5, "aT", [128, 8, 128], BF16)
                bT = sb(P5, "bT", [128, 8, 128], BF16)
                gt = [sb(P5, f"gt{i}", [128, 4096]) for i in range(2)]
                t1 = sb(P5, "mix_t1", [128, 512])
                mixeds = [sb(P5, f"mixed{i}", [128, D], BF16) for i in range(2)]
                for j in range(NJ):
                    mixed = mixeds[j % 2]
                    a_, b_, g_ = at[j % 2], bt[j % 2], gt[j % 2]
                    S.dma("sp", lambda e, j=j, a_=a_: e.dma_start(out=a_.t[:], in_=swa_d[j * 128:(j + 1) * 128, :]),
                          r=[r_swa[j]], w=[a_.r])
                    S.dma("sp", lambda e, j=j, b_=b_: e.dma_start(out=b_.t[:], in_=mla_d[j * 128:(j + 1) * 128, :]),
                          r=[r_mla[j]], w=[b_.r])
                    S.dma("sp", lambda e, j=j, g_=g_: e.dma_start(out=g_.t[:], in_=zg_d[j * 128:(j + 1) * 128, :]),
                          r=[r_zg[j]], w=[g_.r])
                    S.op("act", lambda e, g_=g_: e.activation(out=g_.t[:], in_=g_.t[:], func=AF.Sigmoid),
                         r=[g_.r], w=[g_.r])
                    for src, dstT, pb in ((a_, aT, PB[0]), (b_, bT, PB[1])):
                        for cc in range(8):
                            S.op("pe", lambda e, cc=cc, src=src, pb=pb: e.transpose(
                                out=pb.t[:, cc * 128:(cc + 1) * 128], in_=src.t[:, cc * 128:(cc + 1) * 128],
                                identity=ident.t[:]), r=[src.r, ident.r], w=[pb.r])
                        S.op("dve", lambda e, dstT=dstT, pb=pb: e.tensor_copy(
                            out=dstT.t[:], in_=pb.t[:, 0:1024].rearrange("p (c k) -> p c k", c=8)),
                            r=[pb.r], w=[dstT.r])
                    for g4 in range(4):
                        pa, pbb = PF[(g4 % 2) * 2], PF[(g4 % 2) * 2 + 1]
                        for cc in range(8):
                            S.op("pe", lambda e, cc=cc, g4=g4, pa=pa: e.matmul(
                                pa.t[:, 0:512], lhsT=aT.t[:, cc, :], rhs=wa.t[:, cc, g4 * 512:(g4 + 1) * 512],
                                start=(cc == 0), stop=(cc == 7)), r=[aT.r, wa.r], w=[pa.r])
                        for cc in range(8):
                            S.op("pe", lambda e, cc=cc, g4=g4, pbb=pbb: e.matmul(
                                pbb.t[:, 0:512], lhsT=bT.t[:, cc, :], rhs=wb.t[:, cc, g4 * 512:(g4 + 1) * 512],
                                start=(cc == 0), stop=(cc == 7)), r=[bT.r, wb.r], w=[pbb.r])
                        S.op("dve", lambda e, g4=g4, pa=pa, g_=g_: e.tensor_tensor(
                            out=t1.t[:], in0=pa.t[:, 0:512], in1=g_.t[:, g4 * 512:(g4 + 1) * 512], op=ALU.mult),
                            r=[pa.r, g_.r], w=[t1.r])
                        S.op("dve", lambda e, g4=g4, pbb=pbb, g_=g_: e.tensor_tensor(
                            out=g_.t[:, 2048 + g4 * 512:2048 + (g4 + 1) * 512], in0=pbb.t[:, 0:512],
                            in1=g_.t[:, 2048 + g4 * 512:2048 + (g4 + 1) * 512], op=ALU.mult),
                            r=[pbb.r, g_.r], w=[g_.r])
                        S.op("dve", lambda e, g4=g4, g_=g_, mixed=mixed: e.tensor_tensor(
                            out=mixed.t[:, g4 * 512:(g4 + 1) * 512], in0=t1.t[:],
                            in1=g_.t[:, 2048 + g4 * 512:2048 + (g4 + 1) * 512], op=ALU.add),
                            r=[t1.r, g_.r], w=[mixed.r])
                    S.dma("sp", lambda e, j=j, mixed=mixed: e.dma_start(out=mixed_d[j * 128:(j + 1) * 128, :],
                                                                        in_=mixed.t[:]), r=[mixed.r], w=[r_mixd[j]])
                S.flush()
            with ExitStack() as P5:
                wo = load_w(P5, "wo", w_o, NKC, D)
                xt = [sb(P5, f"x5_{i}", [128, D]) for i in range(2)]
                mxs = [sb(P5, f"mx5_{i}", [128, D], BF16) for i in range(2)]
                mixTs = [sb(P5, f"mixT5_{i}", [128, NKC, 128], BF16) for i in range(2)]
                for j in range(NJ):
                    x_ = xt[j % 2]
                    mx_ = mxs[j % 2]
                    mixT = mixTs[j % 2]
                    S.dma("sp", lambda e, j=j, x_=x_: e.dma_start(out=x_.t[:], in_=x_own[j * 128:(j + 1) * 128, :]),
                          w=[x_.r])
                    S.dma("sp", lambda e, j=j, mx_=mx_: e.dma_start(out=mx_.t[:], in_=mixed_d[j * 128:(j + 1) * 128, :]),
                          r=[r_mixd[j]], w=[mx_.r])
                    for half in range(2):
                        pb = PB[half]
                        for cc in range(8):
                            c = half * 8 + cc
                            S.op("pe", lambda e, c=c, cc=cc, pb=pb, mx_=mx_: e.transpose(
                                out=pb.t[:, cc * 128:(cc + 1) * 128], in_=mx_.t[:, c * 128:(c + 1) * 128],
                                identity=ident.t[:]), r=[mx_.r, ident.r], w=[pb.r])
                        S.op("act", lambda e, half=half, pb=pb, mixT=mixT: e.activation(
                            out=mixT.t[:, half * 8:(half + 1) * 8, :],
                            in_=pb.t[:, 0:1024].rearrange("p (c k) -> p c k", c=8), func=AF.Copy),
                            r=[pb.r], w=[mixT.r])
                    for g4 in range(4):
                        pf = PF[g4]
                        for c in range(NKC):
                            S.op("pe", lambda e, c=c, g4=g4, pf=pf, mixT=mixT: e.matmul(
                                pf.t[:, 0:512], lhsT=mixT.t[:, c, :],
                                rhs=wo.t[:, c, g4 * 512:(g4 + 1) * 512], start=(c == 0), stop=(c == NKC - 1)),
                                r=[mixT.r, wo.r], w=[pf.r])
                        S.op("dve", lambda e, g4=g4, pf=pf, x_=x_: e.tensor_tensor(
                            out=x_.t[:, g4 * 512:(g4 + 1) * 512], in0=pf.t[:, 0:512],
                            in1=x_.t[:, g4 * 512:(g4 + 1) * 512], op=ALU.add), r=[pf.r, x_.r], w=[x_.r])
                    S.dma("sp", lambda e, j=j, x_=x_: e.dma_start(out=x1_d[j * 128:(j + 1) * 128, :], in_=x_.t[:]),
                          r=[x_.r], w=[r_x1[j]])
                S.flush()

        with ExitStack() as P:
            idx_all = sb(P, "idx_all", [128, NJ, 128], I32)
            gate_all = sb(P, "gate_all", [128, NJ, 128])
            r_idx = [Res(f"idx{j}") for j in range(NJ)]
            with ExitStack() as P6:
                wpq = load_w(P6, "wpq", w_pq, NKC, D)
                kT_f = sb(P6, "kT_f", [128, 16, 128])
                kT = sb(P6, "kT_b", [128, 16, 128], BF16)
                S.dma("sp", lambda e: e.dma_start(out=kT_f.t[:], in_=keysT), w=[kT_f.r])
                S.op("dve", lambda e: e.tensor_copy(out=kT.t[:], in_=kT_f.t[:]), r=[kT_f.r], w=[kT.r])
                gffn = sb(P6, "gffn", [128, D])
                S.dma("sp", lambda e: e.dma_start(out=gffn.t[:], in_=g_ffn_bc), w=[gffn.r])
                iota16 = sb(P6, "iota16", [128, 16])
                S.op("pool", lambda e: e.iota(iota16.t[:], pattern=[[1, 16]], base=0, channel_multiplier=0,
                                              allow_small_or_imprecise_dtypes=True), w=[iota16.r])
                xt = [sb(P6, f"x6_{i}", [128, D]) for i in range(2)]
                junk = sb(P6, "junk6", [128, D], BF16)
                ss = sb(P6, "ss6", [128, 1])
                rstd = sb(P6, "rstd6", [128, 1])
                hn = [sb(P6, f"hn6_{i}", [128, D]) for i in range(2)]
                hnb = sb(P6, "hnb6", [128, D], BF16)
                hT6 = sb(P6, "hT6", [128, NKC, 128], BF16)
                qpT = [sb(P6, f"qpT{i}", [128, 128], BF16) for i in range(2)]
                sc = sb(P6, "sc6", [128, 16, 128])
                scw = sb(P6, "scw6", [128, 16, 128])
                tops = sb(P6, "tops", [128, 16, 16])
                topi = sb(P6, "topi", [128, 16, 16], U32)
                topif = sb(P6, "topif", [128, 16, 16])
                cand = sb(P6, "cand", [128, 8, 256])
                candw = sb(P6, "candw", [128, 8, 256])
                best = sb(P6, "best", [128, 8, 16])
                bpos = sb(P6, "bpos", [128, 8, 16], U32)
                k1u = sb(P6, "k1u", [128, 8, 16], U32)
                k2u = sb(P6, "k2u", [128, 8, 16], U32)
                k1f = sb(P6, "k1f", [128, 8, 16])
                k2f = sb(P6, "k2f", [128, 8, 16])
                ohs = [sb(P6, f"oh6_{i}", [128, 16, 16]) for i in range(16)]
                i1 = sb(P6, "i1", [128, 8, 16])
                i2 = sb(P6, "i2", [128, 8, 16])
                bm = sb(P6, "bm", [128, 8])
                bs = sb(P6, "bs", [128, 8])
                for j in range(NJ):
                    x_ = xt[j % 2]
                    hn_ = hn[j % 2]
                    S.dma("sp", lambda e, j=j, x_=x_: e.dma_start(out=x_.t[:], in_=x1_d[j * 128:(j + 1) * 128, :]),
                          r=[r_x1[j]], w=[x_.r])
                    S.op("act", lambda e, x_=x_: e.activation(out=junk.t[:], in_=x_.t[:], func=AF.Square,
                                                              accum_out=ss.t[:]), r=[x_.r], w=[junk.r, ss.r])
                    rstd_from_ss(ss, rstd, D)
                    S.op("dve", lambda e, x_=x_, hn_=hn_: e.scalar_tensor_tensor(
                        out=hn_.t[:], in0=x_.t[:], scalar=rstd.t[:, 0:1], in1=gffn.t[:], op0=ALU.mult, op1=ALU.mult),
                        r=[x_.r, rstd.r, gffn.r], w=[hn_.r])
                    S.dma("sp", lambda e, j=j, hn_=hn_: e.dma_start(out=hn2_d[j * 128:(j + 1) * 128, :], in_=hn_.t[:]),
                          r=[hn_.r], w=[r_hn2[j]])
                    S.op("act", lambda e, hn_=hn_: e.activation(out=hnb.t[:], in_=hn_.t[:], func=AF.Copy),
                         r=[hn_.r], w=[hnb.r])
                    for half in range(2):
                        pb = PB[half]
                        for cc in range(8):
                            c = half * 8 + cc
                            S.op("pe", lambda e, c=c, cc=cc, pb=pb: e.transpose(
                                out=pb.t[:, cc * 128:(cc + 1) * 128], in_=hnb.t[:, c * 128:(c + 1) * 128],
                                identity=ident.t[:]), r=[hnb.r, ident.r], w=[pb.r])
                        S.op("dve" if half else "act", (lambda e, half=half, pb=pb: e.tensor_copy(
                            out=hT6.t[:, half * 8:(half + 1) * 8, :],
                            in_=pb.t[:, 0:1024].rearrange("p (c k) -> p c k", c=8))) if half else
                            (lambda e, half=half, pb=pb: e.activation(
                                out=hT6.t[:, half * 8:(half + 1) * 8, :],
                                in_=pb.t[:, 0:1024].rearrange("p (c k) -> p c k", c=8), func=AF.Copy)),
                            r=[pb.r], w=[hT6.r])
                    for hp_ in range(16):
                        pq = PF[4 + hp_ % 2]
                        for c in range(NKC):
                            S.op("pe", lambda e, c=c, hp_=hp_, pq=pq: e.matmul(
                                pq.t[:, 0:128], lhsT=wpq.t[:, c, hp_ * 128:(hp_ + 1) * 128], rhs=hT6.t[:, c, :],
                                start=(c == 0), stop=(c == NKC - 1)), r=[wpq.r, hT6.r], w=[pq.r])
                        qp = qpT[hp_ % 2]
                        S.op("act", lambda e, pq=pq, qp=qp: e.activation(out=qp.t[:], in_=pq.t[:, 0:128], func=AF.Copy),
                             r=[pq.r], w=[qp.r])
                        psc = PF[hp_ // 4]
                        S.op("pe", lambda e, hp_=hp_, qp=qp, psc=psc: e.matmul(
                            psc.t[:, (hp_ % 4) * 128:(hp_ % 4 + 1) * 128], lhsT=qp.t[:], rhs=kT.t[:, hp_, :],
                            start=True, stop=True), r=[qp.r, kT.r], w=[psc.r])
                    r_sc = [Res() for _ in range(16)]
                    r_scw = [Res() for _ in range(16)]
                    r_tops = [Res() for _ in range(16)]
                    r_topi = [Res() for _ in range(16)]
                    for q4 in range(4):
                        S.op("dve", lambda e, q4=q4: e.tensor_copy(
                            out=sc.t[:, q4 * 4:(q4 + 1) * 4, :],
                            in_=PF[q4].t[:, 0:512].rearrange("p (a k) -> p a k", a=4)),
                            r=[PF[q4].r, sc.r], w=[r_sc[q4 * 4 + i] for i in range(4)])
                    for hp_ in range(16):
                        S.op("dve", lambda e, hp_=hp_: e.max(out=tops.t[:, hp_, 0:8], in_=sc.t[:, hp_, :]),
                             r=[r_sc[hp_], tops.r], w=[r_tops[hp_]])
                    for hp_ in range(16):
                        S.op("dve", lambda e, hp_=hp_: e.max_index(out=topi.t[:, hp_, 0:8], in_max=tops.t[:, hp_, 0:8],
                                                                   in_values=sc.t[:, hp_, :]),
                             r=[r_sc[hp_], r_tops[hp_], topi.r], w=[r_topi[hp_]])
                    for hp_ in range(16):
                        S.op("dve", lambda e, hp_=hp_: e.match_replace(out=scw.t[:, hp_, :],
                                                                       in_to_replace=tops.t[:, hp_, 0:8],
                                                                       in_values=sc.t[:, hp_, :], imm_value=-1e30),
                             r=[r_sc[hp_], r_tops[hp_], scw.r], w=[r_scw[hp_]])
                    for hp_ in range(16):
                        S.op("dve", lambda e, hp_=hp_: e.max(out=tops.t[:, hp_, 8:16], in_=scw.t[:, hp_, :]),
                             r=[r_scw[hp_]], w=[r_tops[hp_]])
                    for hp_ in range(16):
                        S.op("dve", lambda e, hp_=hp_: e.max_index(out=topi.t[:, hp_, 8:16],
                                                                   in_max=tops.t[:, hp_, 8:16],
                                                                   in_values=scw.t[:, hp_, :]),
                             r=[r_scw[hp_], r_tops[hp_]], w=[r_topi[hp_]])
                    S.op("dve", lambda e: e.tensor_copy(out=topif.t[:], in_=topi.t[:]), r=r_topi, w=[topif.r, topi.r])
                    r_cand = [Res() for _ in range(8)]
                    r_candw = [Res() for _ in range(8)]
                    r_best = [Res() for _ in range(8)]
                    r_bpos = [Res() for _ in range(8)]
                    for hd in range(8):
                        S.op("dve", lambda e, hd=hd: e.tensor_tensor(
                            out=cand.t[:, hd, :].rearrange("p (a b) -> p a b", a=16),
                            in0=tops.t[:, 2 * hd, :].unsqueeze(2).to_broadcast([128, 16, 16]),
                            in1=tops.t[:, 2 * hd + 1, :].unsqueeze(1).to_broadcast([128, 16, 16]), op=ALU.add),
                            r=[r_tops[2 * hd], r_tops[2 * hd + 1], cand.r], w=[r_cand[hd]])
                    for hd in range(8):
                        S.op("dve", lambda e, hd=hd: e.max(out=best.t[:, hd, 0:8], in_=cand.t[:, hd, :]),
                             r=[r_cand[hd], best.r], w=[r_best[hd]])
                    for hd in range(8):
                        S.op("dve", lambda e, hd=hd: e.max_index(out=bpos.t[:, hd, 0:8], in_max=best.t[:, hd, 0:8],
                                                                 in_values=cand.t[:, hd, :]),
                             r=[r_cand[hd], r_best[hd], bpos.r], w=[r_bpos[hd]])
                    for hd in range(8):
                        S.op("dve", lambda e, hd=hd: e.match_replace(out=candw.t[:, hd, :],
                                                                     in_to_replace=best.t[:, hd, 0:8],
                                                                     in_values=cand.t[:, hd, :], imm_value=-1e30),
                             r=[r_cand[hd], r_best[hd], candw.r], w=[r_candw[hd]])
                    for hd in range(8):
                        S.op("dve", lambda e, hd=hd: e.max(out=best.t[:, hd, 8:16], in_=candw.t[:, hd, :]),
                             r=[r_candw[hd]], w=[r_best[hd]])
                    for hd in range(8):
                        S.op("dve", lambda e, hd=hd: e.max_index(out=bpos.t[:, hd, 8:16], in_max=best.t[:, hd, 8:16],
                                                                 in_values=candw.t[:, hd, :]),
                             r=[r_candw[hd], r_best[hd]], w=[r_bpos[hd]])
                    S.op("dve", lambda e: e.tensor_copy(out=k1f.t[:, 0, 0:1], in_=k1f.t[:, 0, 0:1]),
                         r=r_bpos + r_best + r_tops + r_cand + r_candw + r_sc + r_scw,
                         w=[bpos.r, best.r, tops.r, cand.r, candw.r, sc.r, scw.r, k1f.r])
                    S.op("dve", lambda e: e.tensor_single_scalar(out=k1u.t[:], in_=bpos.t[:], scalar=4,
                                                                 op=ALU.logical_shift_right), r=[bpos.r], w=[k1u.r])
                    S.op("dve", lambda e: e.tensor_single_scalar(out=k2u.t[:], in_=bpos.t[:], scalar=15,
                                                                 op=ALU.bitwise_and), r=[bpos.r], w=[k2u.r])
                    S.op("dve", lambda e: e.tensor_copy(out=k1f.t[:], in_=k1u.t[:]), r=[k1u.r], w=[k1f.r])
                    S.op("dve", lambda e: e.tensor_copy(out=k2f.t[:], in_=k2u.t[:]), r=[k2u.r], w=[k2f.r])
                    combos = [(hd, side) for hd in range(8) for side in range(2)]
                    r_io = [Res() for _ in combos]
                    for ci, (hd, side) in enumerate(combos):
                        kf_ = k1f if side == 0 else k2f
                        o = ohs[ci]
                        S.op("dve", lambda e, hd=hd, kf_=kf_, o=o: e.tensor_tensor(
                            out=o.t[:], in0=iota16.t[:].unsqueeze(1).to_broadcast([128, 16, 16]),
                            in1=kf_.t[:, hd, :].unsqueeze(2).to_broadcast([128, 16, 16]), op=ALU.is_equal),
                            r=[iota16.r, kf_.r], w=[o.r])
                    for ci, (hd, side) in enumerate(combos):
                        o = ohs[ci]
                        S.op("dve", lambda e, hd=hd, side=side, o=o: e.tensor_tensor(
                            out=o.t[:], in0=o.t[:],
                            in1=topif.t[:, 2 * hd + side, :].unsqueeze(1).to_broadcast([128, 16, 16]),
                            op=ALU.mult), r=[o.r, topif.r], w=[o.r])
                    for ci, (hd, side) in enumerate(combos):
                        o = ohs[ci]
                        io_ = i1 if side == 0 else i2
                        S.op("dve", lambda e, hd=hd, io_=io_, o=o: e.tensor_reduce(out=io_.t[:, hd, :], in_=o.t[:],
                                                                                  axis=AX.X, op=ALU.add),
                             r=[o.r, io_.r], w=[r_io[ci]])
                    S.op("dve", lambda e: e.scalar_tensor_tensor(out=i1.t[:], in0=i1.t[:], scalar=128.0, in1=i2.t[:],
                                                                 op0=ALU.mult, op1=ALU.add),
                         r=r_io + [i1.r, i2.r], w=[i1.r, i2.r])
                    S.op("dve", lambda e, j=j: e.tensor_copy(
                        out=idx_all.t[:, j, :].rearrange("p (a b) -> p a b", a=8), in_=i1.t[:]),
                        r=[i1.r], w=[r_idx[j]])
                    S.op("dve", lambda e: e.tensor_reduce(out=bm.t[:], in_=best.t[:], axis=AX.X, op=ALU.max),
                         r=[best.r], w=[bm.r])
                    S.op("dve", lambda e: e.tensor_tensor(out=best.t[:], in0=best.t[:],
                                                          in1=bm.t[:].unsqueeze(2).to_broadcast([128, 8, 16]),
                                                          op=ALU.subtract), r=[best.r, bm.r], w=[best.r])
                    S.op("act", lambda e: e.activation(out=best.t[:], in_=best.t[:], func=AF.Exp),
                         r=[best.r], w=[best.r])
                    S.op("dve", lambda e: e.tensor_reduce(out=bs.t[:], in_=best.t[:], axis=AX.X, op=ALU.add),
                         r=[best.r], w=[bs.r])
                    S.op("dve", lambda e: e.reciprocal(out=bs.t[:], in_=bs.t[:]), r=[bs.r], w=[bs.r])
                    S.op("dve", lambda e, j=j: e.tensor_tensor(
                        out=gate_all.t[:, j, :].rearrange("p (a b) -> p a b", a=8), in0=best.t[:],
                        in1=bs.t[:].unsqueeze(2).to_broadcast([128, 8, 16]), op=ALU.mult),
                        r=[best.r, bs.r], w=[r_idx[j]])
                S.flush()

            with ExitStack() as P7:
                gfin = sb(P7, "gfin", [128, D])
                S.dma("sp", lambda e: e.dma_start(out=gfin.t[:], in_=g_fin_bc), w=[gfin.r])
                NB = 8
                GS = 4
                gb = [sb(P7, f"gb{i}", [128, 2 * D], BF16) for i in range(NB)]
                hn7 = [sb(P7, f"hn7_{i}", [128, D]) for i in range(2)]
                hnb7 = [sb(P7, f"hnb7_{i}", [128, D], BF16) for i in range(2)]
                x7 = [sb(P7, f"x7_{i}", [128, D]) for i in range(2)]
                acc = sb(P7, "acc7", [128, D])
                junkb = sb(P7, "junk7b", [128, D], BF16)
                diag = [sb(P7, f"diag{i}", [128, 128], BF16) for i in range(8)]
                avs = [sb(P7, f"av7_{i}", [128, GS]) for i in range(4)]
                wvs = [sb(P7, f"wv7_{i}", [128, GS]) for i in range(4)]
                ss = sb(P7, "ss7", [128, 1])
                rstd = sb(P7, "rstd7", [128, 1])
                bi = 0
                gi = 0
                for j in range(NJ):
                    hn_ = hn7[j % 2]
                    hb_ = hnb7[j % 2]
                    x_ = x7[j % 2]
                    S.dma("sp", lambda e, j=j, hn_=hn_: e.dma_start(out=hn_.t[:], in_=hn2_d[j * 128:(j + 1) * 128, :]),
                          r=[r_hn2[j]], w=[hn_.r])
                    S.dma("sp", lambda e, j=j, x_=x_: e.dma_start(out=x_.t[:], in_=x1_d[j * 128:(j + 1) * 128, :]),
                          r=[r_x1[j]], w=[x_.r])
                    S.op("act", lambda e, hn_=hn_, hb_=hb_: e.activation(out=hb_.t[:], in_=hn_.t[:], func=AF.Copy),
                         r=[hn_.r], w=[hb_.r])
                    for s0 in range(0, 128, GS):
                        av = avs[gi % 4]
                        wv7 = wvs[gi % 4]
                        gi += 1
                        gl = []
                        for k in range(GS):
                            s_ = s0 + k
                            g_ = gb[bi % NB]
                            bi += 1
                            gl.append(g_)
                            S.dma("pool", lambda e, g_=g_, j=j, s_=s_: e.indirect_dma_start(
                                out=g_.t[:], out_offset=None, in_=uv_b,
                                in_offset=bass.IndirectOffsetOnAxis(ap=idx_all.t[:, j, s_:s_ + 1], axis=0)),
                                r=[r_idx[j], r_uvb], w=[g_.r])
                            S.op("dve", lambda e, g_=g_, hb_=hb_, k=k, av=av: e.scalar_tensor_tensor(
                                out=junkb.t[:], in0=g_.t[:, 0:D], scalar=1.0, in1=hb_.t[:], op0=ALU.mult, op1=ALU.mult,
                                accum_out=av.t[:, k:k + 1]), r=[g_.r, hb_.r], w=[junkb.r, av.r])
                        S.op("act", lambda e, av=av, wv7=wv7: e.activation(out=wv7.t[:], in_=av.t[:], func=AF.Gelu),
                             r=[av.r], w=[wv7.r])
                        S.op("dve", lambda e, j=j, s0=s0, wv7=wv7: e.tensor_tensor(
                            out=wv7.t[:], in0=wv7.t[:], in1=gate_all.t[:, j, s0:s0 + GS], op=ALU.mult),
                            r=[wv7.r, r_idx[j]], w=[wv7.r])
                        for k in range(GS):
                            s_ = s0 + k
                            g_ = gl[k]
                            dg = diag[s_ % 8]
                            S.op("act", lambda e, dg=dg, k=k, wv7=wv7: e.activation(
                                out=dg.t[:], in_=ident.t[:], func=AF.Copy, scale=wv7.t[:, k:k + 1]),
                                r=[ident.r, wv7.r], w=[dg.r])
                            for q4 in range(4):
                                S.op("pe", lambda e, dg=dg, g_=g_, q4=q4, s_=s_: e.matmul(
                                    PF[q4].t[:, 0:512], lhsT=dg.t[:], rhs=g_.t[:, D + q4 * 512:D + (q4 + 1) * 512],
                                    start=(s_ == 0), stop=(s_ == 127)), r=[dg.r, g_.r], w=[PF[q4].r])
                    for q4 in range(4):
                        S.op("dve", lambda e, q4=q4, x_=x_: e.tensor_tensor(
                            out=x_.t[:, q4 * 512:(q4 + 1) * 512], in0=PF[q4].t[:, 0:512],
                            in1=x_.t[:, q4 * 512:(q4 + 1) * 512], op=ALU.add), r=[PF[q4].r, x_.r], w=[x_.r])
                    S.op("act", lambda e, x_=x_: e.activation(out=junkb.t[:], in_=x_.t[:], func=AF.Square,
                                                              accum_out=ss.t[:]), r=[x_.r], w=[junkb.r, ss.r])
                    rstd_from_ss(ss, rstd, D)
                    S.op("dve", lambda e, x_=x_: e.scalar_tensor_tensor(
                        out=acc.t[:], in0=x_.t[:], scalar=rstd.t[:, 0:1], in1=gfin.t[:], op0=ALU.mult, op1=ALU.mult),
                        r=[x_.r, rstd.r, gfin.r], w=[acc.r])
                    S.dma("sp", lambda e, j=j: e.dma_start(out=out_d[j * 128:(j + 1) * 128, :], in_=acc.t[:]),
                          r=[acc.r], w=[Res("out")])
                S.flush()
    return nc


def make_in_maps(inp, NJ, cores):
    NS = 4 * NJ
    SQ = 128 * NS
    f32 = np.float32
    x = np.asarray(inp["x"], f32)
    pos = np.asarray(inp["positions"], np.int32)
    w_in = np.asarray(inp["w_in"], f32)[0]
    offs = np.cumsum([0, 1024, 256, 256, 512, 256, 64, 2048, 2048])
    q_a, k_a, v_a, c_q, c_kv, k_r, g_a, g_b = [slice(int(offs[i]), int(offs[i + 1])) for i in range(8)]
    C = np.ascontiguousarray

    def bc(v, n=128):
        return C(np.broadcast_to(np.asarray(v, f32)[None, :], (n, len(v))))

    w_ukv = np.asarray(inp["w_ukv"], f32)[0].reshape(256, 8, 256)
    shared = {
        "invf_bc": bc(1.0 / (10000.0 ** (np.arange(0, 64, 2, dtype=np.float32) / 64)).astype(f32)),
        "g_mix_pc": C(np.asarray(inp["g_mix"], f32)[0].reshape(NKC, 128).T),
        "g_ffn_bc": bc(np.asarray(inp["g_ffn"], f32)[0]),
        "g_fin_bc": bc(np.asarray(inp["g_final"], f32)),
        "g_ckv_bc": bc(np.asarray(inp["g_ckv"], f32)[0]),
        "g_cq_bc": bc(np.asarray(inp["g_cq"], f32)[0]),
        "sinks_bc": bc(np.asarray(inp["sinks"], f32)[0]),
        "w_kv": C(np.concatenate([w_in[:, c_kv], w_in[:, k_r]], axis=1)),
        "w_kva": C(np.concatenate([w_in[:, k_a], w_in[:, v_a]], axis=1)),
        "w_cq": C(w_in[:, c_q]),
        "w_qa": C(w_in[:, q_a]),
        "w_gate": C(np.concatenate([w_in[:, g_a], w_in[:, g_b]], axis=1)),
        "w_uq": C(np.asarray(inp["w_uq"], f32)[0]),
        "w_ukv_k": C(w_ukv[:, :, :128].reshape(256, 1024)),
        "w_ukv_v": C(w_ukv[:, :, 128:].reshape(256, 1024)),
        "w_a": C(np.asarray(inp["w_a_proj"], f32)[0]),
        "w_b": C(np.asarray(inp["w_b_proj"], f32)[0]),
        "w_o": C(np.asarray(inp["w_o"], f32)[0]),
        "w_pq": C(np.asarray(inp["w_peer_q"], f32)[0]),
        "keysT": C(np.asarray(inp["peer_keys"], f32)[0].reshape(16, 128, 128).transpose(2, 0, 1)),
        "u_tab": C(np.asarray(inp["peer_u"], f32)[0]),
        "v_tab": C(np.asarray(inp["peer_v"], f32)[0]),
        "ident": np.eye(128, dtype=f32),
    }
    qi = np.arange(128)[:, None]
    kj = np.arange(256)[None, :]
    rel = 128 + qi - kj
    band = (rel >= 0) & (rel < 128)
    maps = []
    for c in cores:
        b, r = c // 4, c % 4
        xs = x[b, :SQ]
        ps = pos[b, :SQ]
        own = [4 * j + r for j in range(NJ)]
        m = dict(shared)
        m["x_seq"] = C(xs)
        m["x_own"] = C(np.concatenate([xs[t * 128:(t + 1) * 128] for t in own], axis=0))
        xp = []
        pk = []
        for t in own:
            if t == 0:
                xp.append(np.zeros((128, D), f32))
                pprev = np.zeros(128, np.int32)
            else:
                xp.append(xs[(t - 1) * 128:t * 128])
                pprev = ps[(t - 1) * 128:t * 128]
            row = np.concatenate([pprev, ps[t * 128:(t + 1) * 128]])
            pk.append(np.broadcast_to(row[None, :], (128, 256)))
        m["x_prev"] = C(np.concatenate(xp, axis=0))
        m["pos_kv_bc"] = C(np.stack(pk, axis=0).astype(np.int32))
        m["pos_seq"] = C(ps.reshape(NS, 128).T)
        m["pos_own"] = C(np.stack([ps[t * 128:(t + 1) * 128] for t in own], axis=1))
        mb0 = np.where(band & ((kj >= 128) | (own[0] > 0)), 0.0, -1e30).astype(f32)
        mb1 = np.where(band, 0.0, -1e30).astype(f32)
        m["swa_maskb"] = C(np.stack([mb0, mb1], axis=0))
        mm = np.zeros((128, 512), f32)
        for kk in range(4):
            if kk < r:
                mm[:, kk * 128:(kk + 1) * 128] = 1.0
            elif kk == r:
                mm[:, kk * 128:(kk + 1) * 128] = (np.arange(128)[None, :] <= np.arange(128)[:, None])
        m["mla_mask"] = mm
        maps.append(m)
    return maps


def run(inputs, NJ, cores):
    nc = build(NJ)
    maps = make_in_maps(inputs, NJ, cores)
    res = run_bass_kernel_spmd(nc, maps, core_ids=list(range(len(cores))))
    return [r["out"] for r in res.results]


def kernel(**inputs):
    NJ = 16
    cores = list(range(8))
    outs = run(inputs, NJ, cores)
    B, SQ = 2, 8192
    out = np.empty((B, SQ, D), np.float32)
    for c, o in zip(cores, outs):
        b, r = c // 4, c % 4
        for j in range(NJ):
            t = 4 * j + r
            out[b, t * 128:(t + 1) * 128] = o[j * 128:(j + 1) * 128]
    return out
```
